# Optimizing a Trainium2 kernel written in Bass

```python
import jax
import jax.numpy as jnp
from jax import lax
import numpy as np

D_MODEL = 1024
BATCH = 8
SEQ = 4096
DEPTH = 2

HEAD_DIM = 64
ROT_DIM = HEAD_DIM // 4
ROPE_THETA = 500000.0
BLOCK_Q = 128
NORM_EPS = 1e-6
D_FF = 2816

DIL_PAIRS = ((128, 1), (512, 4), (2048, 16))
A_GROUPS = len(DIL_PAIRS)
A_SLOTS = 6

B_HEADS = 8
B_KV = 2
B_REP = B_HEADS // B_KV
CMP_BLOCK = 32
CMP_STRIDE = 16
CMP_HIDDEN = 2 * HEAD_DIM
SEL_BLOCK = 64
N_SELECT = 16
WINDOW = 512
FORCE_SCORE = 1e4

C_HEADS = 6

A_WIDTH = A_SLOTS * HEAD_DIM
B_WIDTH = B_HEADS * HEAD_DIM
C_WIDTH = C_HEADS * HEAD_DIM
MIX_WIDTH = A_WIDTH + B_WIDTH + C_WIDTH
A_IN = 3 * A_GROUPS * A_WIDTH
B_Q_IN = B_WIDTH
B_KV_IN = 3 * 2 * B_KV * HEAD_DIM
B_GATE_IN = 3 * B_HEADS
B_IN = B_Q_IN + B_KV_IN + B_GATE_IN
C_IN = 3 * C_WIDTH
IN_WIDTH = A_IN + B_IN + C_IN

kernel_name = 'hybrid_dilated_nsa_stickbreak_macaron'


def rms_norm(x, g):
    xf = x.astype(jnp.float32)
    y = xf * lax.rsqrt(jnp.mean(xf * xf, axis=-1, keepdims=True) + NORM_EPS)
    return (y * g.astype(jnp.float32)).astype(x.dtype)


def swiglu(x, w1, w3, w2):
    return (jax.nn.silu(x @ w1) * (x @ w3)) @ w2


def rope_tables(positions):
    inv = ROPE_THETA ** (-jnp.arange(0, ROT_DIM, 2, dtype=jnp.float32) / ROT_DIM)
    ang = positions.astype(jnp.float32)[..., None] * inv
    return jnp.cos(ang), jnp.sin(ang)


def apply_rope(x, cos, sin):
    half = ROT_DIM // 2
    shp = cos.shape[:2] + (1,) * (x.ndim - 3) + (half,)
    c, s = cos.reshape(shp), sin.reshape(shp)
    xr = x[..., :ROT_DIM].astype(jnp.float32)
    x1, x2 = xr[..., :half], xr[..., half:]
    rot = jnp.concatenate([x1 * c - x2 * s, x2 * c + x1 * s], axis=-1).astype(x.dtype)
    return jnp.concatenate([rot, x[..., ROT_DIM:]], axis=-1)


def masked_softmax(s, mask):
    s = jnp.where(mask, s, -jnp.inf)
    m = jnp.max(s, axis=-1, keepdims=True)
    m = jnp.where(jnp.isfinite(m), m, 0.0)
    p = jnp.exp(s - m)
    l = jnp.sum(p, axis=-1, keepdims=True)
    l_safe = jnp.where(l > 0, l, 1.0)
    return p / l_safe, (m + jnp.log(l_safe))[..., 0]


def band_blocks(x, blk, n_prev):
    n, L = x.shape[:2]
    nb = L // blk
    xb = x.reshape((n, nb, blk) + x.shape[2:])
    xp = jnp.pad(xb, [(0, 0), (n_prev, 0)] + [(0, 0)] * (xb.ndim - 2))
    return jnp.concatenate([xp[:, i:i + nb] for i in range(n_prev + 1)], axis=2)


def banded_attention(q, k, v, blk, n_prev, max_dist):
    n, L, G, R, dh = q.shape
    nb = L // blk
    width = (n_prev + 1) * blk
    qb = q.reshape(n, nb, blk, G, R, dh)
    kb = band_blocks(k, blk, n_prev)
    vb = band_blocks(v, blk, n_prev)
    s = jnp.einsum('nbqgrd,nbkgd->nbgrqk', qb, kb, preferred_element_type=jnp.float32) * (dh ** -0.5)
    qi = jnp.arange(blk)[:, None] + n_prev * blk
    ki = jnp.arange(width)[None, :]
    dist = qi - ki
    key_pos = jnp.arange(nb)[:, None, None] * blk + ki[None] - n_prev * blk
    mask = (dist >= 0) & (dist <= max_dist) & (key_pos >= 0)
    p, lse = masked_softmax(s, mask[:, None, None])
    o = jnp.einsum('nbgrqk,nbkgd->nbqgrd', p, vb.astype(jnp.float32)).astype(q.dtype)
    return o.reshape(n, L, G, R, dh), lse.transpose(0, 1, 4, 2, 3).reshape(n, L, G, R)


def dilated_attention(q, k, v):
    B, S = q.shape[:2]
    outs, lses = [], []
    for g, (window, dil) in enumerate(DIL_PAIRS):
        steps = window // dil
        n_prev = -(-steps // BLOCK_Q)
        unit = dil * BLOCK_Q
        Sp = -(-S // unit) * unit
        M = Sp // dil

        def to_sub(t):
            t = jnp.pad(t, ((0, 0), (0, Sp - S), (0, 0), (0, 0)))
            t = t.reshape(B, M, dil, A_SLOTS, HEAD_DIM).transpose(0, 2, 1, 3, 4)
            return t.reshape(B * dil, M, A_SLOTS, HEAD_DIM)

        qs, ks, vs = to_sub(q[:, :, g]), to_sub(k[:, :, g]), to_sub(v[:, :, g])
        o, lse = banded_attention(qs[:, :, :, None], ks, vs, BLOCK_Q, n_prev, steps)
        o = o[:, :, :, 0].reshape(B, dil, M, A_SLOTS, HEAD_DIM).transpose(0, 2, 1, 3, 4)
        outs.append(o.reshape(B, Sp, A_SLOTS, HEAD_DIM)[:, :S])
        lse = lse[..., 0].reshape(B, dil, M, A_SLOTS).transpose(0, 2, 1, 3)
        lses.append(lse.reshape(B, Sp, A_SLOTS)[:, :S])
    alpha = jax.nn.softmax(jnp.stack(lses, axis=0), axis=0)
    o = jnp.sum(alpha[..., None] * jnp.stack(outs, axis=0).astype(jnp.float32), axis=0)
    return o.reshape(B, S, A_WIDTH).astype(q.dtype)


def nsa_compress(x, pe, w1, w2):
    B, S = x.shape[:2]
    n_chunk = S // CMP_STRIDE
    per = CMP_BLOCK // CMP_STRIDE
    n_cmp = n_chunk - per + 1
    c = x.reshape(B, n_chunk, CMP_STRIDE, B_KV, HEAD_DIM)
    blocks = jnp.concatenate([c[:, i:i + n_cmp] for i in range(per)], axis=2)
    blocks = blocks + pe[:, None, :]
    flat = blocks.transpose(0, 1, 3, 2, 4).reshape(B, n_cmp, B_KV, CMP_BLOCK * HEAD_DIM)
    return jax.nn.gelu(flat @ w1) @ w2


def nsa_attention(q, kv, gate_logits, cos, sin, pe_k, cw1_k, cw2_k, pe_v, cw1_v, cw2_v):
    B, S = q.shape[:2]
    scale = HEAD_DIM ** -0.5
    t = jnp.arange(S)
    per = CMP_BLOCK // CMP_STRIDE
    ratio = SEL_BLOCK // CMP_STRIDE

    kc = nsa_compress(kv[:, :, 0, 0], pe_k, cw1_k, cw2_k)
    vc = nsa_compress(kv[:, :, 0, 1], pe_v, cw1_v, cw2_v)
    n_cmp = kc.shape[1]
    qg = q.reshape(B, S, B_KV, B_REP, HEAD_DIM)
    s = jnp.einsum('bsgrd,bcgd->bgrsc', qg, kc, preferred_element_type=jnp.float32) * scale
    cmp_end = jnp.arange(n_cmp) * CMP_STRIDE + CMP_BLOCK - 1
    p_cmp, _ = masked_softmax(s, cmp_end[None, :] <= t[:, None])
    o_cmp = jnp.einsum('bgrsc,bcgd->bsgrd', p_cmp, vc.astype(jnp.float32))

    n_sel = S // SEL_BLOCK
    imp = jnp.sum(p_cmp, axis=2)
    imp = jnp.pad(imp, ((0, 0), (0, 0), (0, 0), (0, ratio * n_sel + ratio + per - n_cmp)))
    sel_score = imp[..., 0:ratio * n_sel:ratio] * 0.0
    for m in range(ratio):
        for n in range(per):
            sel_score = sel_score + imp[..., m + n:m + n + ratio * n_sel:ratio]
    j = jnp.arange(n_sel)[None, :]
    cur = (t // SEL_BLOCK)[:, None]
    valid = j * SEL_BLOCK <= t[:, None]
    forced = (j == 0) | (j == cur) | (j == cur - 1)
    sel_score = jnp.where(forced, FORCE_SCORE, jnp.where(valid, sel_score, -1.0))
    n_top = min(N_SELECT, n_sel)
    _, idx = lax.top_k(sel_score, n_top)

    qr = apply_rope(q, cos, sin).reshape(B, S, B_KV, B_REP, HEAD_DIM)
    k_slc = apply_rope(kv[:, :, 1, 0], cos, sin)

    def to_sel_blocks(x):
        x = x.reshape(B, n_sel, SEL_BLOCK, B_KV, HEAD_DIM).transpose(0, 3, 1, 2, 4)
        return x.reshape(B, B_KV, n_sel, SEL_BLOCK * HEAD_DIM)

    kb, vb = to_sel_blocks(k_slc), to_sel_blocks(kv[:, :, 1, 1])
    nq = S // BLOCK_Q
    q_steps = qr.reshape(B, nq, BLOCK_Q, B_KV, B_REP, HEAD_DIM).transpose(1, 0, 2, 3, 4, 5)
    idx_steps = idx.reshape(B, B_KV, nq, BLOCK_Q, n_top).transpose(2, 0, 1, 3, 4)
    t_steps = t.reshape(nq, BLOCK_Q)
    bi = jnp.arange(B)[:, None, None]
    gi = jnp.arange(B_KV)[None, :, None]

    def sel_step(args):
        qb, ib, tb = args
        flat = ib.reshape(B, B_KV, BLOCK_Q * n_top)
        kg = kb[bi, gi, flat].reshape(B, B_KV, BLOCK_Q, n_top * SEL_BLOCK, HEAD_DIM)
        vg = vb[bi, gi, flat].reshape(B, B_KV, BLOCK_Q, n_top * SEL_BLOCK, HEAD_DIM)
        kpos = (ib[..., None] * SEL_BLOCK + jnp.arange(SEL_BLOCK)).reshape(B, B_KV, BLOCK_Q, n_top * SEL_BLOCK)
        sc = jnp.einsum('bqgrd,bgqkd->bgrqk', qb, kg, preferred_element_type=jnp.float32) * scale
        p, _ = masked_softmax(sc, (kpos <= tb[:, None])[:, :, None])
        return jnp.einsum('bgrqk,bgqkd->bqgrd', p, vg.astype(jnp.float32))

    o_sel = lax.map(sel_step, (q_steps, idx_steps, t_steps))
    o_sel = o_sel.transpose(1, 0, 2, 3, 4, 5).reshape(B, S, B_KV, B_REP, HEAD_DIM)

    k_win = apply_rope(kv[:, :, 2, 0], cos, sin)
    o_win, _ = banded_attention(qr, k_win, kv[:, :, 2, 1], BLOCK_Q, -(-WINDOW // BLOCK_Q), WINDOW - 1)

    g = jax.nn.sigmoid(gate_logits.astype(jnp.float32)).reshape(B, S, B_KV, B_REP, 3)
    o = g[..., 0:1] * o_cmp + g[..., 1:2] * o_sel + g[..., 2:3] * o_win.astype(jnp.float32)
    return o.reshape(B, S, B_WIDTH).astype(q.dtype)


def stick_breaking_attention(q, k, v):
    B, S, H, dh = q.shape
    nq = S // BLOCK_Q
    scale = dh ** -0.5
    s_pos = jnp.arange(S)
    vf = v.astype(jnp.float32)
    q_steps = q.reshape(B, nq, BLOCK_Q, H, dh).transpose(1, 0, 2, 3, 4)
    t_steps = jnp.arange(S).reshape(nq, BLOCK_Q)

    def step(args):
        qb, tb = args
        z = jnp.einsum('bqhd,bshd->bhqs', qb, k, preferred_element_type=jnp.float32) * scale
        before = s_pos[None, :] < tb[:, None]
        log_1mb = jnp.where(before, jax.nn.log_sigmoid(-z), 0.0)
        between = lax.cumsum(log_1mb, axis=3, reverse=True) - log_1mb
        a = jnp.where(before, jnp.exp(jax.nn.log_sigmoid(z) + between), 0.0)
        return jnp.einsum('bhqs,bshd->bqhd', a, vf)

    o = lax.map(step, (q_steps, t_steps))
    return o.transpose(1, 0, 2, 3, 4).reshape(B, S, H * dh).astype(q.dtype)


def hybrid_layer(x, cos, sin, g_ffn1, f1_w1, f1_w3, f1_w2, g_mix, w_in, pe_k, cw1_k, cw2_k,
                 pe_v, cw1_v, cw2_v, w_gate, w_up, w_out, g_ffn2, f2_w1, f2_w3, f2_w2):
    B, S, D = x.shape
    h = x + 0.5 * swiglu(rms_norm(x, g_ffn1), f1_w1, f1_w3, f1_w2)
    u = rms_norm(h, g_mix)
    proj = u @ w_in
    a_in = proj[..., :A_IN]
    b_in = proj[..., A_IN:A_IN + B_IN]
    c_in = proj[..., A_IN + B_IN:]

    a = a_in.reshape(B, S, 3, A_GROUPS, A_SLOTS, HEAD_DIM)
    qa = apply_rope(a[:, :, 0], cos, sin)
    ka = apply_rope(a[:, :, 1], cos, sin)
    o_a = dilated_attention(qa, ka, a[:, :, 2])

    qb = b_in[..., :B_Q_IN].reshape(B, S, B_HEADS, HEAD_DIM)
    kvb = b_in[..., B_Q_IN:B_Q_IN + B_KV_IN].reshape(B, S, 3, 2, B_KV, HEAD_DIM)
    gb = b_in[..., B_Q_IN + B_KV_IN:]
    o_b = nsa_attention(qb, kvb, gb, cos, sin, pe_k, cw1_k, cw2_k, pe_v, cw1_v, cw2_v)

    c = c_in.reshape(B, S, 3, C_HEADS, HEAD_DIM)
    o_c = stick_breaking_attention(c[:, :, 0], c[:, :, 1], c[:, :, 2])

    gates = jax.nn.sigmoid((u @ w_gate).astype(jnp.float32)).astype(u.dtype).reshape(B, S, 3, D)
    y = (gates[:, :, 0] * (o_a @ w_up[:A_WIDTH])
         + gates[:, :, 1] * (o_b @ w_up[A_WIDTH:A_WIDTH + B_WIDTH])
         + gates[:, :, 2] * (o_c @ w_up[A_WIDTH + B_WIDTH:]))
    h = h + y @ w_out
    return h + 0.5 * swiglu(rms_norm(h, g_ffn2), f2_w1, f2_w3, f2_w2)


def setup_inputs(seed: int = 0) -> dict:
    key = jax.random.key(seed)
    ks = jax.random.split(key, 24)

    def nrm(k, shape, fan_in):
        return jax.random.normal(k, shape, jnp.float32) * (fan_in ** -0.5)

    def gain(k, shape):
        return 1.0 + 0.01 * jax.random.normal(k, shape, jnp.float32)

    L = DEPTH
    x = jax.random.normal(ks[0], (BATCH, SEQ, D_MODEL), jnp.float32)
    offset = jax.random.randint(ks[1], (BATCH, 1), 0, 1024)
    positions = (offset + jnp.arange(SEQ, dtype=jnp.int32)[None, :]).astype(jnp.int32)
    return {
        'x': x,
        'positions': positions,
        'norm_ffn1': gain(ks[2], (L, D_MODEL)),
        'ffn1_w1': nrm(ks[3], (L, D_MODEL, D_FF), D_MODEL),
        'ffn1_w3': nrm(ks[4], (L, D_MODEL, D_FF), D_MODEL),
        'ffn1_w2': nrm(ks[5], (L, D_FF, D_MODEL), D_FF),
        'norm_mix': gain(ks[6], (L, D_MODEL)),
        'w_in': nrm(ks[7], (L, D_MODEL, IN_WIDTH), D_MODEL),
        'cmp_pe_k': 0.1 * jax.random.normal(ks[8], (L, CMP_BLOCK, HEAD_DIM), jnp.float32),
        'cmp_w1_k': nrm(ks[9], (L, CMP_BLOCK * HEAD_DIM, CMP_HIDDEN), CMP_BLOCK * HEAD_DIM),
        'cmp_w2_k': nrm(ks[10], (L, CMP_HIDDEN, HEAD_DIM), CMP_HIDDEN),
        'cmp_pe_v': 0.1 * jax.random.normal(ks[11], (L, CMP_BLOCK, HEAD_DIM), jnp.float32),
        'cmp_w1_v': nrm(ks[12], (L, CMP_BLOCK * HEAD_DIM, CMP_HIDDEN), CMP_BLOCK * HEAD_DIM),
        'cmp_w2_v': nrm(ks[13], (L, CMP_HIDDEN, HEAD_DIM), CMP_HIDDEN),
        'w_gate': nrm(ks[14], (L, D_MODEL, 3 * D_MODEL), D_MODEL),
        'w_up': nrm(ks[15], (L, MIX_WIDTH, D_MODEL), B_WIDTH),
        'w_out': nrm(ks[16], (L, D_MODEL, D_MODEL), D_MODEL),
        'norm_ffn2': gain(ks[17], (L, D_MODEL)),
        'ffn2_w1': nrm(ks[18], (L, D_MODEL, D_FF), D_MODEL),
        'ffn2_w3': nrm(ks[19], (L, D_MODEL, D_FF), D_MODEL),
        'ffn2_w2': nrm(ks[20], (L, D_FF, D_MODEL), D_FF),
        'norm_final': gain(ks[21], (D_MODEL,)),
    }


def reference(x, positions, norm_ffn1, ffn1_w1, ffn1_w3, ffn1_w2, norm_mix, w_in,
              cmp_pe_k, cmp_w1_k, cmp_w2_k, cmp_pe_v, cmp_w1_v, cmp_w2_v,
              w_gate, w_up, w_out, norm_ffn2, ffn2_w1, ffn2_w3, ffn2_w2, norm_final):
    cos, sin = rope_tables(positions)
    h = x
    for i in range(DEPTH):
        h = hybrid_layer(h, cos, sin, norm_ffn1[i], ffn1_w1[i], ffn1_w3[i], ffn1_w2[i],
                         norm_mix[i], w_in[i], cmp_pe_k[i], cmp_w1_k[i], cmp_w2_k[i],
                         cmp_pe_v[i], cmp_w1_v[i], cmp_w2_v[i], w_gate[i], w_up[i], w_out[i],
                         norm_ffn2[i], ffn2_w1[i], ffn2_w3[i], ffn2_w2[i])
    return rms_norm(h, norm_final)
```

```python
import contextlib
import numpy as np
import ml_dtypes
import concourse.bass as bass
import concourse.mybir as mybir
from concourse.bass_utils import run_bass_kernel_spmd

F32 = mybir.dt.float32
BF = mybir.dt.bfloat16
I32 = mybir.dt.int32
AF = mybir.ActivationFunctionType
ALU = mybir.AluOpType
AX = mybir.AxisListType

D = 1024
DFF = 2816
NF = DFF // 128
HD = 64
NEG = -1920.0
EPS = 1e-6
A_IN = 3456
B_IN = 1304
IN_W = 5912
NSEL = 16


class Buf:
    __slots__ = ("name", "w", "r", "sem", "semv", "psum")

    def __init__(self, name):
        self.name = name
        self.psum = False
        self.w = None
        self.r = {}
        self.sem = None
        self.semv = 0


class V:
    __slots__ = ("ap", "bufs")

    def __init__(self, ap, bufs):
        self.ap = ap
        self.bufs = bufs


class Tn:
    def __init__(self, P, name, shape, dtype, space="sbuf", es=None):
        es = es if es is not None else P.es
        P.ntn = getattr(P, "ntn", 0) + 1
        name = f"{name}_{P.ntn}"
        if space == "sbuf":
            self.h = es.enter_context(P.nc.sbuf_tensor(name, list(shape), dtype))
        else:
            self.h = es.enter_context(P.nc.psum_tensor(name, list(shape), dtype))
        self.buf = Buf(name)
        self.buf.psum = (space != "sbuf")
        self.shape = list(shape)
        self.dtype = dtype
        self.P = P

    def __getitem__(self, idx):
        return V(self.h[idx], [self.buf])

    def v(self, ap):
        return V(ap, [self.buf])

    def raw(self, offset, ap):
        return V(bass.AP(tensor=self.h, offset=offset, ap=ap), [self.buf])


class DT:
    def __init__(self, P, name, shape, dtype, kind="Internal"):
        self.t = P.nc.dram_tensor(name, list(shape), dtype, kind=kind)
        self.ap = self.t.ap()
        self.pending = {}
        self.name = name
        self.sem = None
        self.semv = 0
        P.dts.append(self)


class Eng:
    def __init__(self, P, name, h):
        self.P = P
        self.name = name
        self.h = h
        self.sem = P.new_sem("e_" + name)
        self.cnt = 0
        self.waited = {}
        self.pend_r = []
        self.pend_w = []

    def wait(self, toks):
        for tok in toks:
            sem, val = tok[0], tok[1]
            k = sem.num
            if self.waited.get(k, 0) < val:
                self.h.wait_ge(sem, val)
                self.waited[k] = val


class Prog:
    def __init__(self, nc, es):
        self.nc = nc
        self.es = es
        self.nsem = 0
        self.dts = []
        self.scope = None
        self.pe = Eng(self, "pe", nc.tensor)
        self.act = Eng(self, "act", nc.scalar)
        self.dve = Eng(self, "dve", nc.vector)
        self.pool = Eng(self, "pool", nc.gpsimd)
        self.sp = Eng(self, "sp", nc.sync)
        self.engs = [self.pe, self.act, self.dve, self.pool, self.sp]
        self.bar_sem = self.new_sem("bar")
        self.bar_n = 0
        self.dma_toks = {}
        self.n_ins = 0

    def new_sem(self, name):
        self.nsem += 1
        h = self.nc.alloc_semaphore(name=f"{name}_{self.nsem}")
        if self.scope is not None:
            self.scope.append(h)
        return h

    def scope_begin(self):
        assert self.scope is None
        self.scope = []

    def scope_end(self):
        self.barrier()
        sems = self.scope
        self.scope = None
        if sems:
            nums = set(h.num for h in sems)
            self.nc.clear_and_free_semaphores(sems)
            self.op(self.pool, lambda: self.nc.gpsimd.memset(self.scratch.h[:, :], 0.0), [], [self.scratch[:, :]])
            for e in self.engs:
                for k in list(e.waited.keys()):
                    if k in nums:
                        del e.waited[k]
            for k in list(self.dma_toks.keys()):
                if k in nums:
                    del self.dma_toks[k]
            for d in self.dts:
                d.pending = {k: v for k, v in d.pending.items() if k not in nums}
                if d.sem is not None and d.sem.num in nums:
                    d.sem = None
                    d.semv = 0
        self.barrier()

    def _deps(self, E, reads, writes, strict=False):
        toks = []
        for b in reads:
            if b.w is not None:
                t = b.w
                if strict or t[2] != E.name or E.name != "pe":
                    toks.append(t)
            if b.psum:
                for t in b.r.values():
                    if t[2] != E.name:
                        toks.append(t)
        same_ok = (E.name == "pe")
        for b in writes:
            if b.w is not None and (strict or b.w[2] != E.name or not same_ok):
                toks.append(b.w)
            for t in b.r.values():
                if strict or t[2] != E.name or not same_ok:
                    toks.append(t)
        return toks

    def _commit(self, tok, reads, writes):
        ek = tok[2] if tok[2] is not None else tok[0].num
        for b in reads:
            b.r[ek] = tok
        for b in writes:
            b.w = tok
            b.r = {}

    def op(self, E, fn, reads, writes, sig=True):
        rb = [b for v in reads for b in v.bufs]
        wb = [b for v in writes for b in v.bufs]
        for b in rb + wb:
            assert not (b in self.pe.pend_r or b in self.pe.pend_w) or E is self.pe, \
                f"buffer {b.name} has unsignalled PE access"
        E.wait(self._deps(E, rb, wb))
        ins = fn()
        self.n_ins += 1
        if E is self.pe and not sig:
            E.pend_r += rb
            E.pend_w += wb
            return ins
        E.cnt += 1
        ins.then_inc(E.sem, 1)
        tok = (E.sem, E.cnt, E.name)
        if E is self.pe:
            rb = rb + E.pend_r
            wb = wb + E.pend_w
            E.pend_r = []
            E.pend_w = []
        self._commit(tok, rb, wb)
        return ins

    def dma(self, Q, out, in_, **kw):
        o_dram = isinstance(out, tuple)
        i_dram = isinstance(in_, tuple)
        toks = []
        if i_dram:
            toks += list(in_[0].pending.values())
            in_ap = in_[1]
        else:
            in_ap = in_.ap
            for b in in_.bufs:
                if b.w is not None:
                    toks.append(b.w)
        part = kw.pop("part", False)
        if o_dram:
            out_ap = out[1]
        else:
            out_ap = out.ap
            dd = self._deps(Q, [], out.bufs, strict=True)
            if part:
                dd = [t for t in dd if not (t[2] is None and t is out.bufs[0].w)]
            toks += dd
        Q.wait(toks)
        ins = Q.h.dma_start(out=out_ap, in_=in_ap, **kw)
        self.n_ins += 1
        if not o_dram:
            b = out.bufs[0]
        elif not i_dram:
            b = in_.bufs[0]
        else:
            b = out[0]
        if b.sem is None:
            b.sem = self.new_sem("d")
        b.semv += 16
        ins.then_inc(b.sem, 16)
        tok = (b.sem, b.semv, None)
        self.dma_toks[b.sem.num] = tok
        if not o_dram:
            out.bufs[0].w = tok
            out.bufs[0].r = {}
        else:
            out[0].pending[b.sem.num] = tok
            if not i_dram:
                in_.bufs[0].r[b.sem.num] = tok
        return ins

    def barrier(self):
        sp = self.sp
        toks = [(e.sem, e.cnt, e.name) for e in self.engs if e is not sp and e.cnt > 0]
        toks += list(self.dma_toks.values())
        sp.wait(toks)
        self.bar_n += 1
        sp.h.sem_inc(self.bar_sem, 1)
        for e in self.engs:
            if e is not sp:
                e.h.wait_ge(self.bar_sem, self.bar_n)
                for t in toks:
                    k = t[0].num
                    if e.waited.get(k, 0) < t[1]:
                        e.waited[k] = t[1]

    def mm(self, out, lhsT, rhs, start, stop, sig=None):
        if sig is None:
            sig = stop
        return self.op(self.pe,
                       lambda: self.nc.tensor.matmul(out.ap, lhsT=lhsT.ap, rhs=rhs.ap, start=start, stop=stop,
                                                     skip_group_check=True),
                       [lhsT, rhs], [out], sig=sig)

    def transpose(self, out, in_, ident, sig=True):
        return self.op(self.pe, lambda: self.nc.tensor.transpose(out.ap, in_.ap, ident.ap),
                       [in_, ident], [out], sig=sig)

    def activation(self, out, in_, func, scale=1.0, bias=None, eng=None):
        E = eng or self.act
        kw = {}
        reads = [in_]
        if bias is not None:
            if isinstance(bias, V):
                kw["bias"] = bias.ap
                reads.append(bias)
            else:
                kw["bias"] = bias
        if isinstance(scale, V):
            reads.append(scale)
            sc = scale.ap
        else:
            sc = scale
        return self.op(E, lambda: E.h.activation(out=out.ap, in_=in_.ap, func=func, scale=sc, **kw), reads, [out])

    def tt(self, E, out, in0, in1, op):
        return self.op(E, lambda: E.h.tensor_tensor(out=out.ap, in0=in0.ap, in1=in1.ap, op=op), [in0, in1], [out])

    def ts(self, E, out, in0, s1, s2, op0, op1=None):
        reads = [in0]
        a1 = s1
        a2 = s2
        if isinstance(s1, V):
            reads.append(s1)
            a1 = s1.ap
        if isinstance(s2, V):
            reads.append(s2)
            a2 = s2.ap
        if op1 is None:
            return self.op(E, lambda: E.h.tensor_scalar(out=out.ap, in0=in0.ap, scalar1=a1, scalar2=None, op0=op0),
                           reads, [out])
        return self.op(E, lambda: E.h.tensor_scalar(out=out.ap, in0=in0.ap, scalar1=a1, scalar2=a2, op0=op0, op1=op1),
                       reads, [out])

    def stt(self, out, in0, scalar, in1, op0, op1):
        reads = [in0, in1]
        a = scalar
        if isinstance(scalar, V):
            reads.append(scalar)
            a = scalar.ap
        E = self.dve
        return self.op(E, lambda: E.h.scalar_tensor_tensor(out=out.ap, in0=in0.ap, scalar=a, in1=in1.ap, op0=op0, op1=op1),
                       reads, [out])

    def copy(self, E, out, in_):
        if E is self.act:
            return self.op(E, lambda: E.h.copy(out=out.ap, in_=in_.ap), [in_], [out])
        return self.op(E, lambda: E.h.tensor_copy(out=out.ap, in_=in_.ap), [in_], [out])

    def memset(self, E, out, val):
        return self.op(E, lambda: E.h.memset(out.ap, val), [], [out])

    def recip(self, out, in_):
        E = self.dve
        return self.op(E, lambda: E.h.reciprocal(out=out.ap, in_=in_.ap), [in_], [out])


def _fm_blocks():
    blks = []
    for g in range(3):
        for sp in range(3):
            blks.append(("qA", [(0 * 1152 + g * 384 + sp * 128, 128)], g * 3 + sp, None))
    for g in range(3):
        for sp in range(3):
            blks.append(("kA", [(1152 + g * 384 + sp * 128, 128)], 9 + g * 3 + sp, None))
    bq = A_IN
    for r in range(4):
        blks.append(("qB", [(bq + r * 64, 64), (bq + (4 + r) * 64, 64)], 22 + r, 18 + r))
    bkv = A_IN + 512
    blks.append(("kcmp", [(bkv, 128)], None, 26))
    blks.append(("vcmp", [(bkv + 128, 128)], None, 27))
    blks.append(("kslc", [(bkv + 256, 128)], 28, None))
    blks.append(("kwin", [(bkv + 512, 128)], 29, None))
    cb = A_IN + B_IN
    for hp in range(3):
        blks.append(("qC", [(cb + hp * 128, 128)], None, 30 + hp))
    for hp in range(3):
        blks.append(("kC", [(cb + 384 + hp * 128, 128)], None, 33 + hp))
    return blks


FM = _fm_blocks()
NQT = 36
VRUNS = [(2304, 1152), (A_IN + 512 + 256 + 128, 128), (A_IN + 512 + 512 + 128, 128), (A_IN + 1280, 24),
         (A_IN + B_IN + 768, 384)]
NV = 1816
VO_A, VO_SLC, VO_WIN, VO_GATE, VO_C = 0, 1152, 1280, 1408, 1432
NVG = 4
VGW = NV // NVG
A_DIL = (1, 4, 16)


def _consts(S):
    bf = ml_dtypes.bfloat16
    NT = S // 128
    c = {}
    c["c_ident"] = np.eye(128, dtype=np.float32).astype(bf)
    c["c_identf"] = np.eye(128, dtype=np.float32)
    perm = np.zeros((128, 128), np.float32)
    for m in range(128):
        d = m % 64
        if d < 8:
            perm[m + 8, m] = 1.0
        elif d < 16:
            perm[m - 8, m] = 1.0
    c["c_perm"] = perm.astype(bf)
    jj = np.arange(128)
    c["c_U"] = (jj[:, None] >= jj[None, :]).astype(np.float32).astype(bf)
    c["c_ones"] = np.ones((128, 128), np.float32).astype(bf)
    si = np.arange(128)[:, None]
    ti = np.arange(512)[None, :]
    tiles = []
    for dil in A_DIL:
        for dl in range(-3, dil + 1):
            d = 128 * dl + ti - si
            ok = (d >= 0) & (d <= 128 * dil) & (d % dil == 0)
            tiles.append(np.where(ok, 0.0, NEG))
    c["c_amask"] = np.stack(tiles).astype(np.float32).astype(bf)
    tiles = []
    for dl in range(-3, 5):
        d = 128 * dl + ti - si
        tiles.append(np.where((d >= 0) & (d <= 511), 0.0, NEG))
    c["c_wmask"] = np.stack(tiles).astype(np.float32).astype(bf)
    tiles = []
    tiles2 = []
    for dl in range(-3, 1):
        d = 128 * dl + ti - si
        tiles.append(np.where(d >= 0, 0.0, NEG))
        tiles2.append(np.where(d >= 1, 1.0, 0.0))
    c["c_smask"] = np.stack(tiles).astype(np.float32).astype(bf)
    c["c_cmask"] = np.stack(tiles2).astype(np.float32).astype(bf)
    cc = np.arange(256)[:, None]
    tt = np.arange(S)[None, :]
    n_cmp = S // 16 - 1
    c["c_cmpb"] = np.where((16 * cc + 31 <= tt) & (cc < n_cmp), 0.0, NEG).astype(np.float32).astype(bf)
    es = np.zeros((64, NT, 128), np.float32)
    for sg in range(NT):
        for s_ in range(128):
            j = 2 * sg + s_ // 64
            if j < 64:
                es[j, sg, s_] = 1.0
    c["c_esel"] = es.astype(bf)
    w = np.zeros((256, 64), np.float32)
    for j in range(64):
        for m in range(4):
            for n in range(2):
                ci = 4 * j + m + n
                if ci < 256:
                    w[ci, j] += 1.0
    c["c_wsel"] = w.astype(bf)
    t = np.arange(S)[:, None]
    j = np.arange(64)[None, :]
    cur = t // 64
    nsel = S // 64
    valid = (j * 64 <= t) & (j < nsel)
    forced = ((j == 0) | (j == cur) | (j == cur - 1)) & (j < nsel)
    c["c_vnf"] = (valid & ~forced).astype(np.float32)
    c["c_add"] = np.where(j >= nsel, -3.0, np.where(forced, 1e4, np.where(valid, 0.0, -1.0))).astype(np.float32)
    rp = np.zeros((128, 2), np.float32)
    for p in range(128):
        d = p % 64
        if d < 16:
            rp[p, 0] = np.float32(500000.0) ** np.float32(-(2 * (d % 8)) / 16.0)
            rp[p, 1] = -1.0 if d < 8 else 1.0
    c["c_rope"] = rp
    return c


WNAMES = ["norm_ffn1", "ffn1_w1", "ffn1_w3", "ffn1_w2", "norm_mix", "w_in", "cmp_pe_k", "cmp_w1_k", "cmp_w2_k",
          "cmp_pe_v", "cmp_w1_v", "cmp_w2_v", "w_gate", "w_up", "w_out", "norm_ffn2", "ffn2_w1", "ffn2_w3",
          "ffn2_w2"]
WSHAPES = {"norm_ffn1": [D], "ffn1_w1": [D, DFF], "ffn1_w3": [D, DFF], "ffn1_w2": [DFF, D], "norm_mix": [D],
           "w_in": [D, IN_W], "cmp_pe_k": [32, 64], "cmp_w1_k": [2048, 128], "cmp_w2_k": [128, 64],
           "cmp_pe_v": [32, 64], "cmp_w1_v": [2048, 128], "cmp_w2_v": [128, 64], "w_gate": [D, 3 * D],
           "w_up": [1280, D], "w_out": [D, D], "norm_ffn2": [D], "ffn2_w1": [D, DFF], "ffn2_w3": [D, DFF],
           "ffn2_w2": [DFF, D]}


class Ring:
    def __init__(self, tiles):
        self.t = tiles
        self.i = 0

    def next(self):
        t = self.t[self.i % len(self.t)]
        self.i += 1
        return t


class Builder:
    def __init__(self, S, L, TT=512, dbg=(), phases=None):
        self.S, self.L, self.TT = S, L, TT
        self.NT = S // 128
        self.NG = TT // 512
        self.dbg = set(dbg)
        self.phases = phases
        self.consts = _consts(S)
        nc = bass.Bass("TRN2", target_bir_lowering=False)
        self.nc = nc
        self.es = contextlib.ExitStack()
        self.P = Prog(nc, self.es)
        self.din = {}
        import os
        self.qst = {'sp': self.P.sp, 'pool': self.P.pool, 'act': self.P.act}[os.environ.get('K_QST', 'sp')]
        self._ptoks = {}
        self._pn = 0

    def inp(self, name, shape, dtype):
        d = DT(self.P, name, shape, dtype, kind="ExternalInput")
        self.din[name] = d
        return d

    def scr(self, name, shape, dtype):
        kind = "ExternalOutput" if name in self.dbg else "Internal"
        return DT(self.P, name, shape, dtype, kind=kind)

    def sb(self, name, shape, dtype, es=None):
        return Tn(self.P, name, shape, dtype, "sbuf", es)

    def psum_banks(self, es, n=8):
        return [Tn(self.P, f"psb{i}", [128, 512], F32, "psum", es) for i in range(n)]

    def build(self):
        P, S, L = self.P, self.S, self.L
        self.x = self.inp("x", [S, D], F32)
        self.pos = self.inp("pos", [1, S], I32)
        self.W = {}
        for n in WNAMES:
            self.W[n] = self.inp(n, [L] + WSHAPES[n], F32)
        self.nf = self.inp("norm_final", [D], F32)
        self.C = {}
        for k, v in self.consts.items():
            self.C[k] = self.inp(k, list(v.shape), BF if v.dtype == ml_dtypes.bfloat16 else F32)
        self.out = DT(P, "out", [S, D], F32, kind="ExternalOutput")
        self.HT = [self.scr(f"HT{l}", [8, 128, S], F32) for l in range(L)]
        self.HT2 = [self.scr(f"HTb{l}", [8, 128, S], F32) for l in range(L)]
        self.QT = [self.scr(f"QT{l}", [NQT, 128, S], BF) for l in range(L)]
        self.VT = [self.scr(f"VT{l}", [S, NV], BF) for l in range(L)]
        self.GT = [self.scr(f"GT{l}", [24, 128, S], BF) for l in range(L)]
        self.OTOK = [self.scr(f"OTOK{l}", [S, 896], BF) for l in range(L)]
        self.OTC = [self.scr(f"OTC{l}", [3, 128, S], BF) for l in range(L)]
        self.CS = self.scr("CS", [2, 128, S], F32)
        self.WB = []
        for l in range(L):
            d = {}
            for f in (1, 2):
                d[f"w1_{f}"] = self.scr(f"b_w1_{f}_{l}", [NF, 128, 8 * 128], BF)
                d[f"w3_{f}"] = self.scr(f"b_w3_{f}_{l}", [NF, 128, 8 * 128], BF)
                d[f"w2_{f}"] = self.scr(f"b_w2_{f}_{l}", [8, 128, NF * 128], BF)
            d["win"] = self.scr(f"b_win_{l}", [len(FM), 128, 8 * 128], BF)
            d["wv"] = self.scr(f"b_wv_{l}", [NVG, 128, 8 * VGW], BF)
            d["wg"] = self.scr(f"b_wg_{l}", [24, 128, 8 * 128], BF)
            d["wup"] = self.scr(f"b_wup_{l}", [8, 128, 10 * 128], BF)
            d["wout"] = self.scr(f"b_wout_{l}", [8, 128, 8 * 128], BF)
            self.WB.append(d)

        P.scratch = self.sb("p_scratch", [128, 8], F32)
        self.k_ident = self.sb("k_ident", [128, 128], BF)
        self.k_identf = self.sb("k_identf", [128, 128], F32)
        self.k_perm = self.sb("k_perm", [128, 128], BF)
        self.k_U = self.sb("k_U", [128, 128], BF)
        self.k_ones = self.sb("k_ones", [128, 128], BF)
        self.k_rope = self.sb("k_rope", [128, 2], F32)
        self.k_one = self.sb("k_one", [128, 1], F32)
        P.memset(P.dve, self.k_one[:, :], 1.0)
        for t, n in ((self.k_ident, "c_ident"), (self.k_identf, "c_identf"), (self.k_perm, "c_perm"),
                     (self.k_U, "c_U"), (self.k_ones, "c_ones"), (self.k_rope, "c_rope")):
            P.dma(P.sp, t[:, :], (self.C[n], self.C[n].ap))
        self.g = {}
        for n in ("norm_ffn1", "norm_mix", "norm_ffn2"):
            for l in range(L):
                t = self.sb(f"g_{n}_{l}", [128, 8], F32)
                P.dma(P.sp, t[:, :], (self.W[n], self.W[n].ap[l].rearrange("(c p) -> p c", p=128)),
                      allow_slow_non_contiguous=True)
                self.g[(n, l)] = t
        t = self.sb("g_final", [128, 8], F32)
        P.dma(P.sp, t[:, :], (self.nf, self.nf.ap.rearrange("(c p) -> p c", p=128)), allow_slow_non_contiguous=True)
        self.g[("final", 0)] = t

        ph = self.phases
        P.barrier()
        if ph is None or "p0" in ph:
            P.scope_begin()
            self.phase0()
            P.scope_end()
        for l in range(L):
            if ph is None or "p1" in ph:
                P.scope_begin()
                self.phase1(l)
                P.scope_end()
            if ph is None or "p2" in ph:
                self.phase2(l)
            if ph is None or "p3" in ph:
                P.scope_begin()
                self.phase3(l)
                P.scope_end()
        self.es.close()
        return self.nc

    def _prep(self, dst, j, src2d, runs, nchunk, width):
        P = self.P
        off = 0
        dv = dst.ap[j].rearrange("p (c n) -> p c n", n=width)
        for (col, w) in runs:
            src = src2d[:, col:col + w].rearrange("(c p) n -> p c n", p=128)
            P.dma(P.pool, (dst, dv[:, :, off:off + w]), (self.x, src))
            off += w
            self._ptoks[dst.sem.num] = (dst.sem, dst.semv, None)
            self._pn += 1
            if self._pn % 12 == 0:
                P.pool.wait(list(self._ptoks.values()))

    def phase0(self):
        import os
        P, S, L = self.P, self.S, self.L
        if os.environ.get('K_SKIP_ROPE') is None:
            self.rope_tables()
        if os.environ.get('K_SKIP_PREP'):
            return
        for l in range(L):
            wb = self.WB[l]
            for f in (1, 2):
                w1 = self.W[f"ffn{f}_w1"].ap[l]
                w3 = self.W[f"ffn{f}_w3"].ap[l]
                w2 = self.W[f"ffn{f}_w2"].ap[l]
                for j in range(NF):
                    self._prep(wb[f"w1_{f}"], j, w1, [(j * 128, 128)], 8, 128)
                    self._prep(wb[f"w3_{f}"], j, w3, [(j * 128, 128)], 8, 128)
                for j in range(8):
                    self._prep(wb[f"w2_{f}"], j, w2, [(j * 128, 128)], NF, 128)
                if f == 1:
                    win = self.W["w_in"].ap[l]
                    for j, blk in enumerate(FM):
                        self._prep(wb["win"], j, win, blk[1], 8, 128)
                    cols = []
                    for (c0, w) in VRUNS:
                        cols += [(c0, w)]
                    flat = []
                    for (c0, w) in cols:
                        flat.append([c0, w])
                    for gi in range(NVG):
                        need = VGW
                        runs = []
                        while need > 0:
                            c0, w = flat[0]
                            take = min(w, need)
                            runs.append((c0, take))
                            need -= take
                            if take == w:
                                flat.pop(0)
                            else:
                                flat[0] = [c0 + take, w - take]
                        self._prep(wb["wv"], gi, win, runs, 8, VGW)
                    wg = self.W["w_gate"].ap[l]
                    for j in range(24):
                        self._prep(wb["wg"], j, wg, [(j * 128, 128)], 8, 128)
                    wu = self.W["w_up"].ap[l]
                    for j in range(8):
                        self._prep(wb["wup"], j, wu, [(j * 128, 128)], 10, 128)
                    wo = self.W["w_out"].ap[l]
                    for j in range(8):
                        self._prep(wb["wout"], j, wo, [(j * 128, 128)], 8, 128)

    def rope_tables(self):
        P, S = self.P, self.S
        PI = float(np.pi)
        with contextlib.ExitStack() as es:
            pi_ = self.sb("rp_pi", [128, S], I32, es)
            a = self.sb("rp_a", [128, S], F32, es)
            b = self.sb("rp_b", [128, S], F32, es)
            r = self.sb("rp_r", [128, S], F32, es)
            m = self.sb("rp_m", [128, S], F32, es)
            dve = P.dve
            P.dma(P.sp, pi_[:, :], (self.pos, self.pos.ap[0].partition_broadcast(128)))
            P.copy(dve, a[:, :], pi_[:, :])
            P.ts(dve, a[:, :], a[:, :], self.k_rope[:, 0:1], None, ALU.mult)
            P.ts(dve, b[:, :], a[:, :], 1.0 / (2 * PI), None, ALU.mult)
            ki = pi_
            P.copy(dve, ki[:, :], b[:, :])
            P.copy(dve, b[:, :], ki[:, :])
            C1 = 6.28125
            C2 = float(2 * np.pi - 6.28125)
            P.stt(r[:, :], b[:, :], -C1, a[:, :], ALU.mult, ALU.add)
            P.stt(r[:, :], b[:, :], -C2, r[:, :], ALU.mult, ALU.add)
            for (thr, op, adj) in ((PI, ALU.is_gt, -2 * PI), (-PI, ALU.is_lt, 2 * PI), (PI, ALU.is_gt, -2 * PI),
                                   (-PI, ALU.is_lt, 2 * PI)):
                P.ts(dve, m[:, :], r[:, :], thr, adj, op, ALU.mult)
                P.tt(dve, r[:, :], r[:, :], m[:, :], ALU.add)
            LIM = 3.1415925
            P.ts(dve, r[:, :], r[:, :], LIM, -LIM, ALU.min, ALU.max)
            P.activation(m[:, :], r[:, :], AF.Sin)
            P.ts(dve, m[:, :], m[:, :], self.k_rope[:, 1:2], None, ALU.mult)
            P.dma(P.pool, (self.CS, self.CS.ap[1]), m[:, :])
            P.ts(dve, b[:, :], r[:, :], -1.0, None, ALU.mult)
            P.tt(dve, b[:, :], b[:, :], r[:, :], ALU.max)
            P.ts(dve, b[:, :], b[:, :], -1.0, PI / 2, ALU.mult, ALU.add)
            P.activation(a[:, :], b[:, :], AF.Sin)
            P.dma(P.pool, (self.CS, self.CS.ap[0]), a[:, :])
            P.barrier()

    def rl_alloc(self, es):
        TT = self.TT
        R = {}
        R["hT"] = [self.sb(f"hT{c}", [128, TT], F32, es) for c in range(8)]
        R["xn"] = [self.sb(f"xn{c}", [128, TT], BF, es) for c in range(8)]
        R["h1"] = [self.sb(f"h1_{j}", [128, TT], BF, es) for j in range(NF)]
        R["sq"] = Ring([self.sb(f"sq{i}", [128, 512], BF, es) for i in range(2)])
        R["rt"] = [self.sb(f"rt{i}", [128, 512], F32, es) for i in range(self.NG)]
        R["wr"] = Ring([self.sb(f"wr{i}", [128, 8 * 128], BF, es) for i in range(6)])
        R["w2r"] = Ring([self.sb(f"w2r{i}", [128, NF * 128], BF, es) for i in range(2)])
        R["sa"] = Ring([self.sb(f"sa{i}", [128, 512], F32, es) for i in range(2)])
        R["ps"] = Ring(self.psum_banks(es, 7))
        return R

    def rmsnorm(self, R, g, out, out_f32=False):
        P = self.P
        for n in range(self.NG):
            ns = slice(n * 512, (n + 1) * 512)
            ps = R["ps"].next()
            for c in range(8):
                sq = R["sq"].next()
                P.activation(sq[:, :], R["hT"][c][:, ns], AF.Square)
                P.mm(ps[:, :], self.k_ones[:, :], sq[:, :], start=(c == 0), stop=(c == 7), sig=True)
            rt = R["rt"][n]
            P.activation(rt[:, :], ps[:, :], AF.Sqrt, scale=1.0 / D, bias=self.k_eps[:, 0:1])
            P.recip(rt[:, :], rt[:, :])
            for c in range(8):
                P.stt(out[c][:, ns], R["hT"][c][:, ns], g[:, c:c + 1], rt[:, :], ALU.mult, ALU.mult)

    def ffn(self, R, l, f):
        P = self.P
        wb = self.WB[l]
        W1, W3, W2 = wb[f"w1_{f}"], wb[f"w3_{f}"], wb[f"w2_{f}"]
        xn, h1, hT = R["xn"], R["h1"], R["hT"]
        for j in range(NF):
            wa = R["wr"].next()
            P.dma(P.sp, wa[:, :], (W1, W1.ap[j]))
            wc = R["wr"].next()
            P.dma(P.sp, wc[:, :], (W3, W3.ap[j]))
            for n in range(self.NG):
                ns = slice(n * 512, (n + 1) * 512)
                pa = R["ps"].next()
                pb = R["ps"].next()
                for c in range(8):
                    P.mm(pa[:, :], wa[:, c * 128:(c + 1) * 128], xn[c][:, ns], start=(c == 0), stop=(c == 7))
                for c in range(8):
                    P.mm(pb[:, :], wc[:, c * 128:(c + 1) * 128], xn[c][:, ns], start=(c == 0), stop=(c == 7))
                sa = R["sa"].next()
                P.activation(sa[:, :], pa[:, :], AF.Silu)
                P.tt(P.dve, h1[j][:, ns], sa[:, :], pb[:, :], ALU.mult)
        for dm in range(8):
            w2 = R["w2r"].next()
            P.dma(P.sp, w2[:, :], (W2, W2.ap[dm]))
            for n in range(self.NG):
                ns = slice(n * 512, (n + 1) * 512)
                py = R["ps"].next()
                for j in range(NF):
                    P.mm(py[:, :], w2[:, j * 128:(j + 1) * 128], h1[j][:, ns], start=(j == 0), stop=(j == NF - 1))
                P.stt(hT[dm][:, ns], py[:, :], 0.5, hT[dm][:, ns], ALU.mult, ALU.add)

    def phase1(self, l):
        P, S, TT = self.P, self.S, self.TT
        wb = self.WB[l]
        with contextlib.ExitStack() as es:
            R = self.rl_alloc(es)
            hT, xn = R["hT"], R["xn"]
            self.k_eps = self.sb("k_eps", [128, 1], F32, es)
            P.memset(P.dve, self.k_eps[:, :], EPS)
            ctab = self.sb("ctab", [128, TT], F32, es)
            stab = self.sb("stab", [128, TT], F32, es)
            qsb = Ring([self.sb(f"qsb{i}", [128, 512], BF, es) for i in range(3)])
            t1 = Ring([self.sb(f"t1_{i}", [128, 512], F32, es) for i in range(2)])
            t2 = Ring([self.sb(f"t2_{i}", [128, 512], F32, es) for i in range(2)])
            stg = Ring([self.sb(f"stg{i}", [128, 512], BF, es) for i in range(3)])
            wvr = Ring([self.sb(f"wvr{i}", [128, 8 * VGW], BF, es) for i in range(2)])
            vst = [self.sb(f"vst{i}", [128, NV], BF, es) for i in range(TT // 128)]
            xin = Ring([self.sb(f"xin{i}", [128, D], F32, es) for i in range(2)]) if l == 0 else None
            for tt in range(S // TT):
                t0 = tt * TT
                tsl = slice(t0, t0 + TT)
                if l == 0:
                    for ts in range(TT // 128):
                        xi = xin.next()
                        P.dma(P.sp, xi[:, :], (self.x, self.x.ap[t0 + ts * 128:t0 + (ts + 1) * 128, :]))
                        for half in range(2):
                            ps = R["ps"].next()
                            for q in range(4):
                                c = half * 4 + q
                                P.transpose(ps[:, q * 128:(q + 1) * 128], xi[:, c * 128:(c + 1) * 128],
                                            self.k_identf[:, :], sig=(q == 3))
                            for q in range(4):
                                c = half * 4 + q
                                P.copy(P.dve if q % 2 else P.act, hT[c][:, ts * 128:(ts + 1) * 128],
                                       ps[:, q * 128:(q + 1) * 128])
                else:
                    src = self.HT2[l - 1]
                    for c in range(8):
                        P.dma(P.sp, hT[c][:, :], (src, src.ap[c, :, tsl]))
                P.dma(P.sp, ctab[:, :], (self.CS, self.CS.ap[0, :, tsl]))
                P.dma(P.sp, stab[:, :], (self.CS, self.CS.ap[1, :, tsl]))
                import os
                stop = os.environ.get('K_P1_STOP', '')
                if stop == 'load':
                    continue
                self.rmsnorm(R, self.g[("norm_ffn1", l)], xn)
                if stop == 'norm':
                    continue
                self.ffn(R, l, 1)
                if stop == 'ffn':
                    continue
                self.rmsnorm(R, self.g[("norm_mix", l)], xn)
                QT = self.QT[l]
                for bi, blk in enumerate(FM):
                    w = R["wr"].next()
                    P.dma(P.sp, w[:, :], (wb["win"], wb["win"].ap[bi]))
                    for n in range(self.NG):
                        ns = slice(n * 512, (n + 1) * 512)
                        dsl = slice(t0 + n * 512, t0 + (n + 1) * 512)
                        pm = R["ps"].next()
                        for c in range(8):
                            P.mm(pm[:, :], w[:, c * 128:(c + 1) * 128], xn[c][:, ns], start=(c == 0), stop=(c == 7))
                        qs = qsb.next()
                        P.activation(qs[:, :], pm[:, :], AF.Identity)
                        if blk[3] is not None:
                            P.dma(self.qst, (QT, QT.ap[blk[3], :, dsl]), qs[:, :])
                        if blk[2] is not None:
                            psw = R["ps"].next()
                            P.mm(psw[:, :], self.k_perm[:, :], qs[:, :], start=True, stop=True)
                            a = t1.next()
                            b = t2.next()
                            P.tt(P.dve, a[:, :], pm[:, :], ctab[:, ns], ALU.mult)
                            P.tt(P.dve, b[:, :], psw[:, :], stab[:, ns], ALU.mult)
                            st = stg.next()
                            P.tt(P.pool, st[:, :], a[:, :], b[:, :], ALU.add)
                            P.dma(self.qst, (QT, QT.ap[blk[2], :, dsl]), st[:, :])
                if stop == 'fm':
                    continue
                VT = self.VT[l]
                for gi in range(NVG):
                    wv = wvr.next()
                    P.dma(P.sp, wv[:, :], (wb["wv"], wb["wv"].ap[gi]))
                    for ts in range(TT // 128):
                        pv = R["ps"].next()
                        for c in range(8):
                            P.mm(pv[:, 0:VGW], xn[c][:, ts * 128:(ts + 1) * 128], wv[:, c * VGW:(c + 1) * VGW],
                                 start=(c == 0), stop=(c == 7))
                        P.activation(vst[ts][:, gi * VGW:(gi + 1) * VGW], pv[:, 0:VGW], AF.Identity)
                for ts in range(TT // 128):
                    P.dma(self.qst, (VT, VT.ap[t0 + ts * 128:t0 + (ts + 1) * 128, :]), vst[ts][:, :])
                if stop == 'v':
                    continue
                GT = self.GT[l]
                for bi in range(24):
                    w = R["wr"].next()
                    P.dma(P.sp, w[:, :], (wb["wg"], wb["wg"].ap[bi]))
                    for n in range(self.NG):
                        ns = slice(n * 512, (n + 1) * 512)
                        dsl = slice(t0 + n * 512, t0 + (n + 1) * 512)
                        pg = R["ps"].next()
                        for c in range(8):
                            P.mm(pg[:, :], w[:, c * 128:(c + 1) * 128], xn[c][:, ns], start=(c == 0), stop=(c == 7))
                        st = stg.next()
                        P.activation(st[:, :], pg[:, :], AF.Sigmoid)
                        P.dma(self.qst, (GT, GT.ap[bi, :, dsl]), st[:, :])
                if stop == 'gate':
                    continue
                HT = self.HT[l]
                for c in range(8):
                    P.dma(self.qst, (HT, HT.ap[c, :, tsl]), hT[c][:, :])
            P.barrier()

    def phase3(self, l):
        P, S, TT = self.P, self.S, self.TT
        wb = self.WB[l]
        last = (l == self.L - 1)
        with contextlib.ExitStack() as es:
            R = self.rl_alloc(es)
            hT, xn = R["hT"], R["xn"]
            self.k_eps = self.sb("k_eps3", [128, 1], F32, es)
            P.memset(P.dve, self.k_eps[:, :], EPS)
            oT = [self.sb(f"oT{k}", [128, TT], BF, es) for k in range(10)]
            otok = [self.sb(f"otok{i}", [128, 896], BF, es) for i in range(TT // 128)]
            psT = Tn(P, "psT", [128, 1024], BF, "psum", es)
            gr = Ring([self.sb(f"gr{i}", [128, TT], BF, es) for i in range(6)])
            y = [self.sb(f"y{k}", [128, TT], BF, es) for k in range(8)]
            wur = Ring([self.sb(f"wur{i}", [128, 10 * 128], BF, es) for i in range(2)])
            t1 = Ring([self.sb(f"t31_{i}", [128, 512], F32, es) for i in range(2)])
            t2 = Ring([self.sb(f"t32_{i}", [128, 512], F32, es) for i in range(2)])
            if last:
                xof = [self.sb(f"xof{c}", [128, TT], F32, es) for c in range(8)]
                ost = Ring([self.sb(f"ost{i}", [128, D], F32, es) for i in range(2)])
            GT, OTOK, OTC, HT = self.GT[l], self.OTOK[l], self.OTC[l], self.HT[l]
            for tt in range(S // TT):
                t0 = tt * TT
                tsl = slice(t0, t0 + TT)
                for c in range(8):
                    P.dma(P.sp, hT[c][:, :], (HT, HT.ap[c, :, tsl]))
                for hp in range(3):
                    P.dma(P.sp, oT[7 + hp][:, :], (OTC, OTC.ap[hp, :, tsl]))
                for ts in range(TT // 128):
                    P.dma(P.sp, otok[ts][:, :], (OTOK, OTOK.ap[t0 + ts * 128:t0 + (ts + 1) * 128, :]))
                for kb in range(7):
                    for n in range(self.NG):
                        for q in range(4):
                            ts = n * 4 + q
                            P.transpose(psT[:, q * 128:(q + 1) * 128], otok[ts][:, kb * 128:(kb + 1) * 128],
                                        self.k_ident[:, :], sig=(q == 3))
                        P.copy(P.dve if kb % 2 else P.act, oT[kb][:, n * 512:(n + 1) * 512], psT[:, 0:512])
                for dm in range(8):
                    w = wur.next()
                    P.dma(P.sp, w[:, :], (wb["wup"], wb["wup"].ap[dm]))
                    gs = []
                    for m in range(3):
                        gt = gr.next()
                        P.dma(P.sp, gt[:, :], (GT, GT.ap[m * 8 + dm, :, tsl]))
                        gs.append(gt)
                    for n in range(self.NG):
                        ns = slice(n * 512, (n + 1) * 512)
                        pp = []
                        for (k0, k1) in ((0, 3), (3, 7), (7, 10)):
                            p_ = R["ps"].next()
                            for k in range(k0, k1):
                                P.mm(p_[:, :], w[:, k * 128:(k + 1) * 128], oT[k][:, ns], start=(k == k0),
                                     stop=(k == k1 - 1))
                            pp.append(p_)
                        a = t1.next()
                        b = t2.next()
                        P.tt(P.dve, a[:, :], pp[0][:, :], gs[0][:, ns], ALU.mult)
                        P.tt(P.dve, b[:, :], pp[1][:, :], gs[1][:, ns], ALU.mult)
                        P.tt(P.pool, a[:, :], a[:, :], b[:, :], ALU.add)
                        b2 = t2.next()
                        P.tt(P.dve, b2[:, :], pp[2][:, :], gs[2][:, ns], ALU.mult)
                        P.tt(P.pool, y[dm][:, ns], a[:, :], b2[:, :], ALU.add)
                for dm2 in range(8):
                    w = R["wr"].next()
                    P.dma(P.sp, w[:, :], (wb["wout"], wb["wout"].ap[dm2]))
                    for n in range(self.NG):
                        ns = slice(n * 512, (n + 1) * 512)
                        p_ = R["ps"].next()
                        for k in range(8):
                            P.mm(p_[:, :], w[:, k * 128:(k + 1) * 128], y[k][:, ns], start=(k == 0), stop=(k == 7))
                        P.tt(P.dve, hT[dm2][:, ns], p_[:, :], hT[dm2][:, ns], ALU.add)
                self.rmsnorm(R, self.g[("norm_ffn2", l)], xn)
                self.ffn(R, l, 2)
                if not last:
                    H2 = self.HT2[l]
                    for c in range(8):
                        P.dma(self.qst, (H2, H2.ap[c, :, tsl]), hT[c][:, :])
                else:
                    self.rmsnorm(R, self.g[("final", 0)], xof)
                    for ts in range(TT // 128):
                        o_ = ost.next()
                        for half in range(2):
                            ps = R["ps"].next()
                            for q in range(4):
                                c = half * 4 + q
                                P.transpose(ps[:, q * 128:(q + 1) * 128], xof[c][:, ts * 128:(ts + 1) * 128],
                                            self.k_identf[:, :], sig=(q == 3))
                            P.copy(P.dve if half else P.act, o_[:, half * 512:(half + 1) * 512], ps[:, :])
                        P.dma(self.qst, (self.out, self.out.ap[t0 + ts * 128:t0 + (ts + 1) * 128, :]), o_[:, :])
            P.barrier()

    def attn(self, pairs, O, vw, X, nsub=4, outs=None, starts=(0,)):
        P = self.P
        n = len(pairs)
        sps = [None] * n
        pbs = [None] * n

        def stage_s(i):
            p = pairs[i]
            s_ = X["S"].next()
            sps[i] = s_
            nb = len(p["bias"])
            P.mm(s_[:, :], p["kT"], p["q"], start=True, stop=(nb == 0))
            for bi, (lh, rh) in enumerate(p["bias"]):
                P.mm(s_[:, :], lh, rh, start=False, stop=(bi == nb - 1))

        def stage_pv(i):
            t = X["P"].next()
            pbs[i] = t
            P.activation(t[:, :], sps[i][:, :], AF.Exp, scale=0.125)
            for sub in range(nsub):
                o_ = outs[sub] if outs is not None else O[:, sub * vw:(sub + 1) * vw]
                P.mm(o_, t[:, sub * 128:(sub + 1) * 128], pairs[i]["v"],
                     start=(i == 0 and sub in starts), stop=(i == n - 1), sig=(sub == nsub - 1))

        stage_s(0)
        for i in range(n):
            if i + 1 < n:
                stage_s(i + 1)
            stage_pv(i)

    def load_v_tm(self, dst, VT, c0, ncol, nh, wcol):
        P = self.P
        NT = self.NT
        src = VT.ap[:, c0:c0 + ncol].rearrange("(s p) (h d) -> p s h d", p=128, h=nh)
        first = True
        for s0 in range(0, NT, 16):
            s1 = min(NT, s0 + 16)
            for h in range(nh):
                P.dma(P.sp, dst.v(dst.h[:, s0:s1, h, 0:64]), (VT, src[:, s0:s1, h, :]), part=(not first))
                first = False

    def phase2(self, l):
        import os
        sel = os.environ.get("K_P2", "abc")
        for k, fn in (("a", self.mixer_a), ("b", self.mixer_b), ("c", self.mixer_c)):
            if k in sel:
                self.P.scope_begin()
                fn(l)
                self.P.scope_end()

    def p2_pre(self, l):
        pass

    def mixer_a(self, l):
        import os
        skip = os.environ.get('K_A_SKIP', '')
        P, S, NT = self.P, self.S, self.NT
        QT, VT, OTOK = self.QT[l], self.VT[l], self.OTOK[l]
        with contextlib.ExitStack() as es:
            am = self.sb("amask", [128, 33, 512], BF, es)
            for t0 in range(0, 33, 11):
                P.dma(P.sp, am.v(am.h[:, t0:t0 + 11, :]),
                      (self.C["c_amask"], self.C["c_amask"].ap[t0:t0 + 11].rearrange("t p n -> p t n")), part=(t0 > 0))
            qTb = [self.sb(f"a_q{g}", [128, S], BF, es) for g in range(3)]
            kTb = [self.sb(f"a_k{g}", [128, S], BF, es) for g in range(3)]
            Vb = [self.sb(f"a_v{g}", [128, NT, 2, 65], BF, es) for g in range(3)]
            for g in range(3):
                if 'memset' in skip:
                    continue
                P.memset(P.pool, Vb[g].v(Vb[g].h[:, :, :, 64:65]), 1.0)
            X = {"S": Ring(self.psum_banks(es, 3)), "P": Ring([self.sb(f"a_p{i}", [128, 512], BF, es) for i in range(3)])}
            Ob = Ring([Tn(P, f"a_o{i}", [128, 512], F32, "psum", es) for i in range(2)])
            rl = Ring([self.sb(f"a_rl{i}", [128, 4, 1], F32, es) for i in range(2)])
            ost = Ring([self.sb(f"a_ost{i}", [128, 4, 128], BF, es) for i in range(2)])
            base = [0, 5, 13]
            for sp in range(3):
                for g in range(3):
                    P.dma(P.sp, qTb[g][:, :], (QT, QT.ap[g * 3 + sp]))
                    P.dma(P.sp, kTb[g][:, :], (QT, QT.ap[9 + g * 3 + sp]))
                    self.load_v_tm(Vb[g], VT, VO_A + g * 384 + sp * 128, 128, 2, 65)
                for tt in range(S // 512):
                    o_st = ost.next()
                    for h in range(2):
                        hs = slice(64 * h, 64 * h + 64)
                        pairs = []
                        for g in range(3):
                            dil = A_DIL[g]
                            for sg in range(max(0, 4 * tt - dil), 4 * tt + 4):
                                dl = 4 * tt - sg
                                pairs.append({
                                    "kT": kTb[g][hs, sg * 128:(sg + 1) * 128],
                                    "q": qTb[g][hs, tt * 512:(tt + 1) * 512],
                                    "bias": [(self.k_ident[:, :], am.v(am.h[:, base[g] + dl + 3, :]))],
                                    "v": Vb[g].v(Vb[g].h[:, sg, h, :]),
                                })
                        O = Ob.next()
                        if 'attn' not in skip:
                            self.attn(pairs, O, 65, X)
                        if 'epi' in skip:
                            continue
                        O3 = O.h[:, 0:260].rearrange("p (i c) -> p i c", c=65)
                        r_ = rl.next()
                        P.recip(r_[:, :, :], O.v(O3[:, :, 64:65]))
                        P.tt(P.dve, o_st.v(o_st.h[:, :, 64 * h:64 * h + 64]), O.v(O3[:, :, 0:64]),
                             r_.v(r_.h[:, :, :].broadcast_to([128, 4, 64])), ALU.mult)
                    if 'store' in skip:
                        continue
                    P.dma(self.qst, (OTOK, OTOK.ap[tt * 512:(tt + 1) * 512, sp * 128:(sp + 1) * 128]
                                     .rearrange("(i p) c -> p i c", p=128)), o_st[:, :, :])
            P.barrier()

    def mixer_c(self, l):
        P, S, NT = self.P, self.S, self.NT
        QT, VT, OTC = self.QT[l], self.VT[l], self.OTC[l]
        with contextlib.ExitStack() as es:
            cm = self.sb("cmask", [128, 4, 512], BF, es)
            P.dma(P.sp, cm[:, :, :], (self.C["c_cmask"], self.C["c_cmask"].ap.rearrange("t p n -> p t n")))
            qT = self.sb("c_q", [128, S], BF, es)
            kT = self.sb("c_k", [128, S], BF, es)
            Vb = self.sb("c_v", [128, NT, 2, 64], BF, es)
            Zr = Ring(self.psum_banks(es, 3))
            Cr = Ring(self.psum_banks(es, 3))
            Ob = Ring([Tn(P, f"c_o{i}", [128, 512], F32, "psum", es) for i in range(2)])
            Er = Ring([self.sb(f"c_e{i}", [128, 512], F32, es) for i in range(6)])
            Sr = Ring([self.sb(f"c_s{i}", [128, 512], BF, es) for i in range(6)])
            Xr = Ring([self.sb(f"c_x{i}", [128, 512], F32, es) for i in range(4)])
            Ar = Ring([self.sb(f"c_a{i}", [128, 512], BF, es) for i in range(6)])
            ssum = [self.sb(f"c_ss{h}", [128, 512], F32, es) for h in range(2)]
            ssbf = [Ring([self.sb(f"c_sb{h}_{i}", [128, 512], BF, es) for i in range(2)]) for h in range(2)]
            ostg = Ring([self.sb(f"c_og{i}", [128, 512], BF, es) for i in range(2)])
            for hp in range(3):
                P.dma(P.sp, qT[:, :], (QT, QT.ap[30 + hp]))
                P.dma(P.sp, kT[:, :], (QT, QT.ap[33 + hp]))
                self.load_v_tm(Vb, VT, VO_C + hp * 128, 128, 2, 64)
                for tt in range(S // 512):
                    O = Ob.next()
                    sgs = list(range(4 * tt + 3, -1, -1))
                    n = len(sgs)
                    st = {}

                    def s0(i, h):
                        sg = sgs[i]
                        hs = slice(64 * h, 64 * h + 64)
                        z = Zr.next()
                        P.mm(z[:, :], kT[hs, sg * 128:(sg + 1) * 128], qT[hs, tt * 512:(tt + 1) * 512], start=True, stop=True)
                        e = Er.next()
                        P.activation(e[:, :], z[:, :], AF.Exp, scale=0.125)
                        if sg >= 4 * tt:
                            P.tt(P.dve, e[:, :], e[:, :], cm.v(cm.h[:, 4 * tt - sg + 3, :]), ALU.mult)
                        sp_ = Sr.next()
                        P.activation(sp_[:, :], e[:, :], AF.Ln, bias=self.k_one[:, 0:1])
                        st[(i, h)] = [e, sp_, None, None]

                    def s1(i, h):
                        e, sp_, _, _ = st[(i, h)]
                        c_ = Cr.next()
                        P.mm(c_[:, :], self.k_U[:, :], sp_[:, :], start=True, stop=(i == 0))
                        if i > 0:
                            P.mm(c_[:, :], self.k_ones[:, :], st[("sb", h)][:, :], start=False, stop=True)
                        if i == 0:
                            P.copy(P.pool, ssum[h][:, :], sp_[:, :])
                        else:
                            P.tt(P.pool, ssum[h][:, :], ssum[h][:, :], sp_[:, :], ALU.add)
                        if i + 1 < n:
                            sb_ = ssbf[h].next()
                            P.copy(P.pool, sb_[:, :], ssum[h][:, :])
                            st[("sb", h)] = sb_
                        x_ = Xr.next()
                        P.activation(x_[:, :], c_[:, :], AF.Exp, scale=-1.0)
                        a_ = Ar.next()
                        P.tt(P.dve, a_[:, :], e[:, :], x_[:, :], ALU.mult)
                        st[(i, h)][2] = a_

                    def s2(i, h):
                        sg = sgs[i]
                        a_ = st[(i, h)][2]
                        P.mm(O[64 * h:64 * h + 64, :], Vb.v(Vb.h[:, sg, h, :]), a_[:, :], start=(i == 0), stop=(i == n - 1))

                    stages = [s0, s1, s2]
                    for it in range(n + 2):
                        for k, fn in enumerate(stages):
                            i = it - k
                            if 0 <= i < n:
                                for h in range(2):
                                    fn(i, h)
                    og = ostg.next()
                    P.activation(og[:, :], O[:, :], AF.Identity)
                    P.dma(self.qst, (OTC, OTC.ap[hp, :, tt * 512:(tt + 1) * 512]), og[:, :])
            P.barrier()

    def mixer_b(self, l):
        import os
        skip = os.environ.get('K_B_SKIP', '')
        P, S, NT = self.P, self.S, self.NT
        QT, VT, OTOK = self.QT[l], self.VT[l], self.OTOK[l]
        ncmp = S // 16 - 1
        GC = 0.7978845608028654
        with contextlib.ExitStack() as es:
            kslc = self.sb("b_kslc", [128, S], BF, es)
            kwin = self.sb("b_kwin", [128, S], BF, es)
            P.dma(P.sp, kslc[:, :], (QT, QT.ap[28]))
            P.dma(P.sp, kwin[:, :], (QT, QT.ap[29]))
            Vs = self.sb("b_vs", [128, NT, 2, 65], BF, es)
            Vw = self.sb("b_vw", [128, NT, 2, 65], BF, es)
            for (vb, c0) in ((Vs, VO_SLC), (Vw, VO_WIN)):
                P.memset(P.pool, vb.v(vb.h[:, :, :, 64:65]), 1.0)
                self.load_v_tm(vb, VT, c0, 128, 2, 65)
            kcT = self.sb("b_kcT", [128, 256], BF, es)
            vca = [self.sb(f"b_vca{g}", [128, 2, 129], BF, es) for g in range(2)]
            wm = self.sb("b_wm", [128, 8, 512], BF, es)
            P.dma(P.sp, wm[:, :, :], (self.C["c_wmask"], self.C["c_wmask"].ap.rearrange("t p n -> p t n")))
            sm = self.sb("b_sm", [128, 4, 512], BF, es)
            P.dma(P.sp, sm[:, :, :], (self.C["c_smask"], self.C["c_smask"].ap.rearrange("t p n -> p t n")))
            esel = self.sb("b_esel", [128, NT, 128], BF, es)
            for hh in range(2):
                P.dma(P.sp, esel.v(esel.h[64 * hh:64 * hh + 64, :, :]), (self.C["c_esel"], self.C["c_esel"].ap), part=(hh > 0))
            vnf = self.sb("b_vnf", [128, NT, 64], F32, es)
            add = self.sb("b_add", [128, NT, 64], F32, es)
            P.dma(P.sp, vnf[:, :, :], (self.C["c_vnf"], self.C["c_vnf"].ap.rearrange("(s p) j -> p s j", p=128)))
            P.dma(P.sp, add[:, :, :], (self.C["c_add"], self.C["c_add"].ap.rearrange("(s p) j -> p s j", p=128)))
            Sr = Ring(self.psum_banks(es, 3))
            Ob = [Tn(P, f"b_o{i}", [128, 512], F32, "psum", es) for i in range(2)]
            psT = Tn(P, "b_psT", [128, 1024], BF, "psum", es)
            X = {"S": Sr, "P": Ring([self.sb(f"b_p{i}", [128, 512], BF, es) for i in range(3)])}

            with contextlib.ExitStack() as es2:
                for which, (qi, pe_n, w1_n, w2_n) in enumerate(((26, "cmp_pe_k", "cmp_w1_k", "cmp_w2_k"),
                                                              (27, "cmp_pe_v", "cmp_w1_v", "cmp_w2_v"))):
                    if 'pro' in skip:
                        continue
                    src = self.sb(f"b_cin{which}", [128, S], BF, es2)
                    P.dma(P.sp, src[:, :], (QT, QT.ap[qi]))
                    w1T = self.sb(f"b_w1T{which}", [128, 32, 128], BF, es2)
                    peT = self.sb(f"b_peT{which}", [128, 32], F32, es2)
                    w2 = self.sb(f"b_w2{which}", [128, 128], BF, es2)
                    w1src = self.W[w1_n].ap[l].rearrange("(p d) h -> d p h", d=64)
                    pesrc = self.W[pe_n].ap[l].rearrange("p d -> d p")
                    for hh in range(2):
                        P.dma(P.pool, w1T.v(w1T.h[64 * hh:64 * hh + 64, :, :]), (self.x, w1src), part=(hh > 0))
                        P.dma(P.sp, peT.v(peT.h[64 * hh:64 * hh + 64, :]), (self.x, pesrc), part=(hh > 0),
                              allow_slow_non_contiguous=True)
                        P.dma(P.pool, w2.v(w2.h[:, 64 * hh:64 * hh + 64]), (self.x, self.W[w2_n].ap[l]), part=(hh > 0))
                    kpe = self.sb(f"b_kpe{which}", [128, 32, 256], BF, es2)
                    P.memset(P.pool, kpe[:, :, :], 0.0)
                    win_ap = bass.AP(tensor=src.h, offset=0, ap=[[S, 128], [1, 32], [16, ncmp]])
                    P.tt(P.dve, kpe.v(kpe.h[:, :, 0:ncmp]), src.v(win_ap),
                         peT.v(peT.h[:, :].unsqueeze(2).broadcast_to([128, 32, ncmp])), ALU.add)
                    xs = self.sb(f"b_xs{which}", [128, 256], F32, es2)
                    x2 = self.sb(f"b_x2{which}", [128, 256], F32, es2)
                    hg = self.sb(f"b_hg{which}", [128, 256], BF, es2)
                    for g in range(2):
                        hs = slice(64 * g, 64 * g + 64)
                        ph = Sr.next()
                        for p_ in range(32):
                            P.mm(ph[:, 0:256], w1T.v(w1T.h[hs, p_, :]), kpe.v(kpe.h[hs, p_, :]), start=(p_ == 0),
                                 stop=(p_ == 31))
                        P.activation(xs[:, :], ph[:, 0:256], AF.Identity)
                        P.tt(P.dve, x2[:, :], xs[:, :], xs[:, :], ALU.mult)
                        P.ts(P.dve, x2[:, :], x2[:, :], 0.044715, 1.0, ALU.mult, ALU.add)
                        P.tt(P.dve, x2[:, :], x2[:, :], xs[:, :], ALU.mult)
                        P.activation(x2[:, :], x2[:, :], AF.Sigmoid, scale=2.0 * GC)
                        P.tt(P.dve, hg[:, :], xs[:, :], x2[:, :], ALU.mult)
                        if which == 0:
                            pk = Sr.next()
                            P.mm(pk[:, 0:256], w2[:, :], hg[:, :], start=True, stop=True)
                            P.copy(P.act, kcT[hs, :], pk[hs, 0:256])
                        else:
                            for ct in range(2):
                                pv = Sr.next()
                                P.mm(pv[:, 0:64], hg[:, ct * 128:(ct + 1) * 128], w2[:, 0:64], start=True, stop=True)
                                P.copy(P.act, vca[g].v(vca[g].h[:, ct, 0:64]), pv[:, 0:64])
                for g in range(2):
                    P.memset(P.pool, vca[g].v(vca[g].h[:, :, 64:65]), 1.0)
                    P.dma(P.sp, vca[g].v(vca[g].h[:, :, 65:129]),
                          (self.C["c_wsel"], self.C["c_wsel"].ap.rearrange("(ct p) j -> p ct j", p=128)))
                P.barrier()

            qu = [Ring([self.sb(f"b_qu{r}_{i}", [128, 512], BF, es) for i in range(2)]) for r in range(4)]
            qr = [Ring([self.sb(f"b_qr{r}_{i}", [128, 512], BF, es) for i in range(2)]) for r in range(4)]
            glr = Ring([self.sb(f"b_gl{i}", [128, 4, 24], BF, es) for i in range(2)])
            gsr = Ring([self.sb(f"b_gs{i}", [128, 4, 24], F32, es) for i in range(2)])
            cbr = Ring([self.sb(f"b_cb{i}", [128, 2, 512], BF, es) for i in range(2)])
            sacc = [self.sb(f"b_sacc{g}", [128, 4, 64], F32, es) for g in range(2)]
            oB = self.sb("b_oB", [128, 4, 512], F32, es)
            ob16 = Ring([self.sb(f"b_ob16_{i}", [128, 4, 512], BF, es) for i in range(2)])
            BT = self.sb("b_BT", [128, 512], BF, es)
            rlr = Ring([self.sb(f"b_rl{i}", [128, 4, 1], F32, es) for i in range(4)])
            tmpr = Ring([self.sb(f"b_tmp{i}", [128, 4, 64], F32, es) for i in range(3)])
            sc = self.sb("b_sc", [128, 64], F32, es)
            wk = self.sb("b_wk", [128, 64], F32, es)
            m8 = self.sb("b_m8", [128, 8], F32, es)
            m8b = self.sb("b_m8b", [128, 8], F32, es)
            btr = Ring([self.sb(f"b_bt{i}", [128, 128], BF, es) for i in range(4)])
            for tt in range(S // 512):
                if 'main' in skip:
                    continue
                tsl = slice(tt * 512, (tt + 1) * 512)
                qut = []
                qrt = []
                for r in range(4):
                    a = qu[r].next()
                    P.dma(P.sp, a[:, :], (QT, QT.ap[18 + r, :, tsl]))
                    qut.append(a)
                    b = qr[r].next()
                    P.dma(P.sp, b[:, :], (QT, QT.ap[22 + r, :, tsl]))
                    qrt.append(b)
                gl = glr.next()
                P.dma(P.sp, gl[:, :, :], (VT, VT.ap[tsl, VO_GATE:VO_GATE + 24].rearrange("(i p) c -> p i c", p=128)))
                gs = gsr.next()
                P.activation(gs[:, :, :], gl[:, :, :], AF.Sigmoid)
                cb = cbr.next()
                P.dma(P.sp, cb[:, :, :], (self.C["c_cmpb"], self.C["c_cmpb"].ap[:, tsl].rearrange("(ct p) n -> p ct n", p=128)))

                def epilogue(O3, nsub, sub0, h, br, accumulate):
                    r_ = rlr.next()
                    rv = r_.v(r_.h[:, 0:nsub, :])
                    P.ts(P.dve, rv, O3[2], 1e-30, None, ALU.max)
                    P.recip(rv, rv)
                    rg = rlr.next()
                    rgv = rg.v(rg.h[:, 0:nsub, :])
                    P.tt(P.dve, rgv, rv, gs.v(gs.h[:, sub0:sub0 + nsub, 3 * h + br:3 * h + br + 1]), ALU.mult)
                    dst = oB.v(oB.h[:, sub0:sub0 + nsub, 64 * h:64 * h + 64])
                    bc = rg.v(rg.h[:, 0:nsub, :].broadcast_to([128, nsub, 64]))
                    if not accumulate:
                        P.tt(P.dve, dst, O3[0], bc, ALU.mult)
                    else:
                        t_ = tmpr.next()
                        tv = t_.v(t_.h[:, 0:nsub, :])
                        P.tt(P.dve, tv, O3[0], bc, ALU.mult)
                        P.tt(P.dve, dst, dst, tv, ALU.add)
                    return r_

                for h in range(8):
                    if 'cmp' in skip:
                        continue
                    g, r = h // 4, h % 4
                    hs = slice(64 * g, 64 * g + 64)
                    pairs = [{"kT": kcT[hs, ct * 128:(ct + 1) * 128], "q": qut[r][hs, :],
                              "bias": [(self.k_ident[:, :], cb.v(cb.h[:, ct, :]))],
                              "v": vca[g].v(vca[g].h[:, ct, :])} for ct in range(2)]
                    outs = [Ob[i // 2][:, (i % 2) * 129:(i % 2) * 129 + 129] for i in range(4)]
                    self.attn(pairs, None, 129, X, nsub=4, outs=outs, starts=(0, 2))
                    for bk in range(2):
                        O = Ob[bk]
                        O3 = O.h[:, 0:258].rearrange("p (i c) -> p i c", c=129)
                        views = (O.v(O3[:, :, 0:64]), O.v(O3[:, :, 65:129]), O.v(O3[:, :, 64:65]))
                        r_ = epilogue(views, 2, 2 * bk, h, 0, False)
                        bc = r_.v(r_.h[:, 0:2, :].broadcast_to([128, 2, 64]))
                        sd = sacc[g].v(sacc[g].h[:, 2 * bk:2 * bk + 2, :])
                        if r == 0:
                            P.tt(P.dve, sd, views[1], bc, ALU.mult)
                        else:
                            t_ = tmpr.next()
                            tv = t_.v(t_.h[:, 0:2, :])
                            P.tt(P.dve, tv, views[1], bc, ALU.mult)
                            P.tt(P.pool, sd, sd, tv, ALU.add)
                if 'topk' not in skip:
                    for i in range(4):
                        tix = 4 * tt + i
                        bt = btr.next()
                        for g in range(2):
                            P.tt(P.dve, sc[:, :], sacc[g].v(sacc[g].h[:, i, :]), vnf.v(vnf.h[:, tix, :]), ALU.mult)
                            P.tt(P.dve, sc[:, :], sc[:, :], add.v(add.h[:, tix, :]), ALU.add)
                            P.op(P.dve, lambda: self.nc.vector.max(out=m8.h[:, :], in_=sc.h[:, :]), [sc[:, :]], [m8[:, :]])
                            P.op(P.dve, lambda: self.nc.vector.match_replace(out=wk.h[:, :], in_to_replace=m8.h[:, :],
                                                                              in_values=sc.h[:, :], imm_value=-1e30),
                                 [sc[:, :], m8[:, :]], [wk[:, :]])
                            P.op(P.dve, lambda: self.nc.vector.max(out=m8b.h[:, :], in_=wk.h[:, :]), [wk[:, :]], [m8b[:, :]])
                            P.ts(P.dve, bt[:, 64 * g:64 * g + 64], sc[:, :], m8b[:, 7:8], NEG, ALU.is_lt, ALU.mult)
                        P.transpose(psT[:, i * 128:(i + 1) * 128], bt[:, :], self.k_ident[:, :], sig=True)
                    P.copy(P.act, BT[:, :], psT[:, 0:512])
                for h in range(8):
                    g, r = h // 4, h % 4
                    hs = slice(64 * g, 64 * g + 64)
                    for br in (1, 2):
                        if ('sel' in skip and br == 1) or ('win' in skip and br == 2):
                            continue
                        pairs = []
                        if br == 1:
                            for sg in range(0, 4 * tt + 4):
                                bias = [(esel.v(esel.h[hs, sg, :]), BT[hs, :])]
                                if sg >= 4 * tt:
                                    bias.append((self.k_ident[:, :], sm.v(sm.h[:, 4 * tt - sg + 3, :])))
                                pairs.append({"kT": kslc[hs, sg * 128:(sg + 1) * 128], "q": qrt[r][hs, :], "bias": bias,
                                              "v": Vs.v(Vs.h[:, sg, g, :])})
                        else:
                            for sg in range(max(0, 4 * tt - 4), 4 * tt + 4):
                                bias = [(self.k_ident[:, :], wm.v(wm.h[:, 4 * tt - sg + 3, :]))]
                                pairs.append({"kT": kwin[hs, sg * 128:(sg + 1) * 128], "q": qrt[r][hs, :], "bias": bias,
                                              "v": Vw.v(Vw.h[:, sg, g, :])})
                        O = Ob[(2 * h + br) % 2]
                        self.attn(pairs, O, 65, X)
                        O3 = O.h[:, 0:260].rearrange("p (i c) -> p i c", c=65)
                        views = (O.v(O3[:, :, 0:64]), None, O.v(O3[:, :, 64:65]))
                        epilogue(views, 4, 0, h, br, True)
                o16 = ob16.next()
                P.activation(o16[:, :, :], oB[:, :, :], AF.Identity)
                P.dma(self.qst, (OTOK, OTOK.ap[tsl, 384:896].rearrange("(i p) c -> p i c", p=128)), o16[:, :, :])
            P.barrier()

def _in_map(consts, x_b, pos_b, weights, norm_final):
    m = {"x": np.ascontiguousarray(x_b), "pos": np.ascontiguousarray(pos_b).reshape(1, -1).astype(np.int32)}
    for n in WNAMES:
        m[n] = weights[n]
    m["norm_final"] = norm_final
    m.update(consts)
    return m


def kernel(x, positions, norm_ffn1, ffn1_w1, ffn1_w3, ffn1_w2, norm_mix, w_in,
           cmp_pe_k, cmp_w1_k, cmp_w2_k, cmp_pe_v, cmp_w1_v, cmp_w2_v,
           w_gate, w_up, w_out, norm_ffn2, ffn2_w1, ffn2_w3, ffn2_w2, norm_final):
    loc = locals()
    x = np.asarray(x)
    B, S, _ = x.shape
    L = int(np.asarray(norm_ffn1).shape[0])
    weights = {n: np.ascontiguousarray(np.asarray(loc[n], dtype=np.float32)) for n in WNAMES}
    bld = Builder(S, L)
    nc = bld.build()
    nf = np.ascontiguousarray(np.asarray(norm_final, dtype=np.float32))
    pos = np.asarray(positions)
    in_maps = [_in_map(bld.consts, x[b], pos[b], weights, nf) for b in range(B)]
    res = run_bass_kernel_spmd(nc, in_maps, core_ids=list(range(B)))
    return np.stack([np.asarray(res.results[b]["out"]) for b in range(B)], axis=0).astype(np.float32)
```

```python
import contextlib
import numpy as np
import ml_dtypes
import concourse.bass as bass
import concourse.mybir as mybir
from concourse.bass_utils import run_bass_kernel_spmd

F32 = mybir.dt.float32
BF = mybir.dt.bfloat16
I32 = mybir.dt.int32
AF = mybir.ActivationFunctionType
ALU = mybir.AluOpType
AX = mybir.AxisListType

D = 1024
DFF = 2816
NF = DFF // 128
HD = 64
NEG = -1920.0
EPS = 1e-6
A_IN = 3456
B_IN = 1304
IN_W = 5912
NSEL = 16


class Buf:
    __slots__ = ("name", "w", "r", "sem", "semv", "psum")

    def __init__(self, name):
        self.name = name
        self.psum = False
        self.w = None
        self.r = {}
        self.sem = None
        self.semv = 0


class V:
    __slots__ = ("ap", "bufs")

    def __init__(self, ap, bufs):
        self.ap = ap
        self.bufs = bufs


class Tn:
    def __init__(self, P, name, shape, dtype, space="sbuf", es=None):
        es = es if es is not None else P.es
        P.ntn = getattr(P, "ntn", 0) + 1
        name = f"{name}_{P.ntn}"
        if space == "sbuf":
            self.h = es.enter_context(P.nc.sbuf_tensor(name, list(shape), dtype))
        else:
            self.h = es.enter_context(P.nc.psum_tensor(name, list(shape), dtype))
        self.buf = Buf(name)
        self.buf.psum = (space != "sbuf")
        self.shape = list(shape)
        self.dtype = dtype
        self.P = P

    def __getitem__(self, idx):
        return V(self.h[idx], [self.buf])

    def v(self, ap):
        return V(ap, [self.buf])

    def raw(self, offset, ap):
        return V(bass.AP(tensor=self.h, offset=offset, ap=ap), [self.buf])


class DT:
    def __init__(self, P, name, shape, dtype, kind="Internal"):
        self.t = P.nc.dram_tensor(name, list(shape), dtype, kind=kind)
        self.ap = self.t.ap()
        self.pending = {}
        self.name = name
        self.sem = None
        self.semv = 0
        P.dts.append(self)


class Eng:
    def __init__(self, P, name, h):
        self.P = P
        self.name = name
        self.h = h
        self.sem = P.new_sem("e_" + name)
        self.cnt = 0
        self.waited = {}
        self.pend_r = []
        self.pend_w = []

    def wait(self, toks):
        for tok in toks:
            sem, val = tok[0], tok[1]
            k = sem.num
            if self.waited.get(k, 0) < val:
                self.h.wait_ge(sem, val)
                self.waited[k] = val


class Prog:
    def __init__(self, nc, es):
        self.nc = nc
        self.es = es
        self.nsem = 0
        self.dts = []
        self.scope = None
        self.pe = Eng(self, "pe", nc.tensor)
        self.act = Eng(self, "act", nc.scalar)
        self.dve = Eng(self, "dve", nc.vector)
        self.pool = Eng(self, "pool", nc.gpsimd)
        self.sp = Eng(self, "sp", nc.sync)
        self.engs = [self.pe, self.act, self.dve, self.pool, self.sp]
        self.bar_sem = self.new_sem("bar")
        self.bar_n = 0
        self.dma_toks = {}
        self.n_ins = 0

    def new_sem(self, name):
        self.nsem += 1
        h = self.nc.alloc_semaphore(name=f"{name}_{self.nsem}")
        if self.scope is not None:
            self.scope.append(h)
        return h

    def scope_begin(self):
        assert self.scope is None
        self.scope = []

    def scope_end(self):
        self.barrier()
        sems = self.scope
        self.scope = None
        if sems:
            nums = set(h.num for h in sems)
            self.nc.clear_and_free_semaphores(sems)
            self.op(self.pool, lambda: self.nc.gpsimd.memset(self.scratch.h[:, :], 0.0), [], [self.scratch[:, :]])
            for e in self.engs:
                for k in list(e.waited.keys()):
                    if k in nums:
                        del e.waited[k]
            for k in list(self.dma_toks.keys()):
                if k in nums:
                    del self.dma_toks[k]
            for d in self.dts:
                d.pending = {k: v for k, v in d.pending.items() if k not in nums}
                if d.sem is not None and d.sem.num in nums:
                    d.sem = None
                    d.semv = 0
        self.barrier()

    def _deps(self, E, reads, writes, strict=False):
        toks = []
        for b in reads:
            if b.w is not None:
                t = b.w
                if strict or t[2] != E.name or E.name != "pe":
                    toks.append(t)
            if b.psum:
                for t in b.r.values():
                    if t[2] != E.name:
                        toks.append(t)
        same_ok = (E.name == "pe")
        for b in writes:
            if b.w is not None and (strict or b.w[2] != E.name or not same_ok):
                toks.append(b.w)
            for t in b.r.values():
                if strict or t[2] != E.name or not same_ok:
                    toks.append(t)
        return toks

    def _commit(self, tok, reads, writes):
        ek = tok[2] if tok[2] is not None else tok[0].num
        for b in reads:
            b.r[ek] = tok
        for b in writes:
            b.w = tok
            b.r = {}

    def op(self, E, fn, reads, writes, sig=True):
        rb = [b for v in reads for b in v.bufs]
        wb = [b for v in writes for b in v.bufs]
        for b in rb + wb:
            assert not (b in self.pe.pend_r or b in self.pe.pend_w) or E is self.pe, \
                f"buffer {b.name} has unsignalled PE access"
        E.wait(self._deps(E, rb, wb))
        ins = fn()
        self.n_ins += 1
        if E is self.pe and not sig:
            E.pend_r += rb
            E.pend_w += wb
            return ins
        E.cnt += 1
        ins.then_inc(E.sem, 1)
        tok = (E.sem, E.cnt, E.name)
        if E is self.pe:
            rb = rb + E.pend_r
            wb = wb + E.pend_w
            E.pend_r = []
            E.pend_w = []
        self._commit(tok, rb, wb)
        return ins

    def dma(self, Q, out, in_, **kw):
        o_dram = isinstance(out, tuple)
        i_dram = isinstance(in_, tuple)
        toks = []
        if i_dram:
            toks += list(in_[0].pending.values())
            in_ap = in_[1]
        else:
            in_ap = in_.ap
            for b in in_.bufs:
                if b.w is not None:
                    toks.append(b.w)
        part = kw.pop("part", False)
        if o_dram:
            out_ap = out[1]
        else:
            out_ap = out.ap
            dd = self._deps(Q, [], out.bufs, strict=True)
            if part:
                dd = [t for t in dd if not (t[2] is None and t is out.bufs[0].w)]
            toks += dd
        Q.wait(toks)
        ins = Q.h.dma_start(out=out_ap, in_=in_ap, **kw)
        self.n_ins += 1
        if not o_dram:
            b = out.bufs[0]
        elif not i_dram:
            b = in_.bufs[0]
        else:
            b = out[0]
        if b.sem is None:
            b.sem = self.new_sem("d")
        b.semv += 16
        ins.then_inc(b.sem, 16)
        tok = (b.sem, b.semv, None)
        self.dma_toks[b.sem.num] = tok
        if not o_dram:
            out.bufs[0].w = tok
            out.bufs[0].r = {}
        else:
            out[0].pending[b.sem.num] = tok
            if not i_dram:
                in_.bufs[0].r[b.sem.num] = tok
        return ins

    def barrier(self):
        sp = self.sp
        toks = [(e.sem, e.cnt, e.name) for e in self.engs if e is not sp and e.cnt > 0]
        toks += list(self.dma_toks.values())
        sp.wait(toks)
        self.bar_n += 1
        sp.h.sem_inc(self.bar_sem, 1)
        for e in self.engs:
            if e is not sp:
                e.h.wait_ge(self.bar_sem, self.bar_n)
                for t in toks:
                    k = t[0].num
                    if e.waited.get(k, 0) < t[1]:
                        e.waited[k] = t[1]

    def mm(self, out, lhsT, rhs, start, stop, sig=None):
        if sig is None:
            sig = stop
        return self.op(self.pe,
                       lambda: self.nc.tensor.matmul(out.ap, lhsT=lhsT.ap, rhs=rhs.ap, start=start, stop=stop,
                                                     skip_group_check=True),
                       [lhsT, rhs], [out], sig=sig)

    def transpose(self, out, in_, ident, sig=True):
        return self.op(self.pe, lambda: self.nc.tensor.transpose(out.ap, in_.ap, ident.ap),
                       [in_, ident], [out], sig=sig)

    def activation(self, out, in_, func, scale=1.0, bias=None, eng=None):
        E = eng or self.act
        kw = {}
        reads = [in_]
        if bias is not None:
            if isinstance(bias, V):
                kw["bias"] = bias.ap
                reads.append(bias)
            else:
                kw["bias"] = bias
        if isinstance(scale, V):
            reads.append(scale)
            sc = scale.ap
        else:
            sc = scale
        return self.op(E, lambda: E.h.activation(out=out.ap, in_=in_.ap, func=func, scale=sc, **kw), reads, [out])

    def tt(self, E, out, in0, in1, op):
        return self.op(E, lambda: E.h.tensor_tensor(out=out.ap, in0=in0.ap, in1=in1.ap, op=op), [in0, in1], [out])

    def ts(self, E, out, in0, s1, s2, op0, op1=None):
        reads = [in0]
        a1 = s1
        a2 = s2
        if isinstance(s1, V):
            reads.append(s1)
            a1 = s1.ap
        if isinstance(s2, V):
            reads.append(s2)
            a2 = s2.ap
        if op1 is None:
            return self.op(E, lambda: E.h.tensor_scalar(out=out.ap, in0=in0.ap, scalar1=a1, scalar2=None, op0=op0),
                           reads, [out])
        return self.op(E, lambda: E.h.tensor_scalar(out=out.ap, in0=in0.ap, scalar1=a1, scalar2=a2, op0=op0, op1=op1),
                       reads, [out])

    def stt(self, out, in0, scalar, in1, op0, op1):
        reads = [in0, in1]
        a = scalar
        if isinstance(scalar, V):
            reads.append(scalar)
            a = scalar.ap
        E = self.dve
        return self.op(E, lambda: E.h.scalar_tensor_tensor(out=out.ap, in0=in0.ap, scalar=a, in1=in1.ap, op0=op0, op1=op1),
                       reads, [out])

    def copy(self, E, out, in_):
        if E is self.act:
            return self.op(E, lambda: E.h.copy(out=out.ap, in_=in_.ap), [in_], [out])
        return self.op(E, lambda: E.h.tensor_copy(out=out.ap, in_=in_.ap), [in_], [out])

    def memset(self, E, out, val):
        return self.op(E, lambda: E.h.memset(out.ap, val), [], [out])

    def recip(self, out, in_):
        E = self.dve
        return self.op(E, lambda: E.h.reciprocal(out=out.ap, in_=in_.ap), [in_], [out])


def _fm_blocks():
    blks = []
    for g in range(3):
        for sp in range(3):
            blks.append(("qA", [(0 * 1152 + g * 384 + sp * 128, 128)], g * 3 + sp, None))
    for g in range(3):
        for sp in range(3):
            blks.append(("kA", [(1152 + g * 384 + sp * 128, 128)], 9 + g * 3 + sp, None))
    bq = A_IN
    for r in range(4):
        blks.append(("qB", [(bq + r * 64, 64), (bq + (4 + r) * 64, 64)], 22 + r, 18 + r))
    bkv = A_IN + 512
    blks.append(("kcmp", [(bkv, 128)], None, 26))
    blks.append(("vcmp", [(bkv + 128, 128)], None, 27))
    blks.append(("kslc", [(bkv + 256, 128)], 28, None))
    blks.append(("kwin", [(bkv + 512, 128)], 29, None))
    cb = A_IN + B_IN
    for hp in range(3):
        blks.append(("qC", [(cb + hp * 128, 128)], None, 30 + hp))
    for hp in range(3):
        blks.append(("kC", [(cb + 384 + hp * 128, 128)], None, 33 + hp))
    return blks


FM = _fm_blocks()
NQT = 36
VRUNS = [(2304, 1152), (A_IN + 512 + 256 + 128, 128), (A_IN + 512 + 512 + 128, 128), (A_IN + 1280, 24),
         (A_IN + B_IN + 768, 384)]
NV = 1816
VO_A, VO_SLC, VO_WIN, VO_GATE, VO_C = 0, 1152, 1280, 1408, 1432
NVG = 4
VGW = NV // NVG
A_DIL = (1, 4, 16)


def _consts(S):
    bf = ml_dtypes.bfloat16
    NT = S // 128
    c = {}
    c["c_ident"] = np.eye(128, dtype=np.float32).astype(bf)
    c["c_identf"] = np.eye(128, dtype=np.float32)
    perm = np.zeros((128, 128), np.float32)
    for m in range(128):
        d = m % 64
        if d < 8:
            perm[m + 8, m] = 1.0
        elif d < 16:
            perm[m - 8, m] = 1.0
    c["c_perm"] = perm.astype(bf)
    jj = np.arange(128)
    c["c_U"] = (jj[:, None] >= jj[None, :]).astype(np.float32).astype(bf)
    c["c_ones"] = np.ones((128, 128), np.float32).astype(bf)
    si = np.arange(128)[:, None]
    ti = np.arange(512)[None, :]
    tiles = []
    for dil in A_DIL:
        for dl in range(-3, dil + 1):
            d = 128 * dl + ti - si
            ok = (d >= 0) & (d <= 128 * dil) & (d % dil == 0)
            tiles.append(np.where(ok, 0.0, NEG))
    c["c_amask"] = np.stack(tiles).astype(np.float32).astype(bf)
    tiles = []
    for dl in range(-3, 5):
        d = 128 * dl + ti - si
        tiles.append(np.where((d >= 0) & (d <= 511), 0.0, NEG))
    c["c_wmask"] = np.stack(tiles).astype(np.float32).astype(bf)
    tiles = []
    tiles2 = []
    for dl in range(-3, 1):
        d = 128 * dl + ti - si
        tiles.append(np.where(d >= 0, 0.0, NEG))
        tiles2.append(np.where(d >= 1, 1.0, 0.0))
    c["c_smask"] = np.stack(tiles).astype(np.float32).astype(bf)
    c["c_cmask"] = np.stack(tiles2).astype(np.float32).astype(bf)
    cc = np.arange(256)[:, None]
    tt = np.arange(S)[None, :]
    n_cmp = S // 16 - 1
    c["c_cmpb"] = np.where((16 * cc + 31 <= tt) & (cc < n_cmp), 0.0, NEG).astype(np.float32).astype(bf)
    es = np.zeros((64, NT, 128), np.float32)
    for sg in range(NT):
        for s_ in range(128):
            j = 2 * sg + s_ // 64
            if j < 64:
                es[j, sg, s_] = 1.0
    c["c_esel"] = es.astype(bf)
    w = np.zeros((256, 64), np.float32)
    for j in range(64):
        for m in range(4):
            for n in range(2):
                ci = 4 * j + m + n
                if ci < 256:
                    w[ci, j] += 1.0
    c["c_wsel"] = w.astype(bf)
    t = np.arange(S)[:, None]
    j = np.arange(64)[None, :]
    cur = t // 64
    nsel = S // 64
    valid = (j * 64 <= t) & (j < nsel)
    forced = ((j == 0) | (j == cur) | (j == cur - 1)) & (j < nsel)
    c["c_vnf"] = (valid & ~forced).astype(np.float32)
    c["c_add"] = np.where(j >= nsel, -3.0, np.where(forced, 1e4, np.where(valid, 0.0, -1.0))).astype(np.float32)
    rp = np.zeros((128, 2), np.float32)
    for p in range(128):
        d = p % 64
        if d < 16:
            rp[p, 0] = np.float32(500000.0) ** np.float32(-(2 * (d % 8)) / 16.0)
            rp[p, 1] = -1.0 if d < 8 else 1.0
    c["c_rope"] = rp
    return c


WNAMES = ["norm_ffn1", "ffn1_w1", "ffn1_w3", "ffn1_w2", "norm_mix", "w_in", "cmp_pe_k", "cmp_w1_k", "cmp_w2_k",
          "cmp_pe_v", "cmp_w1_v", "cmp_w2_v", "w_gate", "w_up", "w_out", "norm_ffn2", "ffn2_w1", "ffn2_w3",
          "ffn2_w2"]
WSHAPES = {"norm_ffn1": [D], "ffn1_w1": [D, DFF], "ffn1_w3": [D, DFF], "ffn1_w2": [DFF, D], "norm_mix": [D],
           "w_in": [D, IN_W], "cmp_pe_k": [32, 64], "cmp_w1_k": [2048, 128], "cmp_w2_k": [128, 64],
           "cmp_pe_v": [32, 64], "cmp_w1_v": [2048, 128], "cmp_w2_v": [128, 64], "w_gate": [D, 3 * D],
           "w_up": [1280, D], "w_out": [D, D], "norm_ffn2": [D], "ffn2_w1": [D, DFF], "ffn2_w3": [D, DFF],
           "ffn2_w2": [DFF, D]}


class Ring:
    def __init__(self, tiles):
        self.t = tiles
        self.i = 0

    def next(self):
        t = self.t[self.i % len(self.t)]
        self.i += 1
        return t


class Builder:
    def __init__(self, S, L, TT=512, dbg=(), phases=None):
        self.S, self.L, self.TT = S, L, TT
        self.NT = S // 128
        self.NG = TT // 512
        self.dbg = set(dbg)
        self.phases = phases
        self.consts = _consts(S)
        nc = bass.Bass("TRN2", target_bir_lowering=False)
        self.nc = nc
        self.es = contextlib.ExitStack()
        self.P = Prog(nc, self.es)
        self.din = {}
        import os
        self.qst = {'sp': self.P.sp, 'pool': self.P.pool, 'act': self.P.act}[os.environ.get('K_QST', 'pool')]
        self._ptoks = {}
        self._pn = 0

    def inp(self, name, shape, dtype):
        d = DT(self.P, name, shape, dtype, kind="ExternalInput")
        self.din[name] = d
        return d

    def scr(self, name, shape, dtype):
        kind = "ExternalOutput" if name in self.dbg else "Internal"
        return DT(self.P, name, shape, dtype, kind=kind)

    def sb(self, name, shape, dtype, es=None):
        return Tn(self.P, name, shape, dtype, "sbuf", es)

    def psum_banks(self, es, n=8):
        return [Tn(self.P, f"psb{i}", [128, 512], F32, "psum", es) for i in range(n)]

    def build(self):
        P, S, L = self.P, self.S, self.L
        self.x = self.inp("x", [S, D], F32)
        self.pos = self.inp("pos", [1, S], I32)
        self.W = {}
        for n in WNAMES:
            self.W[n] = self.inp(n, [L] + WSHAPES[n], F32)
        self.nf = self.inp("norm_final", [D], F32)
        self.C = {}
        for k, v in self.consts.items():
            self.C[k] = self.inp(k, list(v.shape), BF if v.dtype == ml_dtypes.bfloat16 else F32)
        self.out = DT(P, "out", [S, D], F32, kind="ExternalOutput")
        self.HT = [self.scr(f"HT{l}", [8, 128, S], F32) for l in range(L)]
        self.HT2 = [self.scr(f"HTb{l}", [8, 128, S], F32) for l in range(L)]
        self.QT = [self.scr(f"QT{l}", [NQT, 128, S], BF) for l in range(L)]
        self.VT = [self.scr(f"VT{l}", [S, NV], BF) for l in range(L)]
        self.GT = [self.scr(f"GT{l}", [24, 128, S], BF) for l in range(L)]
        self.OTOK = [self.scr(f"OTOK{l}", [S, 896], BF) for l in range(L)]
        self.OTC = [self.scr(f"OTC{l}", [3, 128, S], BF) for l in range(L)]
        self.CS = self.scr("CS", [2, 128, S], F32)
        self.WB = []
        for l in range(L):
            d = {}
            for f in (1, 2):
                d[f"w1_{f}"] = self.scr(f"b_w1_{f}_{l}", [NF, 128, 8 * 128], BF)
                d[f"w3_{f}"] = self.scr(f"b_w3_{f}_{l}", [NF, 128, 8 * 128], BF)
                d[f"w2_{f}"] = self.scr(f"b_w2_{f}_{l}", [8, 128, NF * 128], BF)
            d["win"] = self.scr(f"b_win_{l}", [len(FM), 128, 8 * 128], BF)
            d["wv"] = self.scr(f"b_wv_{l}", [NVG, 128, 8 * VGW], BF)
            d["wg"] = self.scr(f"b_wg_{l}", [24, 128, 8 * 128], BF)
            d["wup"] = self.scr(f"b_wup_{l}", [8, 128, 10 * 128], BF)
            d["wout"] = self.scr(f"b_wout_{l}", [8, 128, 8 * 128], BF)
            self.WB.append(d)

        P.scratch = self.sb("p_scratch", [128, 8], F32)
        self.k_ident = self.sb("k_ident", [128, 128], BF)
        self.k_identf = self.sb("k_identf", [128, 128], F32)
        self.k_perm = self.sb("k_perm", [128, 128], BF)
        self.k_U = self.sb("k_U", [128, 128], BF)
        self.k_ones = self.sb("k_ones", [128, 128], BF)
        self.k_rope = self.sb("k_rope", [128, 2], F32)
        self.k_one = self.sb("k_one", [128, 1], F32)
        P.memset(P.dve, self.k_one[:, :], 1.0)
        for t, n in ((self.k_ident, "c_ident"), (self.k_identf, "c_identf"), (self.k_perm, "c_perm"),
                     (self.k_U, "c_U"), (self.k_ones, "c_ones"), (self.k_rope, "c_rope")):
            P.dma(P.sp, t[:, :], (self.C[n], self.C[n].ap))
        self.g = {}
        for n in ("norm_ffn1", "norm_mix", "norm_ffn2"):
            for l in range(L):
                t = self.sb(f"g_{n}_{l}", [128, 8], F32)
                P.dma(P.sp, t[:, :], (self.W[n], self.W[n].ap[l].rearrange("(c p) -> p c", p=128)),
                      allow_slow_non_contiguous=True)
                self.g[(n, l)] = t
        t = self.sb("g_final", [128, 8], F32)
        P.dma(P.sp, t[:, :], (self.nf, self.nf.ap.rearrange("(c p) -> p c", p=128)), allow_slow_non_contiguous=True)
        self.g[("final", 0)] = t

        ph = self.phases
        P.barrier()
        if ph is None or "p0" in ph:
            P.scope_begin()
            self.phase0()
            P.scope_end()
        for l in range(L):
            if ph is None or "p1" in ph:
                P.scope_begin()
                self.phase1(l)
                P.scope_end()
            if ph is None or "p2" in ph:
                self.phase2(l)
            if ph is None or "p3" in ph:
                P.scope_begin()
                self.phase3(l)
                P.scope_end()
        self.es.close()
        return self.nc

    def _prep(self, dst, j, src2d, runs, nchunk, width):
        P = self.P
        off = 0
        dv = dst.ap[j].rearrange("p (c n) -> p c n", n=width)
        for (col, w) in runs:
            src = src2d[:, col:col + w].rearrange("(c p) n -> p c n", p=128)
            P.dma(P.pool, (dst, dv[:, :, off:off + w]), (self.x, src))
            off += w
            self._ptoks[dst.sem.num] = (dst.sem, dst.semv, None)
            self._pn += 1
            if self._pn % 12 == 0:
                P.pool.wait(list(self._ptoks.values()))

    def phase0(self):
        import os
        P, S, L = self.P, self.S, self.L
        if os.environ.get('K_SKIP_ROPE') is None:
            self.rope_tables()
        if os.environ.get('K_SKIP_PREP'):
            return
        for l in range(L):
            wb = self.WB[l]
            for f in (1, 2):
                w1 = self.W[f"ffn{f}_w1"].ap[l]
                w3 = self.W[f"ffn{f}_w3"].ap[l]
                w2 = self.W[f"ffn{f}_w2"].ap[l]
                for j in range(NF):
                    self._prep(wb[f"w1_{f}"], j, w1, [(j * 128, 128)], 8, 128)
                    self._prep(wb[f"w3_{f}"], j, w3, [(j * 128, 128)], 8, 128)
                for j in range(8):
                    self._prep(wb[f"w2_{f}"], j, w2, [(j * 128, 128)], NF, 128)
                if f == 1:
                    win = self.W["w_in"].ap[l]
                    for j, blk in enumerate(FM):
                        self._prep(wb["win"], j, win, blk[1], 8, 128)
                    cols = []
                    for (c0, w) in VRUNS:
                        cols += [(c0, w)]
                    flat = []
                    for (c0, w) in cols:
                        flat.append([c0, w])
                    for gi in range(NVG):
                        need = VGW
                        runs = []
                        while need > 0:
                            c0, w = flat[0]
                            take = min(w, need)
                            runs.append((c0, take))
                            need -= take
                            if take == w:
                                flat.pop(0)
                            else:
                                flat[0] = [c0 + take, w - take]
                        self._prep(wb["wv"], gi, win, runs, 8, VGW)
                    wg = self.W["w_gate"].ap[l]
                    for j in range(24):
                        self._prep(wb["wg"], j, wg, [(j * 128, 128)], 8, 128)
                    wu = self.W["w_up"].ap[l]
                    for j in range(8):
                        self._prep(wb["wup"], j, wu, [(j * 128, 128)], 10, 128)
                    wo = self.W["w_out"].ap[l]
                    for j in range(8):
                        self._prep(wb["wout"], j, wo, [(j * 128, 128)], 8, 128)

    def rope_tables(self):
        P, S = self.P, self.S
        PI = float(np.pi)
        with contextlib.ExitStack() as es:
            pi_ = self.sb("rp_pi", [128, S], I32, es)
            a = self.sb("rp_a", [128, S], F32, es)
            b = self.sb("rp_b", [128, S], F32, es)
            r = self.sb("rp_r", [128, S], F32, es)
            m = self.sb("rp_m", [128, S], F32, es)
            dve = P.dve
            P.dma(P.sp, pi_[:, :], (self.pos, self.pos.ap[0].partition_broadcast(128)))
            P.copy(dve, a[:, :], pi_[:, :])
            P.ts(dve, a[:, :], a[:, :], self.k_rope[:, 0:1], None, ALU.mult)
            P.ts(dve, b[:, :], a[:, :], 1.0 / (2 * PI), None, ALU.mult)
            ki = pi_
            P.copy(dve, ki[:, :], b[:, :])
            P.copy(dve, b[:, :], ki[:, :])
            C1 = 6.28125
            C2 = float(2 * np.pi - 6.28125)
            P.stt(r[:, :], b[:, :], -C1, a[:, :], ALU.mult, ALU.add)
            P.stt(r[:, :], b[:, :], -C2, r[:, :], ALU.mult, ALU.add)
            for (thr, op, adj) in ((PI, ALU.is_gt, -2 * PI), (-PI, ALU.is_lt, 2 * PI), (PI, ALU.is_gt, -2 * PI),
                                   (-PI, ALU.is_lt, 2 * PI)):
                P.ts(dve, m[:, :], r[:, :], thr, adj, op, ALU.mult)
                P.tt(dve, r[:, :], r[:, :], m[:, :], ALU.add)
            LIM = 3.1415925
            P.ts(dve, r[:, :], r[:, :], LIM, -LIM, ALU.min, ALU.max)
            P.activation(m[:, :], r[:, :], AF.Sin)
            P.ts(dve, m[:, :], m[:, :], self.k_rope[:, 1:2], None, ALU.mult)
            P.dma(P.pool, (self.CS, self.CS.ap[1]), m[:, :])
            P.ts(dve, b[:, :], r[:, :], -1.0, None, ALU.mult)
            P.tt(dve, b[:, :], b[:, :], r[:, :], ALU.max)
            P.ts(dve, b[:, :], b[:, :], -1.0, PI / 2, ALU.mult, ALU.add)
            P.activation(a[:, :], b[:, :], AF.Sin)
            P.dma(P.pool, (self.CS, self.CS.ap[0]), a[:, :])
            P.barrier()

    def rl_alloc(self, es):
        TT = self.TT
        R = {}
        R["hT"] = [self.sb(f"hT{c}", [128, TT], F32, es) for c in range(8)]
        R["xn"] = [self.sb(f"xn{c}", [128, TT], BF, es) for c in range(8)]
        R["h1"] = [self.sb(f"h1_{j}", [128, TT], BF, es) for j in range(NF)]
        R["sq"] = Ring([self.sb(f"sq{i}", [128, 512], BF, es) for i in range(2)])
        R["rt"] = [self.sb(f"rt{i}", [128, 512], F32, es) for i in range(self.NG)]
        R["wr"] = Ring([self.sb(f"wr{i}", [128, 8 * 128], BF, es) for i in range(6)])
        R["w2r"] = Ring([self.sb(f"w2r{i}", [128, NF * 128], BF, es) for i in range(2)])
        R["sa"] = Ring([self.sb(f"sa{i}", [128, 512], F32, es) for i in range(2)])
        R["ps"] = Ring(self.psum_banks(es, 7))
        return R

    def rmsnorm(self, R, g, out, out_f32=False):
        P = self.P
        for n in range(self.NG):
            ns = slice(n * 512, (n + 1) * 512)
            ps = R["ps"].next()
            for c in range(8):
                sq = R["sq"].next()
                P.activation(sq[:, :], R["hT"][c][:, ns], AF.Square)
                P.mm(ps[:, :], self.k_ones[:, :], sq[:, :], start=(c == 0), stop=(c == 7), sig=True)
            rt = R["rt"][n]
            P.activation(rt[:, :], ps[:, :], AF.Sqrt, scale=1.0 / D, bias=self.k_eps[:, 0:1])
            P.recip(rt[:, :], rt[:, :])
            for c in range(8):
                P.stt(out[c][:, ns], R["hT"][c][:, ns], g[:, c:c + 1], rt[:, :], ALU.mult, ALU.mult)

    def ffn(self, R, l, f):
        P = self.P
        wb = self.WB[l]
        W1, W3, W2 = wb[f"w1_{f}"], wb[f"w3_{f}"], wb[f"w2_{f}"]
        xn, h1, hT = R["xn"], R["h1"], R["hT"]
        for j in range(NF):
            wa = R["wr"].next()
            P.dma(P.sp, wa[:, :], (W1, W1.ap[j]))
            wc = R["wr"].next()
            P.dma(P.sp, wc[:, :], (W3, W3.ap[j]))
            for n in range(self.NG):
                ns = slice(n * 512, (n + 1) * 512)
                pa = R["ps"].next()
                pb = R["ps"].next()
                for c in range(8):
                    P.mm(pa[:, :], wa[:, c * 128:(c + 1) * 128], xn[c][:, ns], start=(c == 0), stop=(c == 7))
                for c in range(8):
                    P.mm(pb[:, :], wc[:, c * 128:(c + 1) * 128], xn[c][:, ns], start=(c == 0), stop=(c == 7))
                sa = R["sa"].next()
                P.activation(sa[:, :], pa[:, :], AF.Silu)
                P.tt(P.dve, h1[j][:, ns], sa[:, :], pb[:, :], ALU.mult)
        for dm in range(8):
            w2 = R["w2r"].next()
            P.dma(P.sp, w2[:, :], (W2, W2.ap[dm]))
            for n in range(self.NG):
                ns = slice(n * 512, (n + 1) * 512)
                py = R["ps"].next()
                for j in range(NF):
                    P.mm(py[:, :], w2[:, j * 128:(j + 1) * 128], h1[j][:, ns], start=(j == 0), stop=(j == NF - 1))
                P.stt(hT[dm][:, ns], py[:, :], 0.5, hT[dm][:, ns], ALU.mult, ALU.add)

    def phase1(self, l):
        P, S, TT = self.P, self.S, self.TT
        wb = self.WB[l]
        with contextlib.ExitStack() as es:
            R = self.rl_alloc(es)
            hT, xn = R["hT"], R["xn"]
            self.k_eps = self.sb("k_eps", [128, 1], F32, es)
            P.memset(P.dve, self.k_eps[:, :], EPS)
            ctab = self.sb("ctab", [128, TT], F32, es)
            stab = self.sb("stab", [128, TT], F32, es)
            qsb = Ring([self.sb(f"qsb{i}", [128, 512], BF, es) for i in range(3)])
            t1 = Ring([self.sb(f"t1_{i}", [128, 512], F32, es) for i in range(2)])
            t2 = Ring([self.sb(f"t2_{i}", [128, 512], F32, es) for i in range(2)])
            stg = Ring([self.sb(f"stg{i}", [128, 512], BF, es) for i in range(3)])
            wvr = Ring([self.sb(f"wvr{i}", [128, 8 * VGW], BF, es) for i in range(2)])
            vst = [self.sb(f"vst{i}", [128, NV], BF, es) for i in range(TT // 128)]
            xin = Ring([self.sb(f"xin{i}", [128, D], F32, es) for i in range(2)]) if l == 0 else None
            for tt in range(S // TT):
                t0 = tt * TT
                tsl = slice(t0, t0 + TT)
                if l == 0:
                    for ts in range(TT // 128):
                        xi = xin.next()
                        P.dma(P.sp, xi[:, :], (self.x, self.x.ap[t0 + ts * 128:t0 + (ts + 1) * 128, :]))
                        for half in range(2):
                            ps = R["ps"].next()
                            for q in range(4):
                                c = half * 4 + q
                                P.transpose(ps[:, q * 128:(q + 1) * 128], xi[:, c * 128:(c + 1) * 128],
                                            self.k_identf[:, :], sig=(q == 3))
                            for q in range(4):
                                c = half * 4 + q
                                P.copy(P.dve if q % 2 else P.act, hT[c][:, ts * 128:(ts + 1) * 128],
                                       ps[:, q * 128:(q + 1) * 128])
                else:
                    src = self.HT2[l - 1]
                    for c in range(8):
                        P.dma(P.sp, hT[c][:, :], (src, src.ap[c, :, tsl]))
                P.dma(P.sp, ctab[:, :], (self.CS, self.CS.ap[0, :, tsl]))
                P.dma(P.sp, stab[:, :], (self.CS, self.CS.ap[1, :, tsl]))
                import os
                stop = os.environ.get('K_P1_STOP', '')
                if stop == 'load':
                    continue
                self.rmsnorm(R, self.g[("norm_ffn1", l)], xn)
                if stop == 'norm':
                    continue
                self.ffn(R, l, 1)
                if stop == 'ffn':
                    continue
                self.rmsnorm(R, self.g[("norm_mix", l)], xn)
                QT = self.QT[l]
                for bi, blk in enumerate(FM):
                    w = R["wr"].next()
                    P.dma(P.sp, w[:, :], (wb["win"], wb["win"].ap[bi]))
                    for n in range(self.NG):
                        ns = slice(n * 512, (n + 1) * 512)
                        dsl = slice(t0 + n * 512, t0 + (n + 1) * 512)
                        pm = R["ps"].next()
                        for c in range(8):
                            P.mm(pm[:, :], w[:, c * 128:(c + 1) * 128], xn[c][:, ns], start=(c == 0), stop=(c == 7))
                        qs = qsb.next()
                        P.activation(qs[:, :], pm[:, :], AF.Identity)
                        if blk[3] is not None:
                            P.dma(self.qst, (QT, QT.ap[blk[3], :, dsl]), qs[:, :])
                        if blk[2] is not None:
                            psw = R["ps"].next()
                            P.mm(psw[:, :], self.k_perm[:, :], qs[:, :], start=True, stop=True)
                            a = t1.next()
                            b = t2.next()
                            P.tt(P.dve, a[:, :], pm[:, :], ctab[:, ns], ALU.mult)
                            P.tt(P.dve, b[:, :], psw[:, :], stab[:, ns], ALU.mult)
                            st = stg.next()
                            P.tt(P.pool, st[:, :], a[:, :], b[:, :], ALU.add)
                            P.dma(self.qst, (QT, QT.ap[blk[2], :, dsl]), st[:, :])
                if stop == 'fm':
                    continue
                VT = self.VT[l]
                for gi in range(NVG):
                    wv = wvr.next()
                    P.dma(P.sp, wv[:, :], (wb["wv"], wb["wv"].ap[gi]))
                    for ts in range(TT // 128):
                        pv = R["ps"].next()
                        for c in range(8):
                            P.mm(pv[:, 0:VGW], xn[c][:, ts * 128:(ts + 1) * 128], wv[:, c * VGW:(c + 1) * VGW],
                                 start=(c == 0), stop=(c == 7))
                        P.activation(vst[ts][:, gi * VGW:(gi + 1) * VGW], pv[:, 0:VGW], AF.Identity)
                for ts in range(TT // 128):
                    P.dma(self.qst, (VT, VT.ap[t0 + ts * 128:t0 + (ts + 1) * 128, :]), vst[ts][:, :])
                if stop == 'v':
                    continue
                GT = self.GT[l]
                for bi in range(24):
                    w = R["wr"].next()
                    P.dma(P.sp, w[:, :], (wb["wg"], wb["wg"].ap[bi]))
                    for n in range(self.NG):
                        ns = slice(n * 512, (n + 1) * 512)
                        dsl = slice(t0 + n * 512, t0 + (n + 1) * 512)
                        pg = R["ps"].next()
                        for c in range(8):
                            P.mm(pg[:, :], w[:, c * 128:(c + 1) * 128], xn[c][:, ns], start=(c == 0), stop=(c == 7))
                        st = stg.next()
                        P.activation(st[:, :], pg[:, :], AF.Sigmoid)
                        P.dma(self.qst, (GT, GT.ap[bi, :, dsl]), st[:, :])
                if stop == 'gate':
                    continue
                HT = self.HT[l]
                for c in range(8):
                    P.dma(self.qst, (HT, HT.ap[c, :, tsl]), hT[c][:, :])
            P.barrier()

    def phase3(self, l):
        P, S, TT = self.P, self.S, self.TT
        wb = self.WB[l]
        last = (l == self.L - 1)
        with contextlib.ExitStack() as es:
            R = self.rl_alloc(es)
            hT, xn = R["hT"], R["xn"]
            self.k_eps = self.sb("k_eps3", [128, 1], F32, es)
            P.memset(P.dve, self.k_eps[:, :], EPS)
            oT = [self.sb(f"oT{k}", [128, TT], BF, es) for k in range(10)]
            otok = [self.sb(f"otok{i}", [128, 896], BF, es) for i in range(TT // 128)]
            psT = Tn(P, "psT", [128, 1024], BF, "psum", es)
            gr = Ring([self.sb(f"gr{i}", [128, TT], BF, es) for i in range(6)])
            y = [self.sb(f"y{k}", [128, TT], BF, es) for k in range(8)]
            wur = Ring([self.sb(f"wur{i}", [128, 10 * 128], BF, es) for i in range(2)])
            t1 = Ring([self.sb(f"t31_{i}", [128, 512], F32, es) for i in range(2)])
            t2 = Ring([self.sb(f"t32_{i}", [128, 512], F32, es) for i in range(2)])
            if last:
                xof = [self.sb(f"xof{c}", [128, TT], F32, es) for c in range(8)]
                ost = Ring([self.sb(f"ost{i}", [128, D], F32, es) for i in range(2)])
            GT, OTOK, OTC, HT = self.GT[l], self.OTOK[l], self.OTC[l], self.HT[l]
            for tt in range(S // TT):
                t0 = tt * TT
                tsl = slice(t0, t0 + TT)
                for c in range(8):
                    P.dma(P.sp, hT[c][:, :], (HT, HT.ap[c, :, tsl]))
                for hp in range(3):
                    P.dma(P.sp, oT[7 + hp][:, :], (OTC, OTC.ap[hp, :, tsl]))
                for ts in range(TT // 128):
                    P.dma(P.sp, otok[ts][:, :], (OTOK, OTOK.ap[t0 + ts * 128:t0 + (ts + 1) * 128, :]))
                for kb in range(7):
                    for n in range(self.NG):
                        for q in range(4):
                            ts = n * 4 + q
                            P.transpose(psT[:, q * 128:(q + 1) * 128], otok[ts][:, kb * 128:(kb + 1) * 128],
                                        self.k_ident[:, :], sig=(q == 3))
                        P.copy(P.dve if kb % 2 else P.act, oT[kb][:, n * 512:(n + 1) * 512], psT[:, 0:512])
                for dm in range(8):
                    w = wur.next()
                    P.dma(P.sp, w[:, :], (wb["wup"], wb["wup"].ap[dm]))
                    gs = []
                    for m in range(3):
                        gt = gr.next()
                        P.dma(P.sp, gt[:, :], (GT, GT.ap[m * 8 + dm, :, tsl]))
                        gs.append(gt)
                    for n in range(self.NG):
                        ns = slice(n * 512, (n + 1) * 512)
                        pp = []
                        for (k0, k1) in ((0, 3), (3, 7), (7, 10)):
                            p_ = R["ps"].next()
                            for k in range(k0, k1):
                                P.mm(p_[:, :], w[:, k * 128:(k + 1) * 128], oT[k][:, ns], start=(k == k0),
                                     stop=(k == k1 - 1))
                            pp.append(p_)
                        a = t1.next()
                        b = t2.next()
                        P.tt(P.dve, a[:, :], pp[0][:, :], gs[0][:, ns], ALU.mult)
                        P.tt(P.dve, b[:, :], pp[1][:, :], gs[1][:, ns], ALU.mult)
                        P.tt(P.pool, a[:, :], a[:, :], b[:, :], ALU.add)
                        b2 = t2.next()
                        P.tt(P.dve, b2[:, :], pp[2][:, :], gs[2][:, ns], ALU.mult)
                        P.tt(P.pool, y[dm][:, ns], a[:, :], b2[:, :], ALU.add)
                for dm2 in range(8):
                    w = R["wr"].next()
                    P.dma(P.sp, w[:, :], (wb["wout"], wb["wout"].ap[dm2]))
                    for n in range(self.NG):
                        ns = slice(n * 512, (n + 1) * 512)
                        p_ = R["ps"].next()
                        for k in range(8):
                            P.mm(p_[:, :], w[:, k * 128:(k + 1) * 128], y[k][:, ns], start=(k == 0), stop=(k == 7))
                        P.tt(P.dve, hT[dm2][:, ns], p_[:, :], hT[dm2][:, ns], ALU.add)
                self.rmsnorm(R, self.g[("norm_ffn2", l)], xn)
                self.ffn(R, l, 2)
                if not last:
                    H2 = self.HT2[l]
                    for c in range(8):
                        P.dma(self.qst, (H2, H2.ap[c, :, tsl]), hT[c][:, :])
                else:
                    self.rmsnorm(R, self.g[("final", 0)], xof)
                    for ts in range(TT // 128):
                        o_ = ost.next()
                        for half in range(2):
                            ps = R["ps"].next()
                            for q in range(4):
                                c = half * 4 + q
                                P.transpose(ps[:, q * 128:(q + 1) * 128], xof[c][:, ts * 128:(ts + 1) * 128],
                                            self.k_identf[:, :], sig=(q == 3))
                            P.copy(P.dve if half else P.act, o_[:, half * 512:(half + 1) * 512], ps[:, :])
                        P.dma(self.qst, (self.out, self.out.ap[t0 + ts * 128:t0 + (ts + 1) * 128, :]), o_[:, :])
            P.barrier()

    def attn(self, pairs, O, vw, X, nsub=4, outs=None, starts=(0,)):
        P = self.P
        n = len(pairs)
        sps = [None] * n
        pbs = [None] * n

        def stage_s(i):
            p = pairs[i]
            s_ = X["S"].next()
            sps[i] = s_
            nb = len(p["bias"])
            P.mm(s_[:, :], p["kT"], p["q"], start=True, stop=(nb == 0))
            for bi, (lh, rh) in enumerate(p["bias"]):
                P.mm(s_[:, :], lh, rh, start=False, stop=(bi == nb - 1))

        def stage_pv(i):
            t = X["P"].next()
            pbs[i] = t
            P.activation(t[:, :], sps[i][:, :], AF.Exp, scale=0.125)
            for sub in range(nsub):
                o_ = outs[sub] if outs is not None else O[:, sub * vw:(sub + 1) * vw]
                P.mm(o_, t[:, sub * 128:(sub + 1) * 128], pairs[i]["v"],
                     start=(i == 0 and sub in starts), stop=(i == n - 1), sig=(sub == nsub - 1))

        LA = 2
        for i in range(min(LA, n)):
            stage_s(i)
        for i in range(n):
            if i + LA < n:
                stage_s(i + LA)
            stage_pv(i)

    def load_v_tm(self, dst, VT, c0, ncol, nh, wcol):
        P = self.P
        NT = self.NT
        src = VT.ap[:, c0:c0 + ncol].rearrange("(s p) (h d) -> p s h d", p=128, h=nh)
        first = True
        for s0 in range(0, NT, 16):
            s1 = min(NT, s0 + 16)
            for h in range(nh):
                P.dma(P.sp, dst.v(dst.h[:, s0:s1, h, 0:64]), (VT, src[:, s0:s1, h, :]), part=(not first))
                first = False

    def phase2(self, l):
        import os
        sel = os.environ.get("K_P2", "abc")
        for k, fn in (("a", self.mixer_a), ("b", self.mixer_b), ("c", self.mixer_c)):
            if k in sel:
                self.P.scope_begin()
                fn(l)
                self.P.scope_end()

    def p2_pre(self, l):
        pass

    def mixer_a(self, l):
        import os
        skip = os.environ.get('K_A_SKIP', '')
        P, S, NT = self.P, self.S, self.NT
        QT, VT, OTOK = self.QT[l], self.VT[l], self.OTOK[l]
        with contextlib.ExitStack() as es:
            am = self.sb("amask", [128, 33, 512], BF, es)
            for t0 in range(0, 33, 11):
                P.dma(P.sp, am.v(am.h[:, t0:t0 + 11, :]),
                      (self.C["c_amask"], self.C["c_amask"].ap[t0:t0 + 11].rearrange("t p n -> p t n")), part=(t0 > 0))
            qTb = [self.sb(f"a_q{g}", [128, S], BF, es) for g in range(3)]
            kTb = [self.sb(f"a_k{g}", [128, S], BF, es) for g in range(3)]
            Vb = [self.sb(f"a_v{g}", [128, NT, 2, 65], BF, es) for g in range(3)]
            for g in range(3):
                if 'memset' in skip:
                    continue
                P.memset(P.pool, Vb[g].v(Vb[g].h[:, :, :, 64:65]), 1.0)
            X = {"S": Ring(self.psum_banks(es, 4)), "P": Ring([self.sb(f"a_p{i}", [128, 512], BF, es) for i in range(4)])}
            Ob = Ring([Tn(P, f"a_o{i}", [128, 512], F32, "psum", es) for i in range(2)])
            rl = Ring([self.sb(f"a_rl{i}", [128, 4, 1], F32, es) for i in range(2)])
            ost = Ring([self.sb(f"a_ost{i}", [128, 4, 128], BF, es) for i in range(2)])
            base = [0, 5, 13]
            for sp in range(3):
                for g in range(3):
                    P.dma(P.sp, qTb[g][:, :], (QT, QT.ap[g * 3 + sp]))
                    P.dma(P.sp, kTb[g][:, :], (QT, QT.ap[9 + g * 3 + sp]))
                    self.load_v_tm(Vb[g], VT, VO_A + g * 384 + sp * 128, 128, 2, 65)
                for tt in range(S // 512):
                    o_st = ost.next()
                    for h in range(2):
                        hs = slice(64 * h, 64 * h + 64)
                        pairs = []
                        for g in range(3):
                            dil = A_DIL[g]
                            for sg in range(max(0, 4 * tt - dil), 4 * tt + 4):
                                dl = 4 * tt - sg
                                pairs.append({
                                    "kT": kTb[g][hs, sg * 128:(sg + 1) * 128],
                                    "q": qTb[g][hs, tt * 512:(tt + 1) * 512],
                                    "bias": [(self.k_ident[:, :], am.v(am.h[:, base[g] + dl + 3, :]))],
                                    "v": Vb[g].v(Vb[g].h[:, sg, h, :]),
                                })
                        O = Ob.next()
                        if 'attn' not in skip:
                            self.attn(pairs, O, 65, X)
                        if 'epi' in skip:
                            continue
                        O3 = O.h[:, 0:260].rearrange("p (i c) -> p i c", c=65)
                        r_ = rl.next()
                        P.recip(r_[:, :, :], O.v(O3[:, :, 64:65]))
                        P.tt(P.dve, o_st.v(o_st.h[:, :, 64 * h:64 * h + 64]), O.v(O3[:, :, 0:64]),
                             r_.v(r_.h[:, :, :].broadcast_to([128, 4, 64])), ALU.mult)
                    if 'store' in skip:
                        continue
                    P.dma(self.qst, (OTOK, OTOK.ap[tt * 512:(tt + 1) * 512, sp * 128:(sp + 1) * 128]
                                     .rearrange("(i p) c -> p i c", p=128)), o_st[:, :, :])
            P.barrier()

    def mixer_c(self, l):
        P, S, NT = self.P, self.S, self.NT
        QT, VT, OTC = self.QT[l], self.VT[l], self.OTC[l]
        with contextlib.ExitStack() as es:
            cm = self.sb("cmask", [128, 4, 512], BF, es)
            P.dma(P.sp, cm[:, :, :], (self.C["c_cmask"], self.C["c_cmask"].ap.rearrange("t p n -> p t n")))
            mU = self.sb("c_mU", [128, 128], BF, es)
            mO = self.sb("c_mO", [128, 128], BF, es)
            P.ts(P.dve, mU[:, :], self.k_U[:, :], -8.0, None, ALU.mult)
            P.memset(P.dve, mO[:, :], -8.0)
            qT = self.sb("c_q", [128, S], BF, es)
            kT = self.sb("c_k", [128, S], BF, es)
            Vb = self.sb("c_v", [128, NT, 2, 64], BF, es)
            Zr = Ring(self.psum_banks(es, 6))
            Ob = Ring([Tn(P, f"c_o{i}", [128, 512], F32, "psum", es) for i in range(1)])
            Er = Ring([self.sb(f"c_e{i}", [128, 512], F32, es) for i in range(4)])
            Sr = Ring([self.sb(f"c_s{i}", [128, 512], BF, es) for i in range(6)])
            Ar = Ring([self.sb(f"c_a{i}", [128, 512], BF, es) for i in range(6)])
            ssum = [self.sb(f"c_ss{h}", [128, 512], F32, es) for h in range(2)]
            ssbf = [Ring([self.sb(f"c_sb{h}_{i}", [128, 512], BF, es) for i in range(2)]) for h in range(2)]
            ostg = Ring([self.sb(f"c_og{i}", [128, 512], BF, es) for i in range(2)])
            for hp in range(3):
                P.dma(P.sp, qT[:, :], (QT, QT.ap[30 + hp]))
                P.dma(P.sp, kT[:, :], (QT, QT.ap[33 + hp]))
                self.load_v_tm(Vb, VT, VO_C + hp * 128, 128, 2, 64)
                for tt in range(S // 512):
                    O = Ob.next()
                    sgs = list(range(4 * tt + 3, -1, -1))
                    n = len(sgs)
                    st = {}

                    def s0(i):
                        sg = sgs[i]
                        zs, es_ = [], []
                        for h in range(2):
                            hs = slice(64 * h, 64 * h + 64)
                            z = Zr.next()
                            P.mm(z[:, :], kT[hs, sg * 128:(sg + 1) * 128], qT[hs, tt * 512:(tt + 1) * 512], start=True,
                                 stop=False, sig=True)
                            zs.append(z)
                        for h in range(2):
                            e = Er.next()
                            P.activation(e[:, :], zs[h][:, :], AF.Exp, scale=0.125)
                            es_.append(e)
                        if sg >= 4 * tt:
                            for h in range(2):
                                P.tt(P.dve, es_[h][:, :], es_[h][:, :], cm.v(cm.h[:, 4 * tt - sg + 3, :]), ALU.mult)
                        for h in range(2):
                            sp_ = Sr.next()
                            P.activation(sp_[:, :], es_[h][:, :], AF.Ln, bias=self.k_one[:, 0:1])
                            st[(i, h)] = [zs[h], sp_, None]

                    def s1(i):
                        sg = sgs[i]
                        for h in range(2):
                            z, sp_, _ = st[(i, h)]
                            P.mm(z[:, :], mU[:, :], sp_[:, :], start=False, stop=(i == 0), sig=(i == 0))
                            if i > 0:
                                P.mm(z[:, :], mO[:, :], st[("sb", h)][:, :], start=False, stop=True, sig=True)
                        for h in range(2):
                            sp_ = st[(i, h)][1]
                            if i == 0:
                                P.copy(P.pool, ssum[h][:, :], sp_[:, :])
                            else:
                                P.tt(P.pool, ssum[h][:, :], ssum[h][:, :], sp_[:, :], ALU.add)
                        if i + 1 < n:
                            for h in range(2):
                                sb_ = ssbf[h].next()
                                P.copy(P.dve, sb_[:, :], ssum[h][:, :])
                                st[("sb", h)] = sb_
                        for h in range(2):
                            a_ = Ar.next()
                            P.activation(a_[:, :], st[(i, h)][0][:, :], AF.Exp, scale=0.125)
                            st[(i, h)][2] = a_
                        if sg >= 4 * tt:
                            for h in range(2):
                                a_ = st[(i, h)][2]
                                P.tt(P.dve, a_[:, :], a_[:, :], cm.v(cm.h[:, 4 * tt - sg + 3, :]), ALU.mult)

                    def s2(i):
                        sg = sgs[i]
                        for h in range(2):
                            a_ = st[(i, h)][2]
                            P.mm(O[64 * h:64 * h + 64, :], Vb.v(Vb.h[:, sg, h, :]), a_[:, :], start=(i == 0), stop=(i == n - 1))

                    stages = [s0, s1, s2]
                    for it in range(n + 2):
                        for k, fn in enumerate(stages):
                            i = it - k
                            if 0 <= i < n:
                                fn(i)
                    og = ostg.next()
                    P.activation(og[:, :], O[:, :], AF.Identity)
                    P.dma(self.qst, (OTC, OTC.ap[hp, :, tt * 512:(tt + 1) * 512]), og[:, :])
            P.barrier()

    def mixer_b(self, l):
        import os
        skip = os.environ.get('K_B_SKIP', '')
        P, S, NT = self.P, self.S, self.NT
        QT, VT, OTOK = self.QT[l], self.VT[l], self.OTOK[l]
        ncmp = S // 16 - 1
        GC = 0.7978845608028654
        with contextlib.ExitStack() as es:
            kslc = self.sb("b_kslc", [128, S], BF, es)
            kwin = self.sb("b_kwin", [128, S], BF, es)
            P.dma(P.sp, kslc[:, :], (QT, QT.ap[28]))
            P.dma(P.sp, kwin[:, :], (QT, QT.ap[29]))
            Vs = self.sb("b_vs", [128, NT, 2, 65], BF, es)
            Vw = self.sb("b_vw", [128, NT, 2, 65], BF, es)
            for (vb, c0) in ((Vs, VO_SLC), (Vw, VO_WIN)):
                P.memset(P.pool, vb.v(vb.h[:, :, :, 64:65]), 1.0)
                self.load_v_tm(vb, VT, c0, 128, 2, 65)
            kcT = self.sb("b_kcT", [128, 256], BF, es)
            vca = [self.sb(f"b_vca{g}", [128, 2, 129], BF, es) for g in range(2)]
            wm = self.sb("b_wm", [128, 8, 512], BF, es)
            P.dma(P.sp, wm[:, :, :], (self.C["c_wmask"], self.C["c_wmask"].ap.rearrange("t p n -> p t n")))
            sm = self.sb("b_sm", [128, 4, 512], BF, es)
            P.dma(P.sp, sm[:, :, :], (self.C["c_smask"], self.C["c_smask"].ap.rearrange("t p n -> p t n")))
            esel = self.sb("b_esel", [128, NT, 128], BF, es)
            for hh in range(2):
                P.dma(P.sp, esel.v(esel.h[64 * hh:64 * hh + 64, :, :]), (self.C["c_esel"], self.C["c_esel"].ap), part=(hh > 0))
            vnf = self.sb("b_vnf", [128, NT, 64], F32, es)
            add = self.sb("b_add", [128, NT, 64], F32, es)
            P.dma(P.sp, vnf[:, :, :], (self.C["c_vnf"], self.C["c_vnf"].ap.rearrange("(s p) j -> p s j", p=128)))
            P.dma(P.sp, add[:, :, :], (self.C["c_add"], self.C["c_add"].ap.rearrange("(s p) j -> p s j", p=128)))
            Sr = Ring(self.psum_banks(es, 4))
            Ob = [Tn(P, f"b_o{i}", [128, 512], F32, "psum", es) for i in range(2)]
            psT = Tn(P, "b_psT", [128, 1024], BF, "psum", es)
            X = {"S": Sr, "P": Ring([self.sb(f"b_p{i}", [128, 512], BF, es) for i in range(4)])}

            with contextlib.ExitStack() as es2:
                for which, (qi, pe_n, w1_n, w2_n) in enumerate(((26, "cmp_pe_k", "cmp_w1_k", "cmp_w2_k"),
                                                              (27, "cmp_pe_v", "cmp_w1_v", "cmp_w2_v"))):
                    if 'pro' in skip:
                        continue
                    src = self.sb(f"b_cin{which}", [128, S], BF, es2)
                    P.dma(P.sp, src[:, :], (QT, QT.ap[qi]))
                    w1T = self.sb(f"b_w1T{which}", [128, 32, 128], BF, es2)
                    peT = self.sb(f"b_peT{which}", [128, 32], F32, es2)
                    w2 = self.sb(f"b_w2{which}", [128, 128], BF, es2)
                    w1src = self.W[w1_n].ap[l].rearrange("(p d) h -> d p h", d=64)
                    pesrc = self.W[pe_n].ap[l].rearrange("p d -> d p")
                    for hh in range(2):
                        P.dma(P.pool, w1T.v(w1T.h[64 * hh:64 * hh + 64, :, :]), (self.x, w1src), part=(hh > 0))
                        P.dma(P.sp, peT.v(peT.h[64 * hh:64 * hh + 64, :]), (self.x, pesrc), part=(hh > 0),
                              allow_slow_non_contiguous=True)
                        P.dma(P.pool, w2.v(w2.h[:, 64 * hh:64 * hh + 64]), (self.x, self.W[w2_n].ap[l]), part=(hh > 0))
                    kpe = self.sb(f"b_kpe{which}", [128, 32, 256], BF, es2)
                    P.memset(P.pool, kpe[:, :, :], 0.0)
                    win_ap = bass.AP(tensor=src.h, offset=0, ap=[[S, 128], [1, 32], [16, ncmp]])
                    P.tt(P.dve, kpe.v(kpe.h[:, :, 0:ncmp]), src.v(win_ap),
                         peT.v(peT.h[:, :].unsqueeze(2).broadcast_to([128, 32, ncmp])), ALU.add)
                    xs = self.sb(f"b_xs{which}", [128, 256], F32, es2)
                    x2 = self.sb(f"b_x2{which}", [128, 256], F32, es2)
                    hg = self.sb(f"b_hg{which}", [128, 256], BF, es2)
                    for g in range(2):
                        hs = slice(64 * g, 64 * g + 64)
                        ph = Sr.next()
                        for p_ in range(32):
                            P.mm(ph[:, 0:256], w1T.v(w1T.h[hs, p_, :]), kpe.v(kpe.h[hs, p_, :]), start=(p_ == 0),
                                 stop=(p_ == 31))
                        P.activation(xs[:, :], ph[:, 0:256], AF.Identity)
                        P.tt(P.dve, x2[:, :], xs[:, :], xs[:, :], ALU.mult)
                        P.ts(P.dve, x2[:, :], x2[:, :], 0.044715, 1.0, ALU.mult, ALU.add)
                        P.tt(P.dve, x2[:, :], x2[:, :], xs[:, :], ALU.mult)
                        P.activation(x2[:, :], x2[:, :], AF.Sigmoid, scale=2.0 * GC)
                        P.tt(P.dve, hg[:, :], xs[:, :], x2[:, :], ALU.mult)
                        if which == 0:
                            pk = Sr.next()
                            P.mm(pk[:, 0:256], w2[:, :], hg[:, :], start=True, stop=True)
                            P.copy(P.act, kcT[hs, :], pk[hs, 0:256])
                        else:
                            for ct in range(2):
                                pv = Sr.next()
                                P.mm(pv[:, 0:64], hg[:, ct * 128:(ct + 1) * 128], w2[:, 0:64], start=True, stop=True)
                                P.copy(P.act, vca[g].v(vca[g].h[:, ct, 0:64]), pv[:, 0:64])
                for g in range(2):
                    P.memset(P.pool, vca[g].v(vca[g].h[:, :, 64:65]), 1.0)
                    P.dma(P.sp, vca[g].v(vca[g].h[:, :, 65:129]),
                          (self.C["c_wsel"], self.C["c_wsel"].ap.rearrange("(ct p) j -> p ct j", p=128)))
                P.barrier()

            qu = [Ring([self.sb(f"b_qu{r}_{i}", [128, 512], BF, es) for i in range(2)]) for r in range(4)]
            qr = [Ring([self.sb(f"b_qr{r}_{i}", [128, 512], BF, es) for i in range(2)]) for r in range(4)]
            glr = Ring([self.sb(f"b_gl{i}", [128, 4, 24], BF, es) for i in range(2)])
            gsr = Ring([self.sb(f"b_gs{i}", [128, 4, 24], F32, es) for i in range(2)])
            cbr = Ring([self.sb(f"b_cb{i}", [128, 2, 512], BF, es) for i in range(2)])
            sacc = [self.sb(f"b_sacc{g}", [128, 4, 64], F32, es) for g in range(2)]
            oB = self.sb("b_oB", [128, 4, 512], F32, es)
            ob16 = Ring([self.sb(f"b_ob16_{i}", [128, 4, 512], BF, es) for i in range(2)])
            BT = self.sb("b_BT", [128, 512], BF, es)
            rlr = Ring([self.sb(f"b_rl{i}", [128, 4, 1], F32, es) for i in range(4)])
            tmpr = Ring([self.sb(f"b_tmp{i}", [128, 4, 64], F32, es) for i in range(3)])
            sc = self.sb("b_sc", [128, 64], F32, es)
            wk = self.sb("b_wk", [128, 64], F32, es)
            m8 = self.sb("b_m8", [128, 8], F32, es)
            m8b = self.sb("b_m8b", [128, 8], F32, es)
            btr = Ring([self.sb(f"b_bt{i}", [128, 128], BF, es) for i in range(4)])
            for tt in range(S // 512):
                if 'main' in skip:
                    continue
                tsl = slice(tt * 512, (tt + 1) * 512)
                qut = []
                qrt = []
                for r in range(4):
                    a = qu[r].next()
                    P.dma(P.sp, a[:, :], (QT, QT.ap[18 + r, :, tsl]))
                    qut.append(a)
                    b = qr[r].next()
                    P.dma(P.sp, b[:, :], (QT, QT.ap[22 + r, :, tsl]))
                    qrt.append(b)
                gl = glr.next()
                P.dma(P.sp, gl[:, :, :], (VT, VT.ap[tsl, VO_GATE:VO_GATE + 24].rearrange("(i p) c -> p i c", p=128)))
                gs = gsr.next()
                P.activation(gs[:, :, :], gl[:, :, :], AF.Sigmoid)
                cb = cbr.next()
                P.dma(P.sp, cb[:, :, :], (self.C["c_cmpb"], self.C["c_cmpb"].ap[:, tsl].rearrange("(ct p) n -> p ct n", p=128)))

                def epilogue(O3, nsub, sub0, h, br, accumulate):
                    r_ = rlr.next()
                    rv = r_.v(r_.h[:, 0:nsub, :])
                    P.ts(P.dve, rv, O3[2], 1e-30, None, ALU.max)
                    P.recip(rv, rv)
                    rg = rlr.next()
                    rgv = rg.v(rg.h[:, 0:nsub, :])
                    P.tt(P.dve, rgv, rv, gs.v(gs.h[:, sub0:sub0 + nsub, 3 * h + br:3 * h + br + 1]), ALU.mult)
                    dst = oB.v(oB.h[:, sub0:sub0 + nsub, 64 * h:64 * h + 64])
                    bc = rg.v(rg.h[:, 0:nsub, :].broadcast_to([128, nsub, 64]))
                    if not accumulate:
                        P.tt(P.dve, dst, O3[0], bc, ALU.mult)
                    else:
                        t_ = tmpr.next()
                        tv = t_.v(t_.h[:, 0:nsub, :])
                        P.tt(P.dve, tv, O3[0], bc, ALU.mult)
                        P.tt(P.dve, dst, dst, tv, ALU.add)
                    return r_

                for h in range(8):
                    if 'cmp' in skip:
                        continue
                    g, r = h // 4, h % 4
                    hs = slice(64 * g, 64 * g + 64)
                    pairs = [{"kT": kcT[hs, ct * 128:(ct + 1) * 128], "q": qut[r][hs, :],
                              "bias": [(self.k_ident[:, :], cb.v(cb.h[:, ct, :]))],
                              "v": vca[g].v(vca[g].h[:, ct, :])} for ct in range(2)]
                    outs = [Ob[i // 2][:, (i % 2) * 129:(i % 2) * 129 + 129] for i in range(4)]
                    self.attn(pairs, None, 129, X, nsub=4, outs=outs, starts=(0, 2))
                    for bk in range(2):
                        O = Ob[bk]
                        O3 = O.h[:, 0:258].rearrange("p (i c) -> p i c", c=129)
                        views = (O.v(O3[:, :, 0:64]), O.v(O3[:, :, 65:129]), O.v(O3[:, :, 64:65]))
                        r_ = epilogue(views, 2, 2 * bk, h, 0, False)
                        bc = r_.v(r_.h[:, 0:2, :].broadcast_to([128, 2, 64]))
                        sd = sacc[g].v(sacc[g].h[:, 2 * bk:2 * bk + 2, :])
                        if r == 0:
                            P.tt(P.dve, sd, views[1], bc, ALU.mult)
                        else:
                            t_ = tmpr.next()
                            tv = t_.v(t_.h[:, 0:2, :])
                            P.tt(P.dve, tv, views[1], bc, ALU.mult)
                            P.tt(P.pool, sd, sd, tv, ALU.add)
                if 'topk' not in skip:
                    for i in range(4):
                        tix = 4 * tt + i
                        bt = btr.next()
                        for g in range(2):
                            P.tt(P.dve, sc[:, :], sacc[g].v(sacc[g].h[:, i, :]), vnf.v(vnf.h[:, tix, :]), ALU.mult)
                            P.tt(P.dve, sc[:, :], sc[:, :], add.v(add.h[:, tix, :]), ALU.add)
                            P.op(P.dve, lambda: self.nc.vector.max(out=m8.h[:, :], in_=sc.h[:, :]), [sc[:, :]], [m8[:, :]])
                            P.op(P.dve, lambda: self.nc.vector.match_replace(out=wk.h[:, :], in_to_replace=m8.h[:, :],
                                                                              in_values=sc.h[:, :], imm_value=-1e30),
                                 [sc[:, :], m8[:, :]], [wk[:, :]])
                            P.op(P.dve, lambda: self.nc.vector.max(out=m8b.h[:, :], in_=wk.h[:, :]), [wk[:, :]], [m8b[:, :]])
                            P.ts(P.dve, bt[:, 64 * g:64 * g + 64], sc[:, :], m8b[:, 7:8], NEG, ALU.is_lt, ALU.mult)
                        P.transpose(psT[:, i * 128:(i + 1) * 128], bt[:, :], self.k_ident[:, :], sig=True)
                    P.copy(P.act, BT[:, :], psT[:, 0:512])
                for h in range(8):
                    g, r = h // 4, h % 4
                    hs = slice(64 * g, 64 * g + 64)
                    for br in (1, 2):
                        if ('sel' in skip and br == 1) or ('win' in skip and br == 2):
                            continue
                        pairs = []
                        if br == 1:
                            for sg in range(0, 4 * tt + 4):
                                bias = [(esel.v(esel.h[hs, sg, :]), BT[hs, :])]
                                if sg >= 4 * tt:
                                    bias.append((self.k_ident[:, :], sm.v(sm.h[:, 4 * tt - sg + 3, :])))
                                pairs.append({"kT": kslc[hs, sg * 128:(sg + 1) * 128], "q": qrt[r][hs, :], "bias": bias,
                                              "v": Vs.v(Vs.h[:, sg, g, :])})
                        else:
                            for sg in range(max(0, 4 * tt - 4), 4 * tt + 4):
                                bias = [(self.k_ident[:, :], wm.v(wm.h[:, 4 * tt - sg + 3, :]))]
                                pairs.append({"kT": kwin[hs, sg * 128:(sg + 1) * 128], "q": qrt[r][hs, :], "bias": bias,
                                              "v": Vw.v(Vw.h[:, sg, g, :])})
                        O = Ob[(2 * h + br) % 2]
                        self.attn(pairs, O, 65, X)
                        O3 = O.h[:, 0:260].rearrange("p (i c) -> p i c", c=65)
                        views = (O.v(O3[:, :, 0:64]), None, O.v(O3[:, :, 64:65]))
                        epilogue(views, 4, 0, h, br, True)
                o16 = ob16.next()
                P.activation(o16[:, :, :], oB[:, :, :], AF.Identity)
                P.dma(self.qst, (OTOK, OTOK.ap[tsl, 384:896].rearrange("(i p) c -> p i c", p=128)), o16[:, :, :])
            P.barrier()

def _in_map(consts, x_b, pos_b, weights, norm_final):
    m = {"x": np.ascontiguousarray(x_b), "pos": np.ascontiguousarray(pos_b).reshape(1, -1).astype(np.int32)}
    for n in WNAMES:
        m[n] = weights[n]
    m["norm_final"] = norm_final
    m.update(consts)
    return m


def kernel(x, positions, norm_ffn1, ffn1_w1, ffn1_w3, ffn1_w2, norm_mix, w_in,
           cmp_pe_k, cmp_w1_k, cmp_w2_k, cmp_pe_v, cmp_w1_v, cmp_w2_v,
           w_gate, w_up, w_out, norm_ffn2, ffn2_w1, ffn2_w3, ffn2_w2, norm_final):
    loc = locals()
    x = np.asarray(x)
    B, S, _ = x.shape
    L = int(np.asarray(norm_ffn1).shape[0])
    weights = {n: np.ascontiguousarray(np.asarray(loc[n], dtype=np.float32)) for n in WNAMES}
    bld = Builder(S, L)
    nc = bld.build()
    nf = np.ascontiguousarray(np.asarray(norm_final, dtype=np.float32))
    pos = np.asarray(positions)
    in_maps = [_in_map(bld.consts, x[b], pos[b], weights, nf) for b in range(B)]
    res = run_bass_kernel_spmd(nc, in_maps, core_ids=list(range(B)))
    return np.stack([np.asarray(res.results[b]["out"]) for b in range(B)], axis=0).astype(np.float32)
```

```python
import contextlib
import numpy as np
import ml_dtypes
import concourse.bass as bass
import concourse.mybir as mybir
from concourse.bass_utils import run_bass_kernel_spmd

F32 = mybir.dt.float32
BF = mybir.dt.bfloat16
I32 = mybir.dt.int32
AF = mybir.ActivationFunctionType
ALU = mybir.AluOpType
AX = mybir.AxisListType

D = 1024
DFF = 2816
NF = DFF // 128
HD = 64
NEG = -1920.0
EPS = 1e-6
A_IN = 3456
B_IN = 1304
IN_W = 5912
NSEL = 16


class Buf:
    __slots__ = ("name", "w", "r", "sem", "semv", "psum")

    def __init__(self, name):
        self.name = name
        self.psum = False
        self.w = None
        self.r = {}
        self.sem = None
        self.semv = 0


class V:
    __slots__ = ("ap", "bufs")

    def __init__(self, ap, bufs):
        self.ap = ap
        self.bufs = bufs


class Tn:
    def __init__(self, P, name, shape, dtype, space="sbuf", es=None):
        es = es if es is not None else P.es
        P.ntn = getattr(P, "ntn", 0) + 1
        name = f"{name}_{P.ntn}"
        if space == "sbuf":
            self.h = es.enter_context(P.nc.sbuf_tensor(name, list(shape), dtype))
        else:
            self.h = es.enter_context(P.nc.psum_tensor(name, list(shape), dtype))
        self.buf = Buf(name)
        self.buf.psum = (space != "sbuf")
        self.shape = list(shape)
        self.dtype = dtype
        self.P = P

    def __getitem__(self, idx):
        return V(self.h[idx], [self.buf])

    def v(self, ap):
        return V(ap, [self.buf])

    def raw(self, offset, ap):
        return V(bass.AP(tensor=self.h, offset=offset, ap=ap), [self.buf])


class DT:
    def __init__(self, P, name, shape, dtype, kind="Internal"):
        self.t = P.nc.dram_tensor(name, list(shape), dtype, kind=kind)
        self.ap = self.t.ap()
        self.pending = {}
        self.name = name
        self.sem = None
        self.semv = 0
        P.dts.append(self)


class Eng:
    def __init__(self, P, name, h):
        self.P = P
        self.name = name
        self.h = h
        self.sem = P.new_sem("e_" + name)
        self.cnt = 0
        self.waited = {}
        self.pend_r = []
        self.pend_w = []

    def wait(self, toks):
        for tok in toks:
            sem, val = tok[0], tok[1]
            k = sem.num
            if self.waited.get(k, 0) < val:
                self.h.wait_ge(sem, val)
                self.waited[k] = val


class Prog:
    def __init__(self, nc, es):
        self.nc = nc
        self.es = es
        self.nsem = 0
        self.dts = []
        self.scope = None
        self.pe = Eng(self, "pe", nc.tensor)
        self.act = Eng(self, "act", nc.scalar)
        self.dve = Eng(self, "dve", nc.vector)
        self.pool = Eng(self, "pool", nc.gpsimd)
        self.sp = Eng(self, "sp", nc.sync)
        self.engs = [self.pe, self.act, self.dve, self.pool, self.sp]
        self.bar_sem = self.new_sem("bar")
        self.bar_n = 0
        self.dma_toks = {}
        self.n_ins = 0

    def new_sem(self, name):
        self.nsem += 1
        h = self.nc.alloc_semaphore(name=f"{name}_{self.nsem}")
        if self.scope is not None:
            self.scope.append(h)
        return h

    def scope_begin(self):
        assert self.scope is None
        self.scope = []

    def scope_end(self):
        self.barrier()
        sems = self.scope
        self.scope = None
        if sems:
            nums = set(h.num for h in sems)
            self.nc.clear_and_free_semaphores(sems)
            self.op(self.pool, lambda: self.nc.gpsimd.memset(self.scratch.h[:, :], 0.0), [], [self.scratch[:, :]])
            for e in self.engs:
                for k in list(e.waited.keys()):
                    if k in nums:
                        del e.waited[k]
            for k in list(self.dma_toks.keys()):
                if k in nums:
                    del self.dma_toks[k]
            for d in self.dts:
                d.pending = {k: v for k, v in d.pending.items() if k not in nums}
                if d.sem is not None and d.sem.num in nums:
                    d.sem = None
                    d.semv = 0
        self.barrier()

    def _deps(self, E, reads, writes, strict=False):
        toks = []
        for b in reads:
            if b.w is not None:
                t = b.w
                if strict or t[2] != E.name or E.name != "pe":
                    toks.append(t)
            if b.psum:
                for t in b.r.values():
                    if t[2] != E.name:
                        toks.append(t)
        same_ok = (E.name == "pe")
        for b in writes:
            if b.w is not None and (strict or b.w[2] != E.name or not same_ok):
                toks.append(b.w)
            for t in b.r.values():
                if strict or t[2] != E.name or not same_ok:
                    toks.append(t)
        return toks

    def _commit(self, tok, reads, writes):
        ek = tok[2] if tok[2] is not None else tok[0].num
        for b in reads:
            b.r[ek] = tok
        for b in writes:
            b.w = tok
            b.r = {}

    def op(self, E, fn, reads, writes, sig=True):
        rb = [b for v in reads for b in v.bufs]
        wb = [b for v in writes for b in v.bufs]
        for b in rb + wb:
            assert not (b in self.pe.pend_r or b in self.pe.pend_w) or E is self.pe, \
                f"buffer {b.name} has unsignalled PE access"
        E.wait(self._deps(E, rb, wb))
        ins = fn()
        self.n_ins += 1
        if E is self.pe and not sig:
            E.pend_r += rb
            E.pend_w += wb
            return ins
        E.cnt += 1
        ins.then_inc(E.sem, 1)
        tok = (E.sem, E.cnt, E.name)
        if E is self.pe:
            rb = rb + E.pend_r
            wb = wb + E.pend_w
            E.pend_r = []
            E.pend_w = []
        self._commit(tok, rb, wb)
        return ins

    def dma(self, Q, out, in_, **kw):
        o_dram = isinstance(out, tuple)
        i_dram = isinstance(in_, tuple)
        toks = []
        if i_dram:
            toks += list(in_[0].pending.values())
            in_ap = in_[1]
        else:
            in_ap = in_.ap
            for b in in_.bufs:
                if b.w is not None:
                    toks.append(b.w)
        part = kw.pop("part", False)
        if o_dram:
            out_ap = out[1]
        else:
            out_ap = out.ap
            dd = self._deps(Q, [], out.bufs, strict=True)
            if part:
                dd = [t for t in dd if not (t[2] is None and t is out.bufs[0].w)]
            toks += dd
        Q.wait(toks)
        ins = Q.h.dma_start(out=out_ap, in_=in_ap, **kw)
        self.n_ins += 1
        if not o_dram:
            b = out.bufs[0]
        elif not i_dram:
            b = in_.bufs[0]
        else:
            b = out[0]
        if b.sem is None:
            b.sem = self.new_sem("d")
        b.semv += 16
        ins.then_inc(b.sem, 16)
        tok = (b.sem, b.semv, None)
        self.dma_toks[b.sem.num] = tok
        if not o_dram:
            out.bufs[0].w = tok
            out.bufs[0].r = {}
        else:
            out[0].pending[b.sem.num] = tok
            if not i_dram:
                in_.bufs[0].r[b.sem.num] = tok
        return ins

    def barrier(self):
        sp = self.sp
        toks = [(e.sem, e.cnt, e.name) for e in self.engs if e is not sp and e.cnt > 0]
        toks += list(self.dma_toks.values())
        sp.wait(toks)
        self.bar_n += 1
        sp.h.sem_inc(self.bar_sem, 1)
        for e in self.engs:
            if e is not sp:
                e.h.wait_ge(self.bar_sem, self.bar_n)
                for t in toks:
                    k = t[0].num
                    if e.waited.get(k, 0) < t[1]:
                        e.waited[k] = t[1]

    def mm(self, out, lhsT, rhs, start, stop, sig=None):
        if sig is None:
            sig = stop
        return self.op(self.pe,
                       lambda: self.nc.tensor.matmul(out.ap, lhsT=lhsT.ap, rhs=rhs.ap, start=start, stop=stop,
                                                     skip_group_check=True),
                       [lhsT, rhs], [out], sig=sig)

    def transpose(self, out, in_, ident, sig=True):
        return self.op(self.pe, lambda: self.nc.tensor.transpose(out.ap, in_.ap, ident.ap),
                       [in_, ident], [out], sig=sig)

    def activation(self, out, in_, func, scale=1.0, bias=None, eng=None):
        E = eng or self.act
        kw = {}
        reads = [in_]
        if bias is not None:
            if isinstance(bias, V):
                kw["bias"] = bias.ap
                reads.append(bias)
            else:
                kw["bias"] = bias
        if isinstance(scale, V):
            reads.append(scale)
            sc = scale.ap
        else:
            sc = scale
        return self.op(E, lambda: E.h.activation(out=out.ap, in_=in_.ap, func=func, scale=sc, **kw), reads, [out])

    def tt(self, E, out, in0, in1, op):
        return self.op(E, lambda: E.h.tensor_tensor(out=out.ap, in0=in0.ap, in1=in1.ap, op=op), [in0, in1], [out])

    def ts(self, E, out, in0, s1, s2, op0, op1=None):
        reads = [in0]
        a1 = s1
        a2 = s2
        if isinstance(s1, V):
            reads.append(s1)
            a1 = s1.ap
        if isinstance(s2, V):
            reads.append(s2)
            a2 = s2.ap
        if op1 is None:
            return self.op(E, lambda: E.h.tensor_scalar(out=out.ap, in0=in0.ap, scalar1=a1, scalar2=None, op0=op0),
                           reads, [out])
        return self.op(E, lambda: E.h.tensor_scalar(out=out.ap, in0=in0.ap, scalar1=a1, scalar2=a2, op0=op0, op1=op1),
                       reads, [out])

    def stt(self, out, in0, scalar, in1, op0, op1):
        reads = [in0, in1]
        a = scalar
        if isinstance(scalar, V):
            reads.append(scalar)
            a = scalar.ap
        E = self.dve
        return self.op(E, lambda: E.h.scalar_tensor_tensor(out=out.ap, in0=in0.ap, scalar=a, in1=in1.ap, op0=op0, op1=op1),
                       reads, [out])

    def copy(self, E, out, in_):
        if E is self.act:
            return self.op(E, lambda: E.h.copy(out=out.ap, in_=in_.ap), [in_], [out])
        return self.op(E, lambda: E.h.tensor_copy(out=out.ap, in_=in_.ap), [in_], [out])

    def memset(self, E, out, val):
        return self.op(E, lambda: E.h.memset(out.ap, val), [], [out])

    def recip(self, out, in_):
        E = self.dve
        return self.op(E, lambda: E.h.reciprocal(out=out.ap, in_=in_.ap), [in_], [out])


def _fm_blocks():
    blks = []
    for g in range(3):
        for sp in range(3):
            blks.append(("qA", [(0 * 1152 + g * 384 + sp * 128, 128)], g * 3 + sp, None))
    for g in range(3):
        for sp in range(3):
            blks.append(("kA", [(1152 + g * 384 + sp * 128, 128)], 9 + g * 3 + sp, None))
    bq = A_IN
    for r in range(4):
        blks.append(("qB", [(bq + r * 64, 64), (bq + (4 + r) * 64, 64)], 22 + r, 18 + r))
    bkv = A_IN + 512
    blks.append(("kcmp", [(bkv, 128)], None, 26))
    blks.append(("vcmp", [(bkv + 128, 128)], None, 27))
    blks.append(("kslc", [(bkv + 256, 128)], 28, None))
    blks.append(("kwin", [(bkv + 512, 128)], 29, None))
    cb = A_IN + B_IN
    for hp in range(3):
        blks.append(("qC", [(cb + hp * 128, 128)], None, 30 + hp))
    for hp in range(3):
        blks.append(("kC", [(cb + 384 + hp * 128, 128)], None, 33 + hp))
    return blks


FM = _fm_blocks()
NQT = 36
VRUNS = [(2304, 1152), (A_IN + 512 + 256 + 128, 128), (A_IN + 512 + 512 + 128, 128), (A_IN + 1280, 24),
         (A_IN + B_IN + 768, 384)]
NV = 1816
VO_A, VO_SLC, VO_WIN, VO_GATE, VO_C = 0, 1152, 1280, 1408, 1432
NVG = 4
VGW = NV // NVG
A_DIL = (1, 4, 16)


def _consts(S):
    bf = ml_dtypes.bfloat16
    NT = S // 128
    c = {}
    c["c_ident"] = np.eye(128, dtype=np.float32).astype(bf)
    c["c_identf"] = np.eye(128, dtype=np.float32)
    perm = np.zeros((128, 128), np.float32)
    for m in range(128):
        d = m % 64
        if d < 8:
            perm[m + 8, m] = 1.0
        elif d < 16:
            perm[m - 8, m] = 1.0
    c["c_perm"] = perm.astype(bf)
    jj = np.arange(128)
    c["c_U"] = (jj[:, None] >= jj[None, :]).astype(np.float32).astype(bf)
    c["c_ones"] = np.ones((128, 128), np.float32).astype(bf)
    si = np.arange(128)[:, None]
    ti = np.arange(512)[None, :]
    tiles = []
    for dil in A_DIL:
        for dl in range(-3, dil + 1):
            d = 128 * dl + ti - si
            ok = (d >= 0) & (d <= 128 * dil) & (d % dil == 0)
            tiles.append(np.where(ok, 0.0, NEG))
    c["c_amask"] = np.stack(tiles).astype(np.float32).astype(bf)
    tiles = []
    for dl in range(-3, 5):
        d = 128 * dl + ti - si
        tiles.append(np.where((d >= 0) & (d <= 511), 0.0, NEG))
    c["c_wmask"] = np.stack(tiles).astype(np.float32).astype(bf)
    tiles = []
    tiles2 = []
    for dl in range(-3, 1):
        d = 128 * dl + ti - si
        tiles.append(np.where(d >= 0, 0.0, NEG))
        tiles2.append(np.where(d >= 1, 1.0, 0.0))
    c["c_smask"] = np.stack(tiles).astype(np.float32).astype(bf)
    c["c_cmask"] = np.stack(tiles2).astype(np.float32).astype(bf)
    cc = np.arange(256)[:, None]
    tt = np.arange(S)[None, :]
    n_cmp = S // 16 - 1
    c["c_cmpb"] = np.where((16 * cc + 31 <= tt) & (cc < n_cmp), 0.0, NEG).astype(np.float32).astype(bf)
    es = np.zeros((64, NT, 128), np.float32)
    for sg in range(NT):
        for s_ in range(128):
            j = 2 * sg + s_ // 64
            if j < 64:
                es[j, sg, s_] = 1.0
    c["c_esel"] = es.astype(bf)
    w = np.zeros((256, 64), np.float32)
    for j in range(64):
        for m in range(4):
            for n in range(2):
                ci = 4 * j + m + n
                if ci < 256:
                    w[ci, j] += 1.0
    c["c_wsel"] = w.astype(bf)
    t = np.arange(S)[:, None]
    j = np.arange(64)[None, :]
    cur = t // 64
    nsel = S // 64
    valid = (j * 64 <= t) & (j < nsel)
    forced = ((j == 0) | (j == cur) | (j == cur - 1)) & (j < nsel)
    c["c_vnf"] = (valid & ~forced).astype(np.float32)
    c["c_add"] = np.where(j >= nsel, -3.0, np.where(forced, 1e4, np.where(valid, 0.0, -1.0))).astype(np.float32)
    rp = np.zeros((128, 2), np.float32)
    for p in range(128):
        d = p % 64
        if d < 16:
            rp[p, 0] = np.float32(500000.0) ** np.float32(-(2 * (d % 8)) / 16.0)
            rp[p, 1] = -1.0 if d < 8 else 1.0
    c["c_rope"] = rp
    return c


WNAMES = ["norm_ffn1", "ffn1_w1", "ffn1_w3", "ffn1_w2", "norm_mix", "w_in", "cmp_pe_k", "cmp_w1_k", "cmp_w2_k",
          "cmp_pe_v", "cmp_w1_v", "cmp_w2_v", "w_gate", "w_up", "w_out", "norm_ffn2", "ffn2_w1", "ffn2_w3",
          "ffn2_w2"]
WSHAPES = {"norm_ffn1": [D], "ffn1_w1": [D, DFF], "ffn1_w3": [D, DFF], "ffn1_w2": [DFF, D], "norm_mix": [D],
           "w_in": [D, IN_W], "cmp_pe_k": [32, 64], "cmp_w1_k": [2048, 128], "cmp_w2_k": [128, 64],
           "cmp_pe_v": [32, 64], "cmp_w1_v": [2048, 128], "cmp_w2_v": [128, 64], "w_gate": [D, 3 * D],
           "w_up": [1280, D], "w_out": [D, D], "norm_ffn2": [D], "ffn2_w1": [D, DFF], "ffn2_w3": [D, DFF],
           "ffn2_w2": [DFF, D]}


class Ring:
    def __init__(self, tiles):
        self.t = tiles
        self.i = 0

    def next(self):
        t = self.t[self.i % len(self.t)]
        self.i += 1
        return t


class Builder:
    def __init__(self, S, L, TT=512, dbg=(), phases=None):
        self.S, self.L, self.TT = S, L, TT
        self.NT = S // 128
        self.NG = TT // 512
        self.dbg = set(dbg)
        self.phases = phases
        self.consts = _consts(S)
        nc = bass.Bass("TRN2", target_bir_lowering=False)
        self.nc = nc
        self.es = contextlib.ExitStack()
        self.P = Prog(nc, self.es)
        self.din = {}
        import os
        self.qst = {'sp': self.P.sp, 'pool': self.P.pool, 'act': self.P.act}[os.environ.get('K_QST', 'pool')]
        self._ptoks = {}
        self._pn = 0

    def inp(self, name, shape, dtype):
        d = DT(self.P, name, shape, dtype, kind="ExternalInput")
        self.din[name] = d
        return d

    def scr(self, name, shape, dtype):
        kind = "ExternalOutput" if name in self.dbg else "Internal"
        return DT(self.P, name, shape, dtype, kind=kind)

    def sb(self, name, shape, dtype, es=None):
        return Tn(self.P, name, shape, dtype, "sbuf", es)

    def psum_banks(self, es, n=8):
        return [Tn(self.P, f"psb{i}", [128, 512], F32, "psum", es) for i in range(n)]

    def build(self):
        P, S, L = self.P, self.S, self.L
        self.x = self.inp("x", [S, D], F32)
        self.pos = self.inp("pos", [1, S], I32)
        self.W = {}
        for n in WNAMES:
            self.W[n] = self.inp(n, [L] + WSHAPES[n], F32)
        self.nf = self.inp("norm_final", [D], F32)
        self.C = {}
        for k, v in self.consts.items():
            self.C[k] = self.inp(k, list(v.shape), BF if v.dtype == ml_dtypes.bfloat16 else F32)
        self.out = DT(P, "out", [S, D], F32, kind="ExternalOutput")
        self.HT = [self.scr(f"HT{l}", [8, 128, S], F32) for l in range(L)]
        self.HT2 = [self.scr(f"HTb{l}", [8, 128, S], F32) for l in range(L)]
        self.QT = [self.scr(f"QT{l}", [NQT, 128, S], BF) for l in range(L)]
        self.VT = [self.scr(f"VT{l}", [S, NV], BF) for l in range(L)]
        self.GT = [self.scr(f"GT{l}", [24, 128, S], BF) for l in range(L)]
        self.OTOK = [self.scr(f"OTOK{l}", [S, 896], BF) for l in range(L)]
        self.OTC = [self.scr(f"OTC{l}", [3, 128, S], BF) for l in range(L)]
        self.CS = self.scr("CS", [2, 128, S], F32)
        self.WB = []
        for l in range(L):
            d = {}
            for f in (1, 2):
                d[f"w1_{f}"] = self.scr(f"b_w1_{f}_{l}", [NF, 128, 8 * 128], BF)
                d[f"w3_{f}"] = self.scr(f"b_w3_{f}_{l}", [NF, 128, 8 * 128], BF)
                d[f"w2_{f}"] = self.scr(f"b_w2_{f}_{l}", [8, 128, NF * 128], BF)
            d["win"] = self.scr(f"b_win_{l}", [len(FM), 128, 8 * 128], BF)
            d["wv"] = self.scr(f"b_wv_{l}", [NVG, 128, 8 * VGW], BF)
            d["wg"] = self.scr(f"b_wg_{l}", [24, 128, 8 * 128], BF)
            d["wup"] = self.scr(f"b_wup_{l}", [8, 128, 10 * 128], BF)
            d["wout"] = self.scr(f"b_wout_{l}", [8, 128, 8 * 128], BF)
            self.WB.append(d)

        P.scratch = self.sb("p_scratch", [128, 8], F32)
        self.k_ident = self.sb("k_ident", [128, 128], BF)
        self.k_identf = self.sb("k_identf", [128, 128], F32)
        self.k_perm = self.sb("k_perm", [128, 128], BF)
        self.k_U = self.sb("k_U", [128, 128], BF)
        self.k_ones = self.sb("k_ones", [128, 128], BF)
        self.k_rope = self.sb("k_rope", [128, 2], F32)
        self.k_one = self.sb("k_one", [128, 1], F32)
        P.memset(P.dve, self.k_one[:, :], 1.0)
        for t, n in ((self.k_ident, "c_ident"), (self.k_identf, "c_identf"), (self.k_perm, "c_perm"),
                     (self.k_U, "c_U"), (self.k_ones, "c_ones"), (self.k_rope, "c_rope")):
            P.dma(P.sp, t[:, :], (self.C[n], self.C[n].ap))
        self.g = {}
        for n in ("norm_ffn1", "norm_mix", "norm_ffn2"):
            for l in range(L):
                t = self.sb(f"g_{n}_{l}", [128, 8], F32)
                P.dma(P.sp, t[:, :], (self.W[n], self.W[n].ap[l].rearrange("(c p) -> p c", p=128)),
                      allow_slow_non_contiguous=True)
                self.g[(n, l)] = t
        t = self.sb("g_final", [128, 8], F32)
        P.dma(P.sp, t[:, :], (self.nf, self.nf.ap.rearrange("(c p) -> p c", p=128)), allow_slow_non_contiguous=True)
        self.g[("final", 0)] = t

        ph = self.phases
        P.barrier()
        if ph is None or "p0" in ph:
            P.scope_begin()
            self.phase0()
            P.scope_end()
        for l in range(L):
            if ph is None or "p1" in ph:
                P.scope_begin()
                self.phase1(l)
                P.scope_end()
            if ph is None or "p2" in ph:
                self.phase2(l)
            if ph is None or "p3" in ph:
                P.scope_begin()
                self.phase3(l)
                P.scope_end()
        self.es.close()
        return self.nc

    def _prep(self, dst, j, src2d, runs, nchunk, width):
        P = self.P
        off = 0
        dv = dst.ap[j].rearrange("p (c n) -> p c n", n=width)
        for (col, w) in runs:
            src = src2d[:, col:col + w].rearrange("(c p) n -> p c n", p=128)
            P.dma(P.pool, (dst, dv[:, :, off:off + w]), (self.x, src))
            off += w
            self._ptoks[dst.sem.num] = (dst.sem, dst.semv, None)
            self._pn += 1
            if self._pn % 12 == 0:
                P.pool.wait(list(self._ptoks.values()))

    def phase0(self):
        import os
        P, S, L = self.P, self.S, self.L
        if os.environ.get('K_SKIP_ROPE') is None:
            self.rope_tables()
        if os.environ.get('K_SKIP_PREP'):
            return
        for l in range(L):
            wb = self.WB[l]
            for f in (1, 2):
                w1 = self.W[f"ffn{f}_w1"].ap[l]
                w3 = self.W[f"ffn{f}_w3"].ap[l]
                w2 = self.W[f"ffn{f}_w2"].ap[l]
                for j in range(NF):
                    self._prep(wb[f"w1_{f}"], j, w1, [(j * 128, 128)], 8, 128)
                    self._prep(wb[f"w3_{f}"], j, w3, [(j * 128, 128)], 8, 128)
                for j in range(8):
                    self._prep(wb[f"w2_{f}"], j, w2, [(j * 128, 128)], NF, 128)
                if f == 1:
                    win = self.W["w_in"].ap[l]
                    for j, blk in enumerate(FM):
                        self._prep(wb["win"], j, win, blk[1], 8, 128)
                    cols = []
                    for (c0, w) in VRUNS:
                        cols += [(c0, w)]
                    flat = []
                    for (c0, w) in cols:
                        flat.append([c0, w])
                    for gi in range(NVG):
                        need = VGW
                        runs = []
                        while need > 0:
                            c0, w = flat[0]
                            take = min(w, need)
                            runs.append((c0, take))
                            need -= take
                            if take == w:
                                flat.pop(0)
                            else:
                                flat[0] = [c0 + take, w - take]
                        self._prep(wb["wv"], gi, win, runs, 8, VGW)
                    wg = self.W["w_gate"].ap[l]
                    for j in range(24):
                        self._prep(wb["wg"], j, wg, [(j * 128, 128)], 8, 128)
                    wu = self.W["w_up"].ap[l]
                    for j in range(8):
                        self._prep(wb["wup"], j, wu, [(j * 128, 128)], 10, 128)
                    wo = self.W["w_out"].ap[l]
                    for j in range(8):
                        self._prep(wb["wout"], j, wo, [(j * 128, 128)], 8, 128)

    def rope_tables(self):
        P, S = self.P, self.S
        PI = float(np.pi)
        with contextlib.ExitStack() as es:
            pi_ = self.sb("rp_pi", [128, S], I32, es)
            a = self.sb("rp_a", [128, S], F32, es)
            b = self.sb("rp_b", [128, S], F32, es)
            r = self.sb("rp_r", [128, S], F32, es)
            m = self.sb("rp_m", [128, S], F32, es)
            dve = P.dve
            P.dma(P.sp, pi_[:, :], (self.pos, self.pos.ap[0].partition_broadcast(128)))
            P.copy(dve, a[:, :], pi_[:, :])
            P.ts(dve, a[:, :], a[:, :], self.k_rope[:, 0:1], None, ALU.mult)
            P.ts(dve, b[:, :], a[:, :], 1.0 / (2 * PI), None, ALU.mult)
            ki = pi_
            P.copy(dve, ki[:, :], b[:, :])
            P.copy(dve, b[:, :], ki[:, :])
            C1 = 6.28125
            C2 = float(2 * np.pi - 6.28125)
            P.stt(r[:, :], b[:, :], -C1, a[:, :], ALU.mult, ALU.add)
            P.stt(r[:, :], b[:, :], -C2, r[:, :], ALU.mult, ALU.add)
            for (thr, op, adj) in ((PI, ALU.is_gt, -2 * PI), (-PI, ALU.is_lt, 2 * PI), (PI, ALU.is_gt, -2 * PI),
                                   (-PI, ALU.is_lt, 2 * PI)):
                P.ts(dve, m[:, :], r[:, :], thr, adj, op, ALU.mult)
                P.tt(dve, r[:, :], r[:, :], m[:, :], ALU.add)
            LIM = 3.1415925
            P.ts(dve, r[:, :], r[:, :], LIM, -LIM, ALU.min, ALU.max)
            P.activation(m[:, :], r[:, :], AF.Sin)
            P.ts(dve, m[:, :], m[:, :], self.k_rope[:, 1:2], None, ALU.mult)
            P.dma(P.pool, (self.CS, self.CS.ap[1]), m[:, :])
            P.ts(dve, b[:, :], r[:, :], -1.0, None, ALU.mult)
            P.tt(dve, b[:, :], b[:, :], r[:, :], ALU.max)
            P.ts(dve, b[:, :], b[:, :], -1.0, PI / 2, ALU.mult, ALU.add)
            P.activation(a[:, :], b[:, :], AF.Sin)
            P.dma(P.pool, (self.CS, self.CS.ap[0]), a[:, :])
            P.barrier()

    def rl_alloc(self, es):
        TT = self.TT
        R = {}
        R["hT"] = [self.sb(f"hT{c}", [128, TT], F32, es) for c in range(8)]
        R["xn"] = [self.sb(f"xn{c}", [128, TT], BF, es) for c in range(8)]
        R["h1"] = [self.sb(f"h1_{j}", [128, TT], BF, es) for j in range(NF)]
        R["sq"] = Ring([self.sb(f"sq{i}", [128, 512], BF, es) for i in range(2)])
        R["rt"] = [self.sb(f"rt{i}", [128, 512], F32, es) for i in range(self.NG)]
        R["wr"] = Ring([self.sb(f"wr{i}", [128, 8 * 128], BF, es) for i in range(6)])
        R["w2r"] = Ring([self.sb(f"w2r{i}", [128, NF * 128], BF, es) for i in range(2)])
        R["sa"] = Ring([self.sb(f"sa{i}", [128, 512], F32, es) for i in range(2)])
        R["ps"] = Ring(self.psum_banks(es, 7))
        return R

    def rmsnorm(self, R, g, out, out_f32=False):
        P = self.P
        for n in range(self.NG):
            ns = slice(n * 512, (n + 1) * 512)
            ps = R["ps"].next()
            for c in range(8):
                sq = R["sq"].next()
                P.activation(sq[:, :], R["hT"][c][:, ns], AF.Square)
                P.mm(ps[:, :], self.k_ones[:, :], sq[:, :], start=(c == 0), stop=(c == 7), sig=True)
            rt = R["rt"][n]
            P.activation(rt[:, :], ps[:, :], AF.Sqrt, scale=1.0 / D, bias=self.k_eps[:, 0:1])
            P.recip(rt[:, :], rt[:, :])
            for c in range(8):
                P.stt(out[c][:, ns], R["hT"][c][:, ns], g[:, c:c + 1], rt[:, :], ALU.mult, ALU.mult)

    def ffn(self, R, l, f):
        P = self.P
        wb = self.WB[l]
        W1, W3, W2 = wb[f"w1_{f}"], wb[f"w3_{f}"], wb[f"w2_{f}"]
        xn, h1, hT = R["xn"], R["h1"], R["hT"]
        for j in range(NF):
            wa = R["wr"].next()
            P.dma(P.sp, wa[:, :], (W1, W1.ap[j]))
            wc = R["wr"].next()
            P.dma(P.sp, wc[:, :], (W3, W3.ap[j]))
            for n in range(self.NG):
                ns = slice(n * 512, (n + 1) * 512)
                pa = R["ps"].next()
                pb = R["ps"].next()
                for c in range(8):
                    P.mm(pa[:, :], wa[:, c * 128:(c + 1) * 128], xn[c][:, ns], start=(c == 0), stop=(c == 7))
                for c in range(8):
                    P.mm(pb[:, :], wc[:, c * 128:(c + 1) * 128], xn[c][:, ns], start=(c == 0), stop=(c == 7))
                sa = R["sa"].next()
                P.activation(sa[:, :], pa[:, :], AF.Silu)
                P.tt(P.dve, h1[j][:, ns], sa[:, :], pb[:, :], ALU.mult)
        for dm in range(8):
            w2 = R["w2r"].next()
            P.dma(P.sp, w2[:, :], (W2, W2.ap[dm]))
            for n in range(self.NG):
                ns = slice(n * 512, (n + 1) * 512)
                py = R["ps"].next()
                for j in range(NF):
                    P.mm(py[:, :], w2[:, j * 128:(j + 1) * 128], h1[j][:, ns], start=(j == 0), stop=(j == NF - 1))
                P.stt(hT[dm][:, ns], py[:, :], 0.5, hT[dm][:, ns], ALU.mult, ALU.add)

    def phase1(self, l):
        P, S, TT = self.P, self.S, self.TT
        wb = self.WB[l]
        with contextlib.ExitStack() as es:
            R = self.rl_alloc(es)
            hT, xn = R["hT"], R["xn"]
            self.k_eps = self.sb("k_eps", [128, 1], F32, es)
            P.memset(P.dve, self.k_eps[:, :], EPS)
            ctab = self.sb("ctab", [128, TT], F32, es)
            stab = self.sb("stab", [128, TT], F32, es)
            qsb = Ring([self.sb(f"qsb{i}", [128, 512], BF, es) for i in range(3)])
            t1 = Ring([self.sb(f"t1_{i}", [128, 512], F32, es) for i in range(2)])
            t2 = Ring([self.sb(f"t2_{i}", [128, 512], F32, es) for i in range(2)])
            stg = Ring([self.sb(f"stg{i}", [128, 512], BF, es) for i in range(3)])
            wvr = Ring([self.sb(f"wvr{i}", [128, 8 * VGW], BF, es) for i in range(2)])
            vst = [self.sb(f"vst{i}", [128, NV], BF, es) for i in range(TT // 128)]
            xin = Ring([self.sb(f"xin{i}", [128, D], F32, es) for i in range(2)]) if l == 0 else None
            for tt in range(S // TT):
                t0 = tt * TT
                tsl = slice(t0, t0 + TT)
                if l == 0:
                    for ts in range(TT // 128):
                        xi = xin.next()
                        P.dma(P.sp, xi[:, :], (self.x, self.x.ap[t0 + ts * 128:t0 + (ts + 1) * 128, :]))
                        for half in range(2):
                            ps = R["ps"].next()
                            for q in range(4):
                                c = half * 4 + q
                                P.transpose(ps[:, q * 128:(q + 1) * 128], xi[:, c * 128:(c + 1) * 128],
                                            self.k_identf[:, :], sig=(q == 3))
                            for q in range(4):
                                c = half * 4 + q
                                P.copy(P.dve if q % 2 else P.act, hT[c][:, ts * 128:(ts + 1) * 128],
                                       ps[:, q * 128:(q + 1) * 128])
                else:
                    src = self.HT2[l - 1]
                    for c in range(8):
                        P.dma(P.sp, hT[c][:, :], (src, src.ap[c, :, tsl]))
                P.dma(P.sp, ctab[:, :], (self.CS, self.CS.ap[0, :, tsl]))
                P.dma(P.sp, stab[:, :], (self.CS, self.CS.ap[1, :, tsl]))
                import os
                stop = os.environ.get('K_P1_STOP', '')
                if stop == 'load':
                    continue
                self.rmsnorm(R, self.g[("norm_ffn1", l)], xn)
                if stop == 'norm':
                    continue
                self.ffn(R, l, 1)
                if stop == 'ffn':
                    continue
                self.rmsnorm(R, self.g[("norm_mix", l)], xn)
                QT = self.QT[l]
                for bi, blk in enumerate(FM):
                    w = R["wr"].next()
                    P.dma(P.sp, w[:, :], (wb["win"], wb["win"].ap[bi]))
                    for n in range(self.NG):
                        ns = slice(n * 512, (n + 1) * 512)
                        dsl = slice(t0 + n * 512, t0 + (n + 1) * 512)
                        pm = R["ps"].next()
                        for c in range(8):
                            P.mm(pm[:, :], w[:, c * 128:(c + 1) * 128], xn[c][:, ns], start=(c == 0), stop=(c == 7))
                        qs = qsb.next()
                        P.activation(qs[:, :], pm[:, :], AF.Identity)
                        if blk[3] is not None:
                            P.dma(self.qst, (QT, QT.ap[blk[3], :, dsl]), qs[:, :])
                        if blk[2] is not None:
                            psw = R["ps"].next()
                            P.mm(psw[:, :], self.k_perm[:, :], qs[:, :], start=True, stop=True)
                            a = t1.next()
                            b = t2.next()
                            P.tt(P.dve, a[:, :], pm[:, :], ctab[:, ns], ALU.mult)
                            P.tt(P.dve, b[:, :], psw[:, :], stab[:, ns], ALU.mult)
                            st = stg.next()
                            P.tt(P.pool, st[:, :], a[:, :], b[:, :], ALU.add)
                            P.dma(self.qst, (QT, QT.ap[blk[2], :, dsl]), st[:, :])
                if stop == 'fm':
                    continue
                VT = self.VT[l]
                for gi in range(NVG):
                    wv = wvr.next()
                    P.dma(P.sp, wv[:, :], (wb["wv"], wb["wv"].ap[gi]))
                    for ts in range(TT // 128):
                        pv = R["ps"].next()
                        for c in range(8):
                            P.mm(pv[:, 0:VGW], xn[c][:, ts * 128:(ts + 1) * 128], wv[:, c * VGW:(c + 1) * VGW],
                                 start=(c == 0), stop=(c == 7))
                        P.activation(vst[ts][:, gi * VGW:(gi + 1) * VGW], pv[:, 0:VGW], AF.Identity)
                for ts in range(TT // 128):
                    P.dma(self.qst, (VT, VT.ap[t0 + ts * 128:t0 + (ts + 1) * 128, :]), vst[ts][:, :])
                if stop == 'v':
                    continue
                GT = self.GT[l]
                for bi in range(24):
                    w = R["wr"].next()
                    P.dma(P.sp, w[:, :], (wb["wg"], wb["wg"].ap[bi]))
                    for n in range(self.NG):
                        ns = slice(n * 512, (n + 1) * 512)
                        dsl = slice(t0 + n * 512, t0 + (n + 1) * 512)
                        pg = R["ps"].next()
                        for c in range(8):
                            P.mm(pg[:, :], w[:, c * 128:(c + 1) * 128], xn[c][:, ns], start=(c == 0), stop=(c == 7))
                        st = stg.next()
                        P.activation(st[:, :], pg[:, :], AF.Sigmoid)
                        P.dma(self.qst, (GT, GT.ap[bi, :, dsl]), st[:, :])
                if stop == 'gate':
                    continue
                HT = self.HT[l]
                for c in range(8):
                    P.dma(self.qst, (HT, HT.ap[c, :, tsl]), hT[c][:, :])
            P.barrier()

    def phase3(self, l):
        P, S, TT = self.P, self.S, self.TT
        wb = self.WB[l]
        last = (l == self.L - 1)
        with contextlib.ExitStack() as es:
            R = self.rl_alloc(es)
            hT, xn = R["hT"], R["xn"]
            self.k_eps = self.sb("k_eps3", [128, 1], F32, es)
            P.memset(P.dve, self.k_eps[:, :], EPS)
            oT = [self.sb(f"oT{k}", [128, TT], BF, es) for k in range(10)]
            otok = [self.sb(f"otok{i}", [128, 896], BF, es) for i in range(TT // 128)]
            psT = Tn(P, "psT", [128, 1024], BF, "psum", es)
            gr = Ring([self.sb(f"gr{i}", [128, TT], BF, es) for i in range(6)])
            y = [self.sb(f"y{k}", [128, TT], BF, es) for k in range(8)]
            wur = Ring([self.sb(f"wur{i}", [128, 10 * 128], BF, es) for i in range(2)])
            t1 = Ring([self.sb(f"t31_{i}", [128, 512], F32, es) for i in range(2)])
            t2 = Ring([self.sb(f"t32_{i}", [128, 512], F32, es) for i in range(2)])
            if last:
                xof = [self.sb(f"xof{c}", [128, TT], F32, es) for c in range(8)]
                ost = Ring([self.sb(f"ost{i}", [128, D], F32, es) for i in range(2)])
            GT, OTOK, OTC, HT = self.GT[l], self.OTOK[l], self.OTC[l], self.HT[l]
            for tt in range(S // TT):
                t0 = tt * TT
                tsl = slice(t0, t0 + TT)
                for c in range(8):
                    P.dma(P.sp, hT[c][:, :], (HT, HT.ap[c, :, tsl]))
                for hp in range(3):
                    P.dma(P.sp, oT[7 + hp][:, :], (OTC, OTC.ap[hp, :, tsl]))
                for ts in range(TT // 128):
                    P.dma(P.sp, otok[ts][:, :], (OTOK, OTOK.ap[t0 + ts * 128:t0 + (ts + 1) * 128, :]))
                for kb in range(7):
                    for n in range(self.NG):
                        for q in range(4):
                            ts = n * 4 + q
                            P.transpose(psT[:, q * 128:(q + 1) * 128], otok[ts][:, kb * 128:(kb + 1) * 128],
                                        self.k_ident[:, :], sig=(q == 3))
                        P.copy(P.dve if kb % 2 else P.act, oT[kb][:, n * 512:(n + 1) * 512], psT[:, 0:512])
                for dm in range(8):
                    w = wur.next()
                    P.dma(P.sp, w[:, :], (wb["wup"], wb["wup"].ap[dm]))
                    gs = []
                    for m in range(3):
                        gt = gr.next()
                        P.dma(P.sp, gt[:, :], (GT, GT.ap[m * 8 + dm, :, tsl]))
                        gs.append(gt)
                    for n in range(self.NG):
                        ns = slice(n * 512, (n + 1) * 512)
                        pp = []
                        for (k0, k1) in ((0, 3), (3, 7), (7, 10)):
                            p_ = R["ps"].next()
                            for k in range(k0, k1):
                                P.mm(p_[:, :], w[:, k * 128:(k + 1) * 128], oT[k][:, ns], start=(k == k0),
                                     stop=(k == k1 - 1))
                            pp.append(p_)
                        a = t1.next()
                        b = t2.next()
                        P.tt(P.dve, a[:, :], pp[0][:, :], gs[0][:, ns], ALU.mult)
                        P.tt(P.dve, b[:, :], pp[1][:, :], gs[1][:, ns], ALU.mult)
                        P.tt(P.pool, a[:, :], a[:, :], b[:, :], ALU.add)
                        b2 = t2.next()
                        P.tt(P.dve, b2[:, :], pp[2][:, :], gs[2][:, ns], ALU.mult)
                        P.tt(P.pool, y[dm][:, ns], a[:, :], b2[:, :], ALU.add)
                for dm2 in range(8):
                    w = R["wr"].next()
                    P.dma(P.sp, w[:, :], (wb["wout"], wb["wout"].ap[dm2]))
                    for n in range(self.NG):
                        ns = slice(n * 512, (n + 1) * 512)
                        p_ = R["ps"].next()
                        for k in range(8):
                            P.mm(p_[:, :], w[:, k * 128:(k + 1) * 128], y[k][:, ns], start=(k == 0), stop=(k == 7))
                        P.tt(P.dve, hT[dm2][:, ns], p_[:, :], hT[dm2][:, ns], ALU.add)
                self.rmsnorm(R, self.g[("norm_ffn2", l)], xn)
                self.ffn(R, l, 2)
                if not last:
                    H2 = self.HT2[l]
                    for c in range(8):
                        P.dma(self.qst, (H2, H2.ap[c, :, tsl]), hT[c][:, :])
                else:
                    self.rmsnorm(R, self.g[("final", 0)], xof)
                    for ts in range(TT // 128):
                        o_ = ost.next()
                        for half in range(2):
                            ps = R["ps"].next()
                            for q in range(4):
                                c = half * 4 + q
                                P.transpose(ps[:, q * 128:(q + 1) * 128], xof[c][:, ts * 128:(ts + 1) * 128],
                                            self.k_identf[:, :], sig=(q == 3))
                            P.copy(P.dve if half else P.act, o_[:, half * 512:(half + 1) * 512], ps[:, :])
                        P.dma(self.qst, (self.out, self.out.ap[t0 + ts * 128:t0 + (ts + 1) * 128, :]), o_[:, :])
            P.barrier()

    def attn(self, pairs, O, vw, X, nsub=4, outs=None, starts=(0,)):
        P = self.P
        n = len(pairs)
        sps = [None] * n
        pbs = [None] * n

        def stage_s(i):
            p = pairs[i]
            s_ = X["S"].next()
            sps[i] = s_
            nb = len(p["bias"])
            P.mm(s_[:, :], p["kT"], p["q"], start=True, stop=(nb == 0))
            for bi, (lh, rh) in enumerate(p["bias"]):
                P.mm(s_[:, :], lh, rh, start=False, stop=(bi == nb - 1))

        def stage_pv(i):
            t = X["P"].next()
            pbs[i] = t
            P.activation(t[:, :], sps[i][:, :], AF.Exp, scale=0.125)
            if pairs[i].get("mask") is not None:
                P.tt(P.dve, t[:, :], t[:, :], pairs[i]["mask"], ALU.mult)
            for sub in range(nsub):
                o_ = outs[sub] if outs is not None else O[:, sub * vw:(sub + 1) * vw]
                P.mm(o_, t[:, sub * 128:(sub + 1) * 128], pairs[i]["v"],
                     start=(i == 0 and sub in starts), stop=(i == n - 1), sig=(sub == nsub - 1))

        LA = 2
        for i in range(min(LA, n)):
            stage_s(i)
        for i in range(n):
            if i + LA < n:
                stage_s(i + LA)
            stage_pv(i)

    def load_v_tm(self, dst, VT, c0, ncol, nh, wcol):
        P = self.P
        NT = self.NT
        src = VT.ap[:, c0:c0 + ncol].rearrange("(s p) (h d) -> p s h d", p=128, h=nh)
        first = True
        for s0 in range(0, NT, 16):
            s1 = min(NT, s0 + 16)
            for h in range(nh):
                P.dma(P.sp, dst.v(dst.h[:, s0:s1, h, 0:64]), (VT, src[:, s0:s1, h, :]), part=(not first))
                first = False

    def phase2(self, l):
        import os
        sel = os.environ.get("K_P2", "abc")
        for k, fn in (("a", self.mixer_a), ("b", self.mixer_b), ("c", self.mixer_c)):
            if k in sel:
                self.P.scope_begin()
                fn(l)
                self.P.scope_end()

    def p2_pre(self, l):
        pass

    def mixer_a(self, l):
        import os
        skip = os.environ.get('K_A_SKIP', '')
        P, S, NT = self.P, self.S, self.NT
        QT, VT, OTOK = self.QT[l], self.VT[l], self.OTOK[l]
        with contextlib.ExitStack() as es:
            am = self.sb("amask", [128, 33, 512], BF, es)
            for t0 in range(0, 33, 11):
                P.dma(P.sp, am.v(am.h[:, t0:t0 + 11, :]),
                      (self.C["c_amask"], self.C["c_amask"].ap[t0:t0 + 11].rearrange("t p n -> p t n")), part=(t0 > 0))
            for t0 in range(0, 33, 11):
                P.ts(P.dve, am.v(am.h[:, t0:t0 + 11, :]), am.v(am.h[:, t0:t0 + 11, :]), NEG / 2, None, ALU.is_gt)
            qTb = [self.sb(f"a_q{g}", [128, S], BF, es) for g in range(3)]
            kTb = [self.sb(f"a_k{g}", [128, S], BF, es) for g in range(3)]
            Vb = [self.sb(f"a_v{g}", [128, NT, 2, 65], BF, es) for g in range(3)]
            for g in range(3):
                if 'memset' in skip:
                    continue
                P.memset(P.pool, Vb[g].v(Vb[g].h[:, :, :, 64:65]), 1.0)
            X = {"S": Ring(self.psum_banks(es, 4)), "P": Ring([self.sb(f"a_p{i}", [128, 512], BF, es) for i in range(4)])}
            Ob = Ring([Tn(P, f"a_o{i}", [128, 512], F32, "psum", es) for i in range(2)])
            rl = Ring([self.sb(f"a_rl{i}", [128, 4, 1], F32, es) for i in range(2)])
            ost = Ring([self.sb(f"a_ost{i}", [128, 4, 128], BF, es) for i in range(2)])
            base = [0, 5, 13]
            for sp in range(3):
                for g in range(3):
                    P.dma(P.sp, qTb[g][:, :], (QT, QT.ap[g * 3 + sp]))
                    P.dma(P.sp, kTb[g][:, :], (QT, QT.ap[9 + g * 3 + sp]))
                    self.load_v_tm(Vb[g], VT, VO_A + g * 384 + sp * 128, 128, 2, 65)
                for tt in range(S // 512):
                    o_st = ost.next()
                    for h in range(2):
                        hs = slice(64 * h, 64 * h + 64)
                        pairs = []
                        for g in range(3):
                            dil = A_DIL[g]
                            for sg in range(max(0, 4 * tt - dil), 4 * tt + 4):
                                dl = 4 * tt - sg
                                pairs.append({
                                    "kT": kTb[g][hs, sg * 128:(sg + 1) * 128],
                                    "q": qTb[g][hs, tt * 512:(tt + 1) * 512],
                                    "bias": [],
                                    "mask": am.v(am.h[:, base[g] + dl + 3, :]),
                                    "v": Vb[g].v(Vb[g].h[:, sg, h, :]),
                                })
                        O = Ob.next()
                        if 'attn' not in skip:
                            self.attn(pairs, O, 65, X)
                        if 'epi' in skip:
                            continue
                        O3 = O.h[:, 0:260].rearrange("p (i c) -> p i c", c=65)
                        r_ = rl.next()
                        P.recip(r_[:, :, :], O.v(O3[:, :, 64:65]))
                        P.tt(P.dve, o_st.v(o_st.h[:, :, 64 * h:64 * h + 64]), O.v(O3[:, :, 0:64]),
                             r_.v(r_.h[:, :, :].broadcast_to([128, 4, 64])), ALU.mult)
                    if 'store' in skip:
                        continue
                    P.dma(self.qst, (OTOK, OTOK.ap[tt * 512:(tt + 1) * 512, sp * 128:(sp + 1) * 128]
                                     .rearrange("(i p) c -> p i c", p=128)), o_st[:, :, :])
            P.barrier()

    def mixer_c(self, l):
        P, S, NT = self.P, self.S, self.NT
        QT, VT, OTC = self.QT[l], self.VT[l], self.OTC[l]
        with contextlib.ExitStack() as es:
            cm = self.sb("cmask", [128, 4, 512], BF, es)
            P.dma(P.sp, cm[:, :, :], (self.C["c_cmask"], self.C["c_cmask"].ap.rearrange("t p n -> p t n")))
            mU = self.sb("c_mU", [128, 128], BF, es)
            mO = self.sb("c_mO", [128, 128], BF, es)
            P.ts(P.dve, mU[:, :], self.k_U[:, :], -8.0, None, ALU.mult)
            P.memset(P.dve, mO[:, :], -8.0)
            qT = self.sb("c_q", [128, S], BF, es)
            kT = self.sb("c_k", [128, S], BF, es)
            Vb = self.sb("c_v", [128, NT, 2, 64], BF, es)
            Zr = Ring(self.psum_banks(es, 6))
            Ob = Ring([Tn(P, f"c_o{i}", [128, 512], F32, "psum", es) for i in range(1)])
            Er = Ring([self.sb(f"c_e{i}", [128, 512], F32, es) for i in range(4)])
            Sr = Ring([self.sb(f"c_s{i}", [128, 512], BF, es) for i in range(6)])
            Ar = Ring([self.sb(f"c_a{i}", [128, 512], BF, es) for i in range(6)])
            ssum = [self.sb(f"c_ss{h}", [128, 512], F32, es) for h in range(2)]
            ssbf = [Ring([self.sb(f"c_sb{h}_{i}", [128, 512], BF, es) for i in range(2)]) for h in range(2)]
            ostg = Ring([self.sb(f"c_og{i}", [128, 512], BF, es) for i in range(2)])
            for hp in range(3):
                P.dma(P.sp, qT[:, :], (QT, QT.ap[30 + hp]))
                P.dma(P.sp, kT[:, :], (QT, QT.ap[33 + hp]))
                self.load_v_tm(Vb, VT, VO_C + hp * 128, 128, 2, 64)
                for tt in range(S // 512):
                    O = Ob.next()
                    sgs = list(range(4 * tt + 3, -1, -1))
                    n = len(sgs)
                    st = {}

                    def s0(i):
                        sg = sgs[i]
                        zs, es_ = [], []
                        for h in range(2):
                            hs = slice(64 * h, 64 * h + 64)
                            z = Zr.next()
                            P.mm(z[:, :], kT[hs, sg * 128:(sg + 1) * 128], qT[hs, tt * 512:(tt + 1) * 512], start=True,
                                 stop=False, sig=True)
                            zs.append(z)
                        for h in range(2):
                            e = Er.next()
                            P.activation(e[:, :], zs[h][:, :], AF.Exp, scale=0.125)
                            es_.append(e)
                        if sg >= 4 * tt:
                            for h in range(2):
                                P.tt(P.dve, es_[h][:, :], es_[h][:, :], cm.v(cm.h[:, 4 * tt - sg + 3, :]), ALU.mult)
                        for h in range(2):
                            sp_ = Sr.next()
                            P.activation(sp_[:, :], es_[h][:, :], AF.Ln, bias=self.k_one[:, 0:1])
                            st[(i, h)] = [zs[h], sp_, None]

                    def s1(i):
                        sg = sgs[i]
                        for h in range(2):
                            z, sp_, _ = st[(i, h)]
                            P.mm(z[:, :], mU[:, :], sp_[:, :], start=False, stop=(i == 0), sig=(i == 0))
                            if i > 0:
                                P.mm(z[:, :], mO[:, :], st[("sb", h)][:, :], start=False, stop=True, sig=True)
                        for h in range(2):
                            sp_ = st[(i, h)][1]
                            if i == 0:
                                P.copy(P.pool, ssum[h][:, :], sp_[:, :])
                            else:
                                P.tt(P.pool, ssum[h][:, :], ssum[h][:, :], sp_[:, :], ALU.add)
                        if i + 1 < n:
                            for h in range(2):
                                sb_ = ssbf[h].next()
                                P.copy(P.dve, sb_[:, :], ssum[h][:, :])
                                st[("sb", h)] = sb_
                        for h in range(2):
                            a_ = Ar.next()
                            P.activation(a_[:, :], st[(i, h)][0][:, :], AF.Exp, scale=0.125)
                            st[(i, h)][2] = a_
                        if sg >= 4 * tt:
                            for h in range(2):
                                a_ = st[(i, h)][2]
                                P.tt(P.dve, a_[:, :], a_[:, :], cm.v(cm.h[:, 4 * tt - sg + 3, :]), ALU.mult)

                    def s2(i):
                        sg = sgs[i]
                        for h in range(2):
                            a_ = st[(i, h)][2]
                            P.mm(O[64 * h:64 * h + 64, :], Vb.v(Vb.h[:, sg, h, :]), a_[:, :], start=(i == 0), stop=(i == n - 1))

                    stages = [s0, s1, s2]
                    for it in range(n + 2):
                        for k, fn in enumerate(stages):
                            i = it - k
                            if 0 <= i < n:
                                fn(i)
                    og = ostg.next()
                    P.activation(og[:, :], O[:, :], AF.Identity)
                    P.dma(self.qst, (OTC, OTC.ap[hp, :, tt * 512:(tt + 1) * 512]), og[:, :])
            P.barrier()

    def mixer_b(self, l):
        import os
        skip = os.environ.get('K_B_SKIP', '')
        P, S, NT = self.P, self.S, self.NT
        QT, VT, OTOK = self.QT[l], self.VT[l], self.OTOK[l]
        ncmp = S // 16 - 1
        GC = 0.7978845608028654
        with contextlib.ExitStack() as es:
            kslc = self.sb("b_kslc", [128, S], BF, es)
            kwin = self.sb("b_kwin", [128, S], BF, es)
            P.dma(P.sp, kslc[:, :], (QT, QT.ap[28]))
            P.dma(P.sp, kwin[:, :], (QT, QT.ap[29]))
            Vs = self.sb("b_vs", [128, NT, 2, 65], BF, es)
            Vw = self.sb("b_vw", [128, NT, 2, 65], BF, es)
            for (vb, c0) in ((Vs, VO_SLC), (Vw, VO_WIN)):
                P.memset(P.pool, vb.v(vb.h[:, :, :, 64:65]), 1.0)
                self.load_v_tm(vb, VT, c0, 128, 2, 65)
            kcT = self.sb("b_kcT", [128, 256], BF, es)
            vca = [self.sb(f"b_vca{g}", [128, 2, 129], BF, es) for g in range(2)]
            wm = self.sb("b_wm", [128, 8, 512], BF, es)
            P.dma(P.sp, wm[:, :, :], (self.C["c_wmask"], self.C["c_wmask"].ap.rearrange("t p n -> p t n")))
            sm = self.sb("b_sm", [128, 4, 512], BF, es)
            P.dma(P.sp, sm[:, :, :], (self.C["c_smask"], self.C["c_smask"].ap.rearrange("t p n -> p t n")))
            P.ts(P.dve, wm[:, :, :], wm[:, :, :], NEG / 2, None, ALU.is_gt)
            P.ts(P.dve, sm[:, :, :], sm[:, :, :], NEG / 2, None, ALU.is_gt)
            esel = self.sb("b_esel", [128, NT, 128], BF, es)
            for hh in range(2):
                P.dma(P.sp, esel.v(esel.h[64 * hh:64 * hh + 64, :, :]), (self.C["c_esel"], self.C["c_esel"].ap), part=(hh > 0))
            vnf = self.sb("b_vnf", [128, NT, 64], F32, es)
            add = self.sb("b_add", [128, NT, 64], F32, es)
            P.dma(P.sp, vnf[:, :, :], (self.C["c_vnf"], self.C["c_vnf"].ap.rearrange("(s p) j -> p s j", p=128)))
            P.dma(P.sp, add[:, :, :], (self.C["c_add"], self.C["c_add"].ap.rearrange("(s p) j -> p s j", p=128)))
            Sr = Ring(self.psum_banks(es, 4))
            Ob = [Tn(P, f"b_o{i}", [128, 512], F32, "psum", es) for i in range(2)]
            psT = Tn(P, "b_psT", [128, 1024], BF, "psum", es)
            X = {"S": Sr, "P": Ring([self.sb(f"b_p{i}", [128, 512], BF, es) for i in range(4)])}

            with contextlib.ExitStack() as es2:
                for which, (qi, pe_n, w1_n, w2_n) in enumerate(((26, "cmp_pe_k", "cmp_w1_k", "cmp_w2_k"),
                                                              (27, "cmp_pe_v", "cmp_w1_v", "cmp_w2_v"))):
                    if 'pro' in skip:
                        continue
                    src = self.sb(f"b_cin{which}", [128, S], BF, es2)
                    P.dma(P.sp, src[:, :], (QT, QT.ap[qi]))
                    w1T = self.sb(f"b_w1T{which}", [128, 32, 128], BF, es2)
                    peT = self.sb(f"b_peT{which}", [128, 32], F32, es2)
                    w2 = self.sb(f"b_w2{which}", [128, 128], BF, es2)
                    w1src = self.W[w1_n].ap[l].rearrange("(p d) h -> d p h", d=64)
                    pesrc = self.W[pe_n].ap[l].rearrange("p d -> d p")
                    for hh in range(2):
                        P.dma(P.pool, w1T.v(w1T.h[64 * hh:64 * hh + 64, :, :]), (self.x, w1src), part=(hh > 0))
                        P.dma(P.sp, peT.v(peT.h[64 * hh:64 * hh + 64, :]), (self.x, pesrc), part=(hh > 0),
                              allow_slow_non_contiguous=True)
                        P.dma(P.pool, w2.v(w2.h[:, 64 * hh:64 * hh + 64]), (self.x, self.W[w2_n].ap[l]), part=(hh > 0))
                    kpe = self.sb(f"b_kpe{which}", [128, 32, 256], BF, es2)
                    P.memset(P.pool, kpe[:, :, :], 0.0)
                    win_ap = bass.AP(tensor=src.h, offset=0, ap=[[S, 128], [1, 32], [16, ncmp]])
                    P.tt(P.dve, kpe.v(kpe.h[:, :, 0:ncmp]), src.v(win_ap),
                         peT.v(peT.h[:, :].unsqueeze(2).broadcast_to([128, 32, ncmp])), ALU.add)
                    xs = self.sb(f"b_xs{which}", [128, 256], F32, es2)
                    x2 = self.sb(f"b_x2{which}", [128, 256], F32, es2)
                    hg = self.sb(f"b_hg{which}", [128, 256], BF, es2)
                    for g in range(2):
                        hs = slice(64 * g, 64 * g + 64)
                        ph = Sr.next()
                        for p_ in range(32):
                            P.mm(ph[:, 0:256], w1T.v(w1T.h[hs, p_, :]), kpe.v(kpe.h[hs, p_, :]), start=(p_ == 0),
                                 stop=(p_ == 31))
                        P.activation(xs[:, :], ph[:, 0:256], AF.Identity)
                        P.tt(P.dve, x2[:, :], xs[:, :], xs[:, :], ALU.mult)
                        P.ts(P.dve, x2[:, :], x2[:, :], 0.044715, 1.0, ALU.mult, ALU.add)
                        P.tt(P.dve, x2[:, :], x2[:, :], xs[:, :], ALU.mult)
                        P.activation(x2[:, :], x2[:, :], AF.Sigmoid, scale=2.0 * GC)
                        P.tt(P.dve, hg[:, :], xs[:, :], x2[:, :], ALU.mult)
                        if which == 0:
                            pk = Sr.next()
                            P.mm(pk[:, 0:256], w2[:, :], hg[:, :], start=True, stop=True)
                            P.copy(P.act, kcT[hs, :], pk[hs, 0:256])
                        else:
                            for ct in range(2):
                                pv = Sr.next()
                                P.mm(pv[:, 0:64], hg[:, ct * 128:(ct + 1) * 128], w2[:, 0:64], start=True, stop=True)
                                P.copy(P.act, vca[g].v(vca[g].h[:, ct, 0:64]), pv[:, 0:64])
                for g in range(2):
                    P.memset(P.pool, vca[g].v(vca[g].h[:, :, 64:65]), 1.0)
                    P.dma(P.sp, vca[g].v(vca[g].h[:, :, 65:129]),
                          (self.C["c_wsel"], self.C["c_wsel"].ap.rearrange("(ct p) j -> p ct j", p=128)))
                P.barrier()

            qu = [Ring([self.sb(f"b_qu{r}_{i}", [128, 512], BF, es) for i in range(2)]) for r in range(4)]
            qr = [Ring([self.sb(f"b_qr{r}_{i}", [128, 512], BF, es) for i in range(2)]) for r in range(4)]
            glr = Ring([self.sb(f"b_gl{i}", [128, 4, 24], BF, es) for i in range(2)])
            gsr = Ring([self.sb(f"b_gs{i}", [128, 4, 24], F32, es) for i in range(2)])
            cbr = Ring([self.sb(f"b_cb{i}", [128, 2, 512], BF, es) for i in range(2)])
            sacc = [self.sb(f"b_sacc{g}", [128, 4, 64], F32, es) for g in range(2)]
            oB = self.sb("b_oB", [128, 4, 512], F32, es)
            ob16 = Ring([self.sb(f"b_ob16_{i}", [128, 4, 512], BF, es) for i in range(2)])
            BT = self.sb("b_BT", [128, 512], BF, es)
            rlr = Ring([self.sb(f"b_rl{i}", [128, 4, 1], F32, es) for i in range(4)])
            tmpr = Ring([self.sb(f"b_tmp{i}", [128, 4, 64], F32, es) for i in range(3)])
            sc = self.sb("b_sc", [128, 64], F32, es)
            wk = self.sb("b_wk", [128, 64], F32, es)
            m8 = self.sb("b_m8", [128, 8], F32, es)
            m8b = self.sb("b_m8b", [128, 8], F32, es)
            btr = Ring([self.sb(f"b_bt{i}", [128, 128], BF, es) for i in range(4)])
            for tt in range(S // 512):
                if 'main' in skip:
                    continue
                tsl = slice(tt * 512, (tt + 1) * 512)
                qut = []
                qrt = []
                for r in range(4):
                    a = qu[r].next()
                    P.dma(P.sp, a[:, :], (QT, QT.ap[18 + r, :, tsl]))
                    qut.append(a)
                    b = qr[r].next()
                    P.dma(P.sp, b[:, :], (QT, QT.ap[22 + r, :, tsl]))
                    qrt.append(b)
                gl = glr.next()
                P.dma(P.sp, gl[:, :, :], (VT, VT.ap[tsl, VO_GATE:VO_GATE + 24].rearrange("(i p) c -> p i c", p=128)))
                gs = gsr.next()
                P.activation(gs[:, :, :], gl[:, :, :], AF.Sigmoid)
                cb = cbr.next()
                P.dma(P.sp, cb[:, :, :], (self.C["c_cmpb"], self.C["c_cmpb"].ap[:, tsl].rearrange("(ct p) n -> p ct n", p=128)))
                P.ts(P.dve, cb[:, :, :], cb[:, :, :], NEG / 2, None, ALU.is_gt)

                def epilogue(O3, nsub, sub0, h, br, accumulate):
                    r_ = rlr.next()
                    rv = r_.v(r_.h[:, 0:nsub, :])
                    P.ts(P.dve, rv, O3[2], 1e-30, None, ALU.max)
                    P.recip(rv, rv)
                    rg = rlr.next()
                    rgv = rg.v(rg.h[:, 0:nsub, :])
                    P.tt(P.dve, rgv, rv, gs.v(gs.h[:, sub0:sub0 + nsub, 3 * h + br:3 * h + br + 1]), ALU.mult)
                    dst = oB.v(oB.h[:, sub0:sub0 + nsub, 64 * h:64 * h + 64])
                    bc = rg.v(rg.h[:, 0:nsub, :].broadcast_to([128, nsub, 64]))
                    if not accumulate:
                        P.tt(P.dve, dst, O3[0], bc, ALU.mult)
                    else:
                        t_ = tmpr.next()
                        tv = t_.v(t_.h[:, 0:nsub, :])
                        P.tt(P.dve, tv, O3[0], bc, ALU.mult)
                        P.tt(P.dve, dst, dst, tv, ALU.add)
                    return r_

                for h in range(8):
                    if 'cmp' in skip:
                        continue
                    g, r = h // 4, h % 4
                    hs = slice(64 * g, 64 * g + 64)
                    pairs = [{"kT": kcT[hs, ct * 128:(ct + 1) * 128], "q": qut[r][hs, :],
                              "bias": [], "mask": cb.v(cb.h[:, ct, :]),
                              "v": vca[g].v(vca[g].h[:, ct, :])} for ct in range(2)]
                    outs = [Ob[i // 2][:, (i % 2) * 129:(i % 2) * 129 + 129] for i in range(4)]
                    self.attn(pairs, None, 129, X, nsub=4, outs=outs, starts=(0, 2))
                    for bk in range(2):
                        O = Ob[bk]
                        O3 = O.h[:, 0:258].rearrange("p (i c) -> p i c", c=129)
                        views = (O.v(O3[:, :, 0:64]), O.v(O3[:, :, 65:129]), O.v(O3[:, :, 64:65]))
                        r_ = epilogue(views, 2, 2 * bk, h, 0, False)
                        bc = r_.v(r_.h[:, 0:2, :].broadcast_to([128, 2, 64]))
                        sd = sacc[g].v(sacc[g].h[:, 2 * bk:2 * bk + 2, :])
                        if r == 0:
                            P.tt(P.dve, sd, views[1], bc, ALU.mult)
                        else:
                            t_ = tmpr.next()
                            tv = t_.v(t_.h[:, 0:2, :])
                            P.tt(P.dve, tv, views[1], bc, ALU.mult)
                            P.tt(P.pool, sd, sd, tv, ALU.add)
                if 'topk' not in skip:
                    for i in range(4):
                        tix = 4 * tt + i
                        bt = btr.next()
                        for g in range(2):
                            P.tt(P.dve, sc[:, :], sacc[g].v(sacc[g].h[:, i, :]), vnf.v(vnf.h[:, tix, :]), ALU.mult)
                            P.tt(P.dve, sc[:, :], sc[:, :], add.v(add.h[:, tix, :]), ALU.add)
                            P.op(P.dve, lambda: self.nc.vector.max(out=m8.h[:, :], in_=sc.h[:, :]), [sc[:, :]], [m8[:, :]])
                            P.op(P.dve, lambda: self.nc.vector.match_replace(out=wk.h[:, :], in_to_replace=m8.h[:, :],
                                                                              in_values=sc.h[:, :], imm_value=-1e30),
                                 [sc[:, :], m8[:, :]], [wk[:, :]])
                            P.op(P.dve, lambda: self.nc.vector.max(out=m8b.h[:, :], in_=wk.h[:, :]), [wk[:, :]], [m8b[:, :]])
                            P.ts(P.dve, bt[:, 64 * g:64 * g + 64], sc[:, :], m8b[:, 7:8], NEG, ALU.is_lt, ALU.mult)
                        P.transpose(psT[:, i * 128:(i + 1) * 128], bt[:, :], self.k_ident[:, :], sig=True)
                    P.copy(P.act, BT[:, :], psT[:, 0:512])
                for h in range(8):
                    g, r = h // 4, h % 4
                    hs = slice(64 * g, 64 * g + 64)
                    for br in (1, 2):
                        if ('sel' in skip and br == 1) or ('win' in skip and br == 2):
                            continue
                        pairs = []
                        if br == 1:
                            for sg in range(0, 4 * tt + 4):
                                bias = [(esel.v(esel.h[hs, sg, :]), BT[hs, :])]
                                mk = sm.v(sm.h[:, 4 * tt - sg + 3, :]) if sg >= 4 * tt else None
                                pairs.append({"kT": kslc[hs, sg * 128:(sg + 1) * 128], "q": qrt[r][hs, :], "bias": bias,
                                              "mask": mk, "v": Vs.v(Vs.h[:, sg, g, :])})
                        else:
                            for sg in range(max(0, 4 * tt - 4), 4 * tt + 4):
                                pairs.append({"kT": kwin[hs, sg * 128:(sg + 1) * 128], "q": qrt[r][hs, :], "bias": [],
                                              "mask": wm.v(wm.h[:, 4 * tt - sg + 3, :]), "v": Vw.v(Vw.h[:, sg, g, :])})
                        O = Ob[(2 * h + br) % 2]
                        self.attn(pairs, O, 65, X)
                        O3 = O.h[:, 0:260].rearrange("p (i c) -> p i c", c=65)
                        views = (O.v(O3[:, :, 0:64]), None, O.v(O3[:, :, 64:65]))
                        epilogue(views, 4, 0, h, br, True)
                o16 = ob16.next()
                P.activation(o16[:, :, :], oB[:, :, :], AF.Identity)
                P.dma(self.qst, (OTOK, OTOK.ap[tsl, 384:896].rearrange("(i p) c -> p i c", p=128)), o16[:, :, :])
            P.barrier()

def _in_map(consts, x_b, pos_b, weights, norm_final):
    m = {"x": np.ascontiguousarray(x_b), "pos": np.ascontiguousarray(pos_b).reshape(1, -1).astype(np.int32)}
    for n in WNAMES:
        m[n] = weights[n]
    m["norm_final"] = norm_final
    m.update(consts)
    return m


def kernel(x, positions, norm_ffn1, ffn1_w1, ffn1_w3, ffn1_w2, norm_mix, w_in,
           cmp_pe_k, cmp_w1_k, cmp_w2_k, cmp_pe_v, cmp_w1_v, cmp_w2_v,
           w_gate, w_up, w_out, norm_ffn2, ffn2_w1, ffn2_w3, ffn2_w2, norm_final):
    loc = locals()
    x = np.asarray(x)
    B, S, _ = x.shape
    L = int(np.asarray(norm_ffn1).shape[0])
    weights = {n: np.ascontiguousarray(np.asarray(loc[n], dtype=np.float32)) for n in WNAMES}
    bld = Builder(S, L)
    nc = bld.build()
    nf = np.ascontiguousarray(np.asarray(norm_final, dtype=np.float32))
    pos = np.asarray(positions)
    in_maps = [_in_map(bld.consts, x[b], pos[b], weights, nf) for b in range(B)]
    res = run_bass_kernel_spmd(nc, in_maps, core_ids=list(range(B)))
    return np.stack([np.asarray(res.results[b]["out"]) for b in range(B)], axis=0).astype(np.float32)
```

```python
import contextlib
import numpy as np
import ml_dtypes
import concourse.bass as bass
import concourse.mybir as mybir
from concourse.bass_utils import run_bass_kernel_spmd

F32 = mybir.dt.float32
BF = mybir.dt.bfloat16
I32 = mybir.dt.int32
AF = mybir.ActivationFunctionType
ALU = mybir.AluOpType
AX = mybir.AxisListType

D = 1024
DFF = 2816
NF = DFF // 128
HD = 64
NEG = -1920.0
EPS = 1e-6
A_IN = 3456
B_IN = 1304
IN_W = 5912
NSEL = 16


class Buf:
    __slots__ = ("name", "w", "r", "sem", "semv", "psum")

    def __init__(self, name):
        self.name = name
        self.psum = False
        self.w = None
        self.r = {}
        self.sem = None
        self.semv = 0


class V:
    __slots__ = ("ap", "bufs")

    def __init__(self, ap, bufs):
        self.ap = ap
        self.bufs = bufs


class Tn:
    def __init__(self, P, name, shape, dtype, space="sbuf", es=None):
        es = es if es is not None else P.es
        P.ntn = getattr(P, "ntn", 0) + 1
        name = f"{name}_{P.ntn}"
        if space == "sbuf":
            self.h = es.enter_context(P.nc.sbuf_tensor(name, list(shape), dtype))
        else:
            self.h = es.enter_context(P.nc.psum_tensor(name, list(shape), dtype))
        self.buf = Buf(name)
        self.buf.psum = (space != "sbuf")
        self.shape = list(shape)
        self.dtype = dtype
        self.P = P

    def __getitem__(self, idx):
        return V(self.h[idx], [self.buf])

    def v(self, ap):
        return V(ap, [self.buf])

    def raw(self, offset, ap):
        return V(bass.AP(tensor=self.h, offset=offset, ap=ap), [self.buf])


class DT:
    def __init__(self, P, name, shape, dtype, kind="Internal"):
        self.t = P.nc.dram_tensor(name, list(shape), dtype, kind=kind)
        self.ap = self.t.ap()
        self.pending = {}
        self.name = name
        self.sem = None
        self.semv = 0
        P.dts.append(self)


class Eng:
    def __init__(self, P, name, h):
        self.P = P
        self.name = name
        self.h = h
        self.sem = P.new_sem("e_" + name)
        self.cnt = 0
        self.waited = {}
        self.pend_r = []
        self.pend_w = []

    def wait(self, toks):
        for tok in toks:
            sem, val = tok[0], tok[1]
            k = sem.num
            if self.waited.get(k, 0) < val:
                self.h.wait_ge(sem, val)
                self.waited[k] = val


class Prog:
    def __init__(self, nc, es):
        self.nc = nc
        self.es = es
        self.nsem = 0
        self.dts = []
        self.scope = None
        self.pe = Eng(self, "pe", nc.tensor)
        self.act = Eng(self, "act", nc.scalar)
        self.dve = Eng(self, "dve", nc.vector)
        self.pool = Eng(self, "pool", nc.gpsimd)
        self.sp = Eng(self, "sp", nc.sync)
        self.engs = [self.pe, self.act, self.dve, self.pool, self.sp]
        self.bar_sem = self.new_sem("bar")
        self.bar_n = 0
        self.dma_toks = {}
        self.n_ins = 0

    def new_sem(self, name):
        self.nsem += 1
        h = self.nc.alloc_semaphore(name=f"{name}_{self.nsem}")
        if self.scope is not None:
            self.scope.append(h)
        return h

    def scope_begin(self):
        assert self.scope is None
        self.scope = []

    def scope_end(self):
        self.barrier()
        sems = self.scope
        self.scope = None
        if sems:
            nums = set(h.num for h in sems)
            self.nc.clear_and_free_semaphores(sems)
            self.op(self.pool, lambda: self.nc.gpsimd.memset(self.scratch.h[:, :], 0.0), [], [self.scratch[:, :]])
            for e in self.engs:
                for k in list(e.waited.keys()):
                    if k in nums:
                        del e.waited[k]
            for k in list(self.dma_toks.keys()):
                if k in nums:
                    del self.dma_toks[k]
            for d in self.dts:
                d.pending = {k: v for k, v in d.pending.items() if k not in nums}
                if d.sem is not None and d.sem.num in nums:
                    d.sem = None
                    d.semv = 0
        self.barrier()

    def _deps(self, E, reads, writes, strict=False):
        toks = []
        for b in reads:
            if b.w is not None:
                t = b.w
                if strict or t[2] != E.name or E.name != "pe":
                    toks.append(t)
            if b.psum:
                for t in b.r.values():
                    if t[2] != E.name:
                        toks.append(t)
        same_ok = (E.name == "pe")
        for b in writes:
            if b.w is not None and (strict or b.w[2] != E.name or not same_ok):
                toks.append(b.w)
            for t in b.r.values():
                if strict or t[2] != E.name or not same_ok:
                    toks.append(t)
        return toks

    def _commit(self, tok, reads, writes):
        ek = tok[2] if tok[2] is not None else tok[0].num
        for b in reads:
            b.r[ek] = tok
        for b in writes:
            b.w = tok
            b.r = {}

    def op(self, E, fn, reads, writes, sig=True):
        rb = [b for v in reads for b in v.bufs]
        wb = [b for v in writes for b in v.bufs]
        for b in rb + wb:
            assert not (b in self.pe.pend_r or b in self.pe.pend_w) or E is self.pe, \
                f"buffer {b.name} has unsignalled PE access"
        E.wait(self._deps(E, rb, wb))
        ins = fn()
        self.n_ins += 1
        if E is self.pe and not sig:
            E.pend_r += rb
            E.pend_w += wb
            return ins
        E.cnt += 1
        ins.then_inc(E.sem, 1)
        tok = (E.sem, E.cnt, E.name)
        if E is self.pe:
            rb = rb + E.pend_r
            wb = wb + E.pend_w
            E.pend_r = []
            E.pend_w = []
        self._commit(tok, rb, wb)
        return ins

    def dma(self, Q, out, in_, **kw):
        o_dram = isinstance(out, tuple)
        i_dram = isinstance(in_, tuple)
        toks = []
        if i_dram:
            toks += list(in_[0].pending.values())
            in_ap = in_[1]
        else:
            in_ap = in_.ap
            for b in in_.bufs:
                if b.w is not None:
                    toks.append(b.w)
        part = kw.pop("part", False)
        if o_dram:
            out_ap = out[1]
        else:
            out_ap = out.ap
            dd = self._deps(Q, [], out.bufs, strict=True)
            if part:
                dd = [t for t in dd if not (t[2] is None and t is out.bufs[0].w)]
            toks += dd
        Q.wait(toks)
        ins = Q.h.dma_start(out=out_ap, in_=in_ap, **kw)
        self.n_ins += 1
        if not o_dram:
            b = out.bufs[0]
        elif not i_dram:
            b = in_.bufs[0]
        else:
            b = out[0]
        if b.sem is None:
            b.sem = self.new_sem("d")
        b.semv += 16
        ins.then_inc(b.sem, 16)
        tok = (b.sem, b.semv, None)
        self.dma_toks[b.sem.num] = tok
        if not o_dram:
            out.bufs[0].w = tok
            out.bufs[0].r = {}
        else:
            out[0].pending[b.sem.num] = tok
            if not i_dram:
                in_.bufs[0].r[b.sem.num] = tok
        return ins

    def barrier(self):
        sp = self.sp
        toks = [(e.sem, e.cnt, e.name) for e in self.engs if e is not sp and e.cnt > 0]
        toks += list(self.dma_toks.values())
        sp.wait(toks)
        self.bar_n += 1
        sp.h.sem_inc(self.bar_sem, 1)
        for e in self.engs:
            if e is not sp:
                e.h.wait_ge(self.bar_sem, self.bar_n)
                for t in toks:
                    k = t[0].num
                    if e.waited.get(k, 0) < t[1]:
                        e.waited[k] = t[1]

    def mm(self, out, lhsT, rhs, start, stop, sig=None):
        if sig is None:
            sig = stop
        return self.op(self.pe,
                       lambda: self.nc.tensor.matmul(out.ap, lhsT=lhsT.ap, rhs=rhs.ap, start=start, stop=stop,
                                                     skip_group_check=True),
                       [lhsT, rhs], [out], sig=sig)

    def transpose(self, out, in_, ident, sig=True):
        return self.op(self.pe, lambda: self.nc.tensor.transpose(out.ap, in_.ap, ident.ap),
                       [in_, ident], [out], sig=sig)

    def activation(self, out, in_, func, scale=1.0, bias=None, eng=None):
        E = eng or self.act
        kw = {}
        reads = [in_]
        if bias is not None:
            if isinstance(bias, V):
                kw["bias"] = bias.ap
                reads.append(bias)
            else:
                kw["bias"] = bias
        if isinstance(scale, V):
            reads.append(scale)
            sc = scale.ap
        else:
            sc = scale
        return self.op(E, lambda: E.h.activation(out=out.ap, in_=in_.ap, func=func, scale=sc, **kw), reads, [out])

    def tt(self, E, out, in0, in1, op):
        return self.op(E, lambda: E.h.tensor_tensor(out=out.ap, in0=in0.ap, in1=in1.ap, op=op), [in0, in1], [out])

    def ts(self, E, out, in0, s1, s2, op0, op1=None):
        reads = [in0]
        a1 = s1
        a2 = s2
        if isinstance(s1, V):
            reads.append(s1)
            a1 = s1.ap
        if isinstance(s2, V):
            reads.append(s2)
            a2 = s2.ap
        if op1 is None:
            return self.op(E, lambda: E.h.tensor_scalar(out=out.ap, in0=in0.ap, scalar1=a1, scalar2=None, op0=op0),
                           reads, [out])
        return self.op(E, lambda: E.h.tensor_scalar(out=out.ap, in0=in0.ap, scalar1=a1, scalar2=a2, op0=op0, op1=op1),
                       reads, [out])

    def stt(self, out, in0, scalar, in1, op0, op1):
        reads = [in0, in1]
        a = scalar
        if isinstance(scalar, V):
            reads.append(scalar)
            a = scalar.ap
        E = self.dve
        return self.op(E, lambda: E.h.scalar_tensor_tensor(out=out.ap, in0=in0.ap, scalar=a, in1=in1.ap, op0=op0, op1=op1),
                       reads, [out])

    def copy(self, E, out, in_):
        if E is self.act:
            return self.op(E, lambda: E.h.copy(out=out.ap, in_=in_.ap), [in_], [out])
        return self.op(E, lambda: E.h.tensor_copy(out=out.ap, in_=in_.ap), [in_], [out])

    def memset(self, E, out, val):
        return self.op(E, lambda: E.h.memset(out.ap, val), [], [out])

    def recip(self, out, in_):
        E = self.dve
        return self.op(E, lambda: E.h.reciprocal(out=out.ap, in_=in_.ap), [in_], [out])


def _fm_blocks():
    blks = []
    for g in range(3):
        for sp in range(3):
            blks.append(("qA", [(0 * 1152 + g * 384 + sp * 128, 128)], g * 3 + sp, None))
    for g in range(3):
        for sp in range(3):
            blks.append(("kA", [(1152 + g * 384 + sp * 128, 128)], 9 + g * 3 + sp, None))
    bq = A_IN
    for r in range(4):
        blks.append(("qB", [(bq + r * 64, 64), (bq + (4 + r) * 64, 64)], 22 + r, 18 + r))
    bkv = A_IN + 512
    blks.append(("kcmp", [(bkv, 128)], None, 26))
    blks.append(("vcmp", [(bkv + 128, 128)], None, 27))
    blks.append(("kslc", [(bkv + 256, 128)], 28, None))
    blks.append(("kwin", [(bkv + 512, 128)], 29, None))
    cb = A_IN + B_IN
    for hp in range(3):
        blks.append(("qC", [(cb + hp * 128, 128)], None, 30 + hp))
    for hp in range(3):
        blks.append(("kC", [(cb + 384 + hp * 128, 128)], None, 33 + hp))
    return blks


FM = _fm_blocks()
NQT = 36
VRUNS = [(2304, 1152), (A_IN + 512 + 256 + 128, 128), (A_IN + 512 + 512 + 128, 128), (A_IN + 1280, 24),
         (A_IN + B_IN + 768, 384)]
NV = 1816
VO_A, VO_SLC, VO_WIN, VO_GATE, VO_C = 0, 1152, 1280, 1408, 1432
NVG = 4
VGW = NV // NVG
A_DIL = (1, 4, 16)


def _consts(S):
    bf = ml_dtypes.bfloat16
    NT = S // 128
    c = {}
    c["c_ident"] = np.eye(128, dtype=np.float32).astype(bf)
    c["c_identf"] = np.eye(128, dtype=np.float32)
    perm = np.zeros((128, 128), np.float32)
    for m in range(128):
        d = m % 64
        if d < 8:
            perm[m + 8, m] = 1.0
        elif d < 16:
            perm[m - 8, m] = 1.0
    c["c_perm"] = perm.astype(bf)
    jj = np.arange(128)
    c["c_U"] = (jj[:, None] >= jj[None, :]).astype(np.float32).astype(bf)
    c["c_ones"] = np.ones((128, 128), np.float32).astype(bf)
    si = np.arange(128)[:, None]
    ti = np.arange(512)[None, :]
    tiles = []
    for dil in A_DIL:
        for dl in range(-3, dil + 1):
            d = 128 * dl + ti - si
            ok = (d >= 0) & (d <= 128 * dil) & (d % dil == 0)
            tiles.append(np.where(ok, 0.0, NEG))
    c["c_amask"] = np.stack(tiles).astype(np.float32).astype(bf)
    tiles = []
    for dl in range(-3, 5):
        d = 128 * dl + ti - si
        tiles.append(np.where((d >= 0) & (d <= 511), 0.0, NEG))
    c["c_wmask"] = np.stack(tiles).astype(np.float32).astype(bf)
    tiles = []
    tiles2 = []
    for dl in range(-3, 1):
        d = 128 * dl + ti - si
        tiles.append(np.where(d >= 0, 0.0, NEG))
        tiles2.append(np.where(d >= 1, 1.0, 0.0))
    c["c_smask"] = np.stack(tiles).astype(np.float32).astype(bf)
    c["c_cmask"] = np.stack(tiles2).astype(np.float32).astype(bf)
    cc = np.arange(256)[:, None]
    tt = np.arange(S)[None, :]
    n_cmp = S // 16 - 1
    c["c_cmpb"] = np.where((16 * cc + 31 <= tt) & (cc < n_cmp), 0.0, NEG).astype(np.float32).astype(bf)
    es = np.zeros((64, NT, 128), np.float32)
    for sg in range(NT):
        for s_ in range(128):
            j = 2 * sg + s_ // 64
            if j < 64:
                es[j, sg, s_] = 1.0
    c["c_esel"] = es.astype(bf)
    w = np.zeros((256, 64), np.float32)
    for j in range(64):
        for m in range(4):
            for n in range(2):
                ci = 4 * j + m + n
                if ci < 256:
                    w[ci, j] += 1.0
    c["c_wsel"] = w.astype(bf)
    t = np.arange(S)[:, None]
    j = np.arange(64)[None, :]
    cur = t // 64
    nsel = S // 64
    valid = (j * 64 <= t) & (j < nsel)
    forced = ((j == 0) | (j == cur) | (j == cur - 1)) & (j < nsel)
    c["c_vnf"] = (valid & ~forced).astype(np.float32)
    c["c_add"] = np.where(j >= nsel, -3.0, np.where(forced, 1e4, np.where(valid, 0.0, -1.0))).astype(np.float32)
    rp = np.zeros((128, 2), np.float32)
    for p in range(128):
        d = p % 64
        if d < 16:
            rp[p, 0] = np.float32(500000.0) ** np.float32(-(2 * (d % 8)) / 16.0)
            rp[p, 1] = -1.0 if d < 8 else 1.0
    c["c_rope"] = rp
    return c


WNAMES = ["norm_ffn1", "ffn1_w1", "ffn1_w3", "ffn1_w2", "norm_mix", "w_in", "cmp_pe_k", "cmp_w1_k", "cmp_w2_k",
          "cmp_pe_v", "cmp_w1_v", "cmp_w2_v", "w_gate", "w_up", "w_out", "norm_ffn2", "ffn2_w1", "ffn2_w3",
          "ffn2_w2"]
WSHAPES = {"norm_ffn1": [D], "ffn1_w1": [D, DFF], "ffn1_w3": [D, DFF], "ffn1_w2": [DFF, D], "norm_mix": [D],
           "w_in": [D, IN_W], "cmp_pe_k": [32, 64], "cmp_w1_k": [2048, 128], "cmp_w2_k": [128, 64],
           "cmp_pe_v": [32, 64], "cmp_w1_v": [2048, 128], "cmp_w2_v": [128, 64], "w_gate": [D, 3 * D],
           "w_up": [1280, D], "w_out": [D, D], "norm_ffn2": [D], "ffn2_w1": [D, DFF], "ffn2_w3": [D, DFF],
           "ffn2_w2": [DFF, D]}


class Ring:
    def __init__(self, tiles):
        self.t = tiles
        self.i = 0

    def next(self):
        t = self.t[self.i % len(self.t)]
        self.i += 1
        return t


class Builder:
    def __init__(self, S, L, TT=512, dbg=(), phases=None):
        self.S, self.L, self.TT = S, L, TT
        self.NT = S // 128
        self.NG = TT // 512
        self.dbg = set(dbg)
        self.phases = phases
        self.consts = _consts(S)
        nc = bass.Bass("TRN2", target_bir_lowering=False)
        self.nc = nc
        self.es = contextlib.ExitStack()
        self.P = Prog(nc, self.es)
        self.din = {}
        import os
        self.qst = {'sp': self.P.sp, 'pool': self.P.pool, 'act': self.P.act}[os.environ.get('K_QST', 'pool')]
        self._ptoks = {}
        self._pn = 0

    def inp(self, name, shape, dtype):
        d = DT(self.P, name, shape, dtype, kind="ExternalInput")
        self.din[name] = d
        return d

    def scr(self, name, shape, dtype):
        kind = "ExternalOutput" if name in self.dbg else "Internal"
        return DT(self.P, name, shape, dtype, kind=kind)

    def sb(self, name, shape, dtype, es=None):
        return Tn(self.P, name, shape, dtype, "sbuf", es)

    def psum_banks(self, es, n=8):
        return [Tn(self.P, f"psb{i}", [128, 512], F32, "psum", es) for i in range(n)]

    def build(self):
        P, S, L = self.P, self.S, self.L
        self.x = self.inp("x", [S, D], F32)
        self.pos = self.inp("pos", [1, S], I32)
        self.W = {}
        for n in WNAMES:
            self.W[n] = self.inp(n, [L] + WSHAPES[n], F32)
        self.nf = self.inp("norm_final", [D], F32)
        self.C = {}
        for k, v in self.consts.items():
            self.C[k] = self.inp(k, list(v.shape), BF if v.dtype == ml_dtypes.bfloat16 else F32)
        self.out = DT(P, "out", [S, D], F32, kind="ExternalOutput")
        self.HT = [self.scr(f"HT{l}", [8, 128, S], F32) for l in range(L)]
        self.HT2 = [self.scr(f"HTb{l}", [8, 128, S], F32) for l in range(L)]
        self.QT = [self.scr(f"QT{l}", [NQT, 128, S], BF) for l in range(L)]
        self.VT = [self.scr(f"VT{l}", [S, NV], BF) for l in range(L)]
        self.GT = [self.scr(f"GT{l}", [24, 128, S], BF) for l in range(L)]
        self.OTOK = [self.scr(f"OTOK{l}", [S, 896], BF) for l in range(L)]
        self.OTC = [self.scr(f"OTC{l}", [3, 128, S], BF) for l in range(L)]
        self.CS = self.scr("CS", [2, 128, S], F32)
        self.WB = []
        for l in range(L):
            d = {}
            for f in (1, 2):
                d[f"w1_{f}"] = self.scr(f"b_w1_{f}_{l}", [NF, 128, 8 * 128], BF)
                d[f"w3_{f}"] = self.scr(f"b_w3_{f}_{l}", [NF, 128, 8 * 128], BF)
                d[f"w2_{f}"] = self.scr(f"b_w2_{f}_{l}", [8, 128, NF * 128], BF)
            d["win"] = self.scr(f"b_win_{l}", [len(FM), 128, 8 * 128], BF)
            d["wv"] = self.scr(f"b_wv_{l}", [NVG, 128, 8 * VGW], BF)
            d["wg"] = self.scr(f"b_wg_{l}", [24, 128, 8 * 128], BF)
            d["wup"] = self.scr(f"b_wup_{l}", [8, 128, 10 * 128], BF)
            d["wout"] = self.scr(f"b_wout_{l}", [8, 128, 8 * 128], BF)
            self.WB.append(d)

        P.scratch = self.sb("p_scratch", [128, 8], F32)
        self.k_ident = self.sb("k_ident", [128, 128], BF)
        self.k_identf = self.sb("k_identf", [128, 128], F32)
        self.k_perm = self.sb("k_perm", [128, 128], BF)
        self.k_U = self.sb("k_U", [128, 128], BF)
        self.k_ones = self.sb("k_ones", [128, 128], BF)
        self.k_rope = self.sb("k_rope", [128, 2], F32)
        self.k_one = self.sb("k_one", [128, 1], F32)
        P.memset(P.dve, self.k_one[:, :], 1.0)
        for t, n in ((self.k_ident, "c_ident"), (self.k_identf, "c_identf"), (self.k_perm, "c_perm"),
                     (self.k_U, "c_U"), (self.k_ones, "c_ones"), (self.k_rope, "c_rope")):
            P.dma(P.sp, t[:, :], (self.C[n], self.C[n].ap))
        self.g = {}
        for n in ("norm_ffn1", "norm_mix", "norm_ffn2"):
            for l in range(L):
                t = self.sb(f"g_{n}_{l}", [128, 8], F32)
                P.dma(P.sp, t[:, :], (self.W[n], self.W[n].ap[l].rearrange("(c p) -> p c", p=128)),
                      allow_slow_non_contiguous=True)
                self.g[(n, l)] = t
        t = self.sb("g_final", [128, 8], F32)
        P.dma(P.sp, t[:, :], (self.nf, self.nf.ap.rearrange("(c p) -> p c", p=128)), allow_slow_non_contiguous=True)
        self.g[("final", 0)] = t

        ph = self.phases
        P.barrier()
        if ph is None or "p0" in ph:
            P.scope_begin()
            self.phase0()
            P.scope_end()
        for l in range(L):
            if ph is None or "p1" in ph:
                P.scope_begin()
                self.phase1(l)
                P.scope_end()
            if ph is None or "p2" in ph:
                self.phase2(l)
            if ph is None or "p3" in ph:
                P.scope_begin()
                self.phase3(l)
                P.scope_end()
        self.es.close()
        return self.nc

    def _prep(self, dst, j, src2d, runs, nchunk, width):
        P = self.P
        off = 0
        dv = dst.ap[j].rearrange("p (c n) -> p c n", n=width)
        for (col, w) in runs:
            src = src2d[:, col:col + w].rearrange("(c p) n -> p c n", p=128)
            P.dma(P.pool, (dst, dv[:, :, off:off + w]), (self.x, src))
            off += w
            self._ptoks[dst.sem.num] = (dst.sem, dst.semv, None)
            self._pn += 1
            if self._pn % 12 == 0:
                P.pool.wait(list(self._ptoks.values()))

    def phase0(self):
        import os
        P, S, L = self.P, self.S, self.L
        if os.environ.get('K_SKIP_ROPE') is None:
            self.rope_tables()
        if os.environ.get('K_SKIP_PREP'):
            return
        for l in range(L):
            wb = self.WB[l]
            for f in (1, 2):
                w1 = self.W[f"ffn{f}_w1"].ap[l]
                w3 = self.W[f"ffn{f}_w3"].ap[l]
                w2 = self.W[f"ffn{f}_w2"].ap[l]
                for j in range(NF):
                    self._prep(wb[f"w1_{f}"], j, w1, [(j * 128, 128)], 8, 128)
                    self._prep(wb[f"w3_{f}"], j, w3, [(j * 128, 128)], 8, 128)
                for j in range(8):
                    self._prep(wb[f"w2_{f}"], j, w2, [(j * 128, 128)], NF, 128)
                if f == 1:
                    win = self.W["w_in"].ap[l]
                    for j, blk in enumerate(FM):
                        self._prep(wb["win"], j, win, blk[1], 8, 128)
                    cols = []
                    for (c0, w) in VRUNS:
                        cols += [(c0, w)]
                    flat = []
                    for (c0, w) in cols:
                        flat.append([c0, w])
                    for gi in range(NVG):
                        need = VGW
                        runs = []
                        while need > 0:
                            c0, w = flat[0]
                            take = min(w, need)
                            runs.append((c0, take))
                            need -= take
                            if take == w:
                                flat.pop(0)
                            else:
                                flat[0] = [c0 + take, w - take]
                        self._prep(wb["wv"], gi, win, runs, 8, VGW)
                    wg = self.W["w_gate"].ap[l]
                    for j in range(24):
                        self._prep(wb["wg"], j, wg, [(j * 128, 128)], 8, 128)
                    wu = self.W["w_up"].ap[l]
                    for j in range(8):
                        self._prep(wb["wup"], j, wu, [(j * 128, 128)], 10, 128)
                    wo = self.W["w_out"].ap[l]
                    for j in range(8):
                        self._prep(wb["wout"], j, wo, [(j * 128, 128)], 8, 128)

    def rope_tables(self):
        P, S = self.P, self.S
        PI = float(np.pi)
        with contextlib.ExitStack() as es:
            pi_ = self.sb("rp_pi", [128, S], I32, es)
            a = self.sb("rp_a", [128, S], F32, es)
            b = self.sb("rp_b", [128, S], F32, es)
            r = self.sb("rp_r", [128, S], F32, es)
            m = self.sb("rp_m", [128, S], F32, es)
            dve = P.dve
            P.dma(P.sp, pi_[:, :], (self.pos, self.pos.ap[0].partition_broadcast(128)))
            P.copy(dve, a[:, :], pi_[:, :])
            P.ts(dve, a[:, :], a[:, :], self.k_rope[:, 0:1], None, ALU.mult)
            P.ts(dve, b[:, :], a[:, :], 1.0 / (2 * PI), None, ALU.mult)
            ki = pi_
            P.copy(dve, ki[:, :], b[:, :])
            P.copy(dve, b[:, :], ki[:, :])
            C1 = 6.28125
            C2 = float(2 * np.pi - 6.28125)
            P.stt(r[:, :], b[:, :], -C1, a[:, :], ALU.mult, ALU.add)
            P.stt(r[:, :], b[:, :], -C2, r[:, :], ALU.mult, ALU.add)
            for (thr, op, adj) in ((PI, ALU.is_gt, -2 * PI), (-PI, ALU.is_lt, 2 * PI), (PI, ALU.is_gt, -2 * PI),
                                   (-PI, ALU.is_lt, 2 * PI)):
                P.ts(dve, m[:, :], r[:, :], thr, adj, op, ALU.mult)
                P.tt(dve, r[:, :], r[:, :], m[:, :], ALU.add)
            LIM = 3.1415925
            P.ts(dve, r[:, :], r[:, :], LIM, -LIM, ALU.min, ALU.max)
            P.activation(m[:, :], r[:, :], AF.Sin)
            P.ts(dve, m[:, :], m[:, :], self.k_rope[:, 1:2], None, ALU.mult)
            P.dma(P.pool, (self.CS, self.CS.ap[1]), m[:, :])
            P.ts(dve, b[:, :], r[:, :], -1.0, None, ALU.mult)
            P.tt(dve, b[:, :], b[:, :], r[:, :], ALU.max)
            P.ts(dve, b[:, :], b[:, :], -1.0, PI / 2, ALU.mult, ALU.add)
            P.activation(a[:, :], b[:, :], AF.Sin)
            P.dma(P.pool, (self.CS, self.CS.ap[0]), a[:, :])
            P.barrier()

    def rl_alloc(self, es):
        TT = self.TT
        R = {}
        R["hT"] = [self.sb(f"hT{c}", [128, TT], F32, es) for c in range(8)]
        R["xn"] = [self.sb(f"xn{c}", [128, TT], BF, es) for c in range(8)]
        R["h1"] = [self.sb(f"h1_{j}", [128, TT], BF, es) for j in range(NF)]
        R["sq"] = Ring([self.sb(f"sq{i}", [128, 512], BF, es) for i in range(2)])
        R["rt"] = [self.sb(f"rt{i}", [128, 512], F32, es) for i in range(self.NG)]
        R["wr"] = Ring([self.sb(f"wr{i}", [128, 8 * 128], BF, es) for i in range(6)])
        R["w2r"] = Ring([self.sb(f"w2r{i}", [128, NF * 128], BF, es) for i in range(2)])
        R["sa"] = Ring([self.sb(f"sa{i}", [128, 512], F32, es) for i in range(2)])
        R["ps"] = Ring(self.psum_banks(es, 7))
        return R

    def rmsnorm(self, R, g, out, out_f32=False):
        P = self.P
        for n in range(self.NG):
            ns = slice(n * 512, (n + 1) * 512)
            ps = R["ps"].next()
            for c in range(8):
                sq = R["sq"].next()
                P.activation(sq[:, :], R["hT"][c][:, ns], AF.Square)
                P.mm(ps[:, :], self.k_ones[:, :], sq[:, :], start=(c == 0), stop=(c == 7), sig=True)
            rt = R["rt"][n]
            P.activation(rt[:, :], ps[:, :], AF.Sqrt, scale=1.0 / D, bias=self.k_eps[:, 0:1])
            P.recip(rt[:, :], rt[:, :])
            for c in range(8):
                P.stt(out[c][:, ns], R["hT"][c][:, ns], g[:, c:c + 1], rt[:, :], ALU.mult, ALU.mult)

    def ffn(self, R, l, f):
        P = self.P
        wb = self.WB[l]
        W1, W3, W2 = wb[f"w1_{f}"], wb[f"w3_{f}"], wb[f"w2_{f}"]
        xn, h1, hT = R["xn"], R["h1"], R["hT"]
        for j in range(NF):
            wa = R["wr"].next()
            P.dma(P.sp, wa[:, :], (W1, W1.ap[j]))
            wc = R["wr"].next()
            P.dma(P.sp, wc[:, :], (W3, W3.ap[j]))
            for n in range(self.NG):
                ns = slice(n * 512, (n + 1) * 512)
                pa = R["ps"].next()
                pb = R["ps"].next()
                for c in range(8):
                    P.mm(pa[:, :], wa[:, c * 128:(c + 1) * 128], xn[c][:, ns], start=(c == 0), stop=(c == 7))
                for c in range(8):
                    P.mm(pb[:, :], wc[:, c * 128:(c + 1) * 128], xn[c][:, ns], start=(c == 0), stop=(c == 7))
                sa = R["sa"].next()
                P.activation(sa[:, :], pa[:, :], AF.Silu)
                P.tt(P.dve, h1[j][:, ns], sa[:, :], pb[:, :], ALU.mult)
        for dm in range(8):
            w2 = R["w2r"].next()
            P.dma(P.sp, w2[:, :], (W2, W2.ap[dm]))
            for n in range(self.NG):
                ns = slice(n * 512, (n + 1) * 512)
                py = R["ps"].next()
                for j in range(NF):
                    P.mm(py[:, :], w2[:, j * 128:(j + 1) * 128], h1[j][:, ns], start=(j == 0), stop=(j == NF - 1))
                P.stt(hT[dm][:, ns], py[:, :], 0.5, hT[dm][:, ns], ALU.mult, ALU.add)

    def phase1(self, l):
        P, S, TT = self.P, self.S, self.TT
        wb = self.WB[l]
        with contextlib.ExitStack() as es:
            R = self.rl_alloc(es)
            hT, xn = R["hT"], R["xn"]
            self.k_eps = self.sb("k_eps", [128, 1], F32, es)
            P.memset(P.dve, self.k_eps[:, :], EPS)
            ctab = self.sb("ctab", [128, TT], F32, es)
            stab = self.sb("stab", [128, TT], F32, es)
            qsb = Ring([self.sb(f"qsb{i}", [128, 512], BF, es) for i in range(3)])
            t1 = Ring([self.sb(f"t1_{i}", [128, 512], F32, es) for i in range(2)])
            t2 = Ring([self.sb(f"t2_{i}", [128, 512], F32, es) for i in range(2)])
            stg = Ring([self.sb(f"stg{i}", [128, 512], BF, es) for i in range(3)])
            wvr = Ring([self.sb(f"wvr{i}", [128, 8 * VGW], BF, es) for i in range(2)])
            vst = [self.sb(f"vst{i}", [128, NV], BF, es) for i in range(TT // 128)]
            xin = Ring([self.sb(f"xin{i}", [128, D], F32, es) for i in range(2)]) if l == 0 else None
            for tt in range(S // TT):
                t0 = tt * TT
                tsl = slice(t0, t0 + TT)
                if l == 0:
                    for ts in range(TT // 128):
                        xi = xin.next()
                        P.dma(P.sp, xi[:, :], (self.x, self.x.ap[t0 + ts * 128:t0 + (ts + 1) * 128, :]))
                        for half in range(2):
                            ps = R["ps"].next()
                            for q in range(4):
                                c = half * 4 + q
                                P.transpose(ps[:, q * 128:(q + 1) * 128], xi[:, c * 128:(c + 1) * 128],
                                            self.k_identf[:, :], sig=(q == 3))
                            for q in range(4):
                                c = half * 4 + q
                                P.copy(P.dve if q % 2 else P.act, hT[c][:, ts * 128:(ts + 1) * 128],
                                       ps[:, q * 128:(q + 1) * 128])
                else:
                    src = self.HT2[l - 1]
                    for c in range(8):
                        P.dma(P.sp, hT[c][:, :], (src, src.ap[c, :, tsl]))
                P.dma(P.sp, ctab[:, :], (self.CS, self.CS.ap[0, :, tsl]))
                P.dma(P.sp, stab[:, :], (self.CS, self.CS.ap[1, :, tsl]))
                import os
                stop = os.environ.get('K_P1_STOP', '')
                if stop == 'load':
                    continue
                self.rmsnorm(R, self.g[("norm_ffn1", l)], xn)
                if stop == 'norm':
                    continue
                self.ffn(R, l, 1)
                if stop == 'ffn':
                    continue
                self.rmsnorm(R, self.g[("norm_mix", l)], xn)
                QT = self.QT[l]
                for bi, blk in enumerate(FM):
                    w = R["wr"].next()
                    P.dma(P.sp, w[:, :], (wb["win"], wb["win"].ap[bi]))
                    for n in range(self.NG):
                        ns = slice(n * 512, (n + 1) * 512)
                        dsl = slice(t0 + n * 512, t0 + (n + 1) * 512)
                        pm = R["ps"].next()
                        for c in range(8):
                            P.mm(pm[:, :], w[:, c * 128:(c + 1) * 128], xn[c][:, ns], start=(c == 0), stop=(c == 7))
                        qs = qsb.next()
                        P.activation(qs[:, :], pm[:, :], AF.Identity)
                        if blk[3] is not None:
                            P.dma(self.qst, (QT, QT.ap[blk[3], :, dsl]), qs[:, :])
                        if blk[2] is not None:
                            psw = R["ps"].next()
                            P.mm(psw[:, :], self.k_perm[:, :], qs[:, :], start=True, stop=True)
                            a = t1.next()
                            b = t2.next()
                            P.tt(P.dve, a[:, :], pm[:, :], ctab[:, ns], ALU.mult)
                            P.tt(P.dve, b[:, :], psw[:, :], stab[:, ns], ALU.mult)
                            st = stg.next()
                            P.tt(P.pool, st[:, :], a[:, :], b[:, :], ALU.add)
                            P.dma(self.qst, (QT, QT.ap[blk[2], :, dsl]), st[:, :])
                if stop == 'fm':
                    continue
                VT = self.VT[l]
                for gi in range(NVG):
                    wv = wvr.next()
                    P.dma(P.sp, wv[:, :], (wb["wv"], wb["wv"].ap[gi]))
                    for ts in range(TT // 128):
                        pv = R["ps"].next()
                        for c in range(8):
                            P.mm(pv[:, 0:VGW], xn[c][:, ts * 128:(ts + 1) * 128], wv[:, c * VGW:(c + 1) * VGW],
                                 start=(c == 0), stop=(c == 7))
                        P.activation(vst[ts][:, gi * VGW:(gi + 1) * VGW], pv[:, 0:VGW], AF.Identity)
                for ts in range(TT // 128):
                    P.dma(self.qst, (VT, VT.ap[t0 + ts * 128:t0 + (ts + 1) * 128, :]), vst[ts][:, :])
                if stop == 'v':
                    continue
                GT = self.GT[l]
                for bi in range(24):
                    w = R["wr"].next()
                    P.dma(P.sp, w[:, :], (wb["wg"], wb["wg"].ap[bi]))
                    for n in range(self.NG):
                        ns = slice(n * 512, (n + 1) * 512)
                        dsl = slice(t0 + n * 512, t0 + (n + 1) * 512)
                        pg = R["ps"].next()
                        for c in range(8):
                            P.mm(pg[:, :], w[:, c * 128:(c + 1) * 128], xn[c][:, ns], start=(c == 0), stop=(c == 7))
                        st = stg.next()
                        P.activation(st[:, :], pg[:, :], AF.Sigmoid)
                        P.dma(self.qst, (GT, GT.ap[bi, :, dsl]), st[:, :])
                if stop == 'gate':
                    continue
                HT = self.HT[l]
                for c in range(8):
                    P.dma(self.qst, (HT, HT.ap[c, :, tsl]), hT[c][:, :])
            P.barrier()

    def phase3(self, l):
        P, S, TT = self.P, self.S, self.TT
        wb = self.WB[l]
        last = (l == self.L - 1)
        with contextlib.ExitStack() as es:
            R = self.rl_alloc(es)
            hT, xn = R["hT"], R["xn"]
            self.k_eps = self.sb("k_eps3", [128, 1], F32, es)
            P.memset(P.dve, self.k_eps[:, :], EPS)
            oT = [self.sb(f"oT{k}", [128, TT], BF, es) for k in range(10)]
            otok = [self.sb(f"otok{i}", [128, 896], BF, es) for i in range(TT // 128)]
            psT = Tn(P, "psT", [128, 1024], BF, "psum", es)
            gr = Ring([self.sb(f"gr{i}", [128, TT], BF, es) for i in range(6)])
            y = [self.sb(f"y{k}", [128, TT], BF, es) for k in range(8)]
            wur = Ring([self.sb(f"wur{i}", [128, 10 * 128], BF, es) for i in range(2)])
            t1 = Ring([self.sb(f"t31_{i}", [128, 512], F32, es) for i in range(2)])
            t2 = Ring([self.sb(f"t32_{i}", [128, 512], F32, es) for i in range(2)])
            if last:
                xof = [self.sb(f"xof{c}", [128, TT], F32, es) for c in range(8)]
                ost = Ring([self.sb(f"ost{i}", [128, D], F32, es) for i in range(2)])
            GT, OTOK, OTC, HT = self.GT[l], self.OTOK[l], self.OTC[l], self.HT[l]
            for tt in range(S // TT):
                t0 = tt * TT
                tsl = slice(t0, t0 + TT)
                for c in range(8):
                    P.dma(P.sp, hT[c][:, :], (HT, HT.ap[c, :, tsl]))
                for hp in range(3):
                    P.dma(P.sp, oT[7 + hp][:, :], (OTC, OTC.ap[hp, :, tsl]))
                for ts in range(TT // 128):
                    P.dma(P.sp, otok[ts][:, :], (OTOK, OTOK.ap[t0 + ts * 128:t0 + (ts + 1) * 128, :]))
                for kb in range(7):
                    for n in range(self.NG):
                        for q in range(4):
                            ts = n * 4 + q
                            P.transpose(psT[:, q * 128:(q + 1) * 128], otok[ts][:, kb * 128:(kb + 1) * 128],
                                        self.k_ident[:, :], sig=(q == 3))
                        P.copy(P.dve if kb % 2 else P.act, oT[kb][:, n * 512:(n + 1) * 512], psT[:, 0:512])
                for dm in range(8):
                    w = wur.next()
                    P.dma(P.sp, w[:, :], (wb["wup"], wb["wup"].ap[dm]))
                    gs = []
                    for m in range(3):
                        gt = gr.next()
                        P.dma(P.sp, gt[:, :], (GT, GT.ap[m * 8 + dm, :, tsl]))
                        gs.append(gt)
                    for n in range(self.NG):
                        ns = slice(n * 512, (n + 1) * 512)
                        pp = []
                        for (k0, k1) in ((0, 3), (3, 7), (7, 10)):
                            p_ = R["ps"].next()
                            for k in range(k0, k1):
                                P.mm(p_[:, :], w[:, k * 128:(k + 1) * 128], oT[k][:, ns], start=(k == k0),
                                     stop=(k == k1 - 1))
                            pp.append(p_)
                        a = t1.next()
                        b = t2.next()
                        P.tt(P.dve, a[:, :], pp[0][:, :], gs[0][:, ns], ALU.mult)
                        P.tt(P.dve, b[:, :], pp[1][:, :], gs[1][:, ns], ALU.mult)
                        P.tt(P.pool, a[:, :], a[:, :], b[:, :], ALU.add)
                        b2 = t2.next()
                        P.tt(P.dve, b2[:, :], pp[2][:, :], gs[2][:, ns], ALU.mult)
                        P.tt(P.pool, y[dm][:, ns], a[:, :], b2[:, :], ALU.add)
                for dm2 in range(8):
                    w = R["wr"].next()
                    P.dma(P.sp, w[:, :], (wb["wout"], wb["wout"].ap[dm2]))
                    for n in range(self.NG):
                        ns = slice(n * 512, (n + 1) * 512)
                        p_ = R["ps"].next()
                        for k in range(8):
                            P.mm(p_[:, :], w[:, k * 128:(k + 1) * 128], y[k][:, ns], start=(k == 0), stop=(k == 7))
                        P.tt(P.dve, hT[dm2][:, ns], p_[:, :], hT[dm2][:, ns], ALU.add)
                self.rmsnorm(R, self.g[("norm_ffn2", l)], xn)
                self.ffn(R, l, 2)
                if not last:
                    H2 = self.HT2[l]
                    for c in range(8):
                        P.dma(self.qst, (H2, H2.ap[c, :, tsl]), hT[c][:, :])
                else:
                    self.rmsnorm(R, self.g[("final", 0)], xof)
                    for ts in range(TT // 128):
                        o_ = ost.next()
                        for half in range(2):
                            ps = R["ps"].next()
                            for q in range(4):
                                c = half * 4 + q
                                P.transpose(ps[:, q * 128:(q + 1) * 128], xof[c][:, ts * 128:(ts + 1) * 128],
                                            self.k_identf[:, :], sig=(q == 3))
                            P.copy(P.dve if half else P.act, o_[:, half * 512:(half + 1) * 512], ps[:, :])
                        P.dma(self.qst, (self.out, self.out.ap[t0 + ts * 128:t0 + (ts + 1) * 128, :]), o_[:, :])
            P.barrier()

    def attn(self, pairs, O, vw, X, nsub=4, outs=None, starts=(0,)):
        P = self.P
        n = len(pairs)
        sps = [None] * n
        pbs = [None] * n

        def stage_s(i):
            p = pairs[i]
            s_ = X["S"].next()
            sps[i] = s_
            nb = len(p["bias"])
            P.mm(s_[:, :], p["kT"], p["q"], start=True, stop=(nb == 0))
            for bi, (lh, rh) in enumerate(p["bias"]):
                P.mm(s_[:, :], lh, rh, start=False, stop=(bi == nb - 1))

        def stage_pv(i):
            t = X["P"].next()
            pbs[i] = t
            P.activation(t[:, :], sps[i][:, :], AF.Exp, scale=0.125)
            if pairs[i].get("mask") is not None:
                P.tt(P.dve, t[:, :], t[:, :], pairs[i]["mask"], ALU.mult)
            for sub in range(nsub):
                o_ = outs[sub] if outs is not None else O[:, sub * vw:(sub + 1) * vw]
                P.mm(o_, t[:, sub * 128:(sub + 1) * 128], pairs[i]["v"],
                     start=(i == 0 and sub in starts), stop=(i == n - 1), sig=(sub == nsub - 1))

        LA = 2
        for i in range(min(LA, n)):
            stage_s(i)
        for i in range(n):
            if i + LA < n:
                stage_s(i + LA)
            stage_pv(i)

    def attn_lanes(self, lanes, vw, X, nsub=4):
        P = self.P
        n = len(lanes[0][0])
        sps = [[None] * n for _ in lanes]

        def stage_s(i):
            for li, (pairs, O) in enumerate(lanes):
                p = pairs[i]
                s_ = X["S"].next()
                sps[li][i] = s_
                nb = len(p["bias"])
                P.mm(s_[:, :], p["kT"], p["q"], start=True, stop=(nb == 0))
                for bi, (lh, rh) in enumerate(p["bias"]):
                    P.mm(s_[:, :], lh, rh, start=False, stop=(bi == nb - 1))

        def stage_pv(i):
            ts = []
            for li, (pairs, O) in enumerate(lanes):
                t = X["P"].next()
                P.activation(t[:, :], sps[li][i][:, :], AF.Exp, scale=0.125)
                ts.append(t)
            for li, (pairs, O) in enumerate(lanes):
                if pairs[i].get("mask") is not None:
                    P.tt(P.dve, ts[li][:, :], ts[li][:, :], pairs[i]["mask"], ALU.mult)
            for li, (pairs, O) in enumerate(lanes):
                for sub in range(nsub):
                    P.mm(O[:, sub * vw:(sub + 1) * vw], ts[li][:, sub * 128:(sub + 1) * 128], pairs[i]["v"],
                         start=(i == 0 and sub == 0), stop=(i == n - 1), sig=(sub == nsub - 1))

        stage_s(0)
        for i in range(n):
            if i + 1 < n:
                stage_s(i + 1)
            stage_pv(i)

    def load_v_tm(self, dst, VT, c0, ncol, nh, wcol):
        P = self.P
        NT = self.NT
        src = VT.ap[:, c0:c0 + ncol].rearrange("(s p) (h d) -> p s h d", p=128, h=nh)
        first = True
        for s0 in range(0, NT, 16):
            s1 = min(NT, s0 + 16)
            for h in range(nh):
                P.dma(P.sp, dst.v(dst.h[:, s0:s1, h, 0:64]), (VT, src[:, s0:s1, h, :]), part=(not first))
                first = False

    def phase2(self, l):
        import os
        sel = os.environ.get("K_P2", "abc")
        for k, fn in (("a", self.mixer_a), ("b", self.mixer_b), ("c", self.mixer_c)):
            if k in sel:
                self.P.scope_begin()
                fn(l)
                self.P.scope_end()

    def p2_pre(self, l):
        pass

    def mixer_a(self, l):
        import os
        skip = os.environ.get('K_A_SKIP', '')
        P, S, NT = self.P, self.S, self.NT
        QT, VT, OTOK = self.QT[l], self.VT[l], self.OTOK[l]
        with contextlib.ExitStack() as es:
            am = self.sb("amask", [128, 33, 512], BF, es)
            for t0 in range(0, 33, 11):
                P.dma(P.sp, am.v(am.h[:, t0:t0 + 11, :]),
                      (self.C["c_amask"], self.C["c_amask"].ap[t0:t0 + 11].rearrange("t p n -> p t n")), part=(t0 > 0))
            for t0 in range(0, 33, 11):
                P.ts(P.dve, am.v(am.h[:, t0:t0 + 11, :]), am.v(am.h[:, t0:t0 + 11, :]), NEG / 2, None, ALU.is_gt)
            qTb = [self.sb(f"a_q{g}", [128, S], BF, es) for g in range(3)]
            kTb = [self.sb(f"a_k{g}", [128, S], BF, es) for g in range(3)]
            Vb = [self.sb(f"a_v{g}", [128, NT, 2, 65], BF, es) for g in range(3)]
            for g in range(3):
                if 'memset' in skip:
                    continue
                P.memset(P.pool, Vb[g].v(Vb[g].h[:, :, :, 64:65]), 1.0)
            X = {"S": Ring(self.psum_banks(es, 4)), "P": Ring([self.sb(f"a_p{i}", [128, 512], BF, es) for i in range(6)])}
            Ob = Ring([Tn(P, f"a_o{i}", [128, 512], F32, "psum", es) for i in range(4)])
            rl = Ring([self.sb(f"a_rl{i}", [128, 4, 1], F32, es) for i in range(4)])
            ost = Ring([self.sb(f"a_ost{i}", [128, 4, 128], BF, es) for i in range(2)])
            base = [0, 5, 13]
            for sp in range(3):
                for g in range(3):
                    P.dma(P.sp, qTb[g][:, :], (QT, QT.ap[g * 3 + sp]))
                    P.dma(P.sp, kTb[g][:, :], (QT, QT.ap[9 + g * 3 + sp]))
                    self.load_v_tm(Vb[g], VT, VO_A + g * 384 + sp * 128, 128, 2, 65)
                for tt in range(S // 512):
                    o_st = ost.next()
                    lanes = []
                    for h in range(2):
                        hs = slice(64 * h, 64 * h + 64)
                        pairs = []
                        for g in range(3):
                            dil = A_DIL[g]
                            for sg in range(max(0, 4 * tt - dil), 4 * tt + 4):
                                dl = 4 * tt - sg
                                pairs.append({
                                    "kT": kTb[g][hs, sg * 128:(sg + 1) * 128],
                                    "q": qTb[g][hs, tt * 512:(tt + 1) * 512],
                                    "bias": [],
                                    "mask": am.v(am.h[:, base[g] + dl + 3, :]),
                                    "v": Vb[g].v(Vb[g].h[:, sg, h, :]),
                                })
                        lanes.append((pairs, Ob.next()))
                    self.attn_lanes(lanes, 65, X)
                    for h in range(2):
                        O = lanes[h][1]
                        O3 = O.h[:, 0:260].rearrange("p (i c) -> p i c", c=65)
                        r_ = rl.next()
                        P.recip(r_[:, :, :], O.v(O3[:, :, 64:65]))
                        P.tt(P.dve, o_st.v(o_st.h[:, :, 64 * h:64 * h + 64]), O.v(O3[:, :, 0:64]),
                             r_.v(r_.h[:, :, :].broadcast_to([128, 4, 64])), ALU.mult)
                    if 'store' in skip:
                        continue
                    P.dma(self.qst, (OTOK, OTOK.ap[tt * 512:(tt + 1) * 512, sp * 128:(sp + 1) * 128]
                                     .rearrange("(i p) c -> p i c", p=128)), o_st[:, :, :])
            P.barrier()

    def mixer_c(self, l):
        P, S, NT = self.P, self.S, self.NT
        QT, VT, OTC = self.QT[l], self.VT[l], self.OTC[l]
        with contextlib.ExitStack() as es:
            cm = self.sb("cmask", [128, 4, 512], BF, es)
            P.dma(P.sp, cm[:, :, :], (self.C["c_cmask"], self.C["c_cmask"].ap.rearrange("t p n -> p t n")))
            mU = self.sb("c_mU", [128, 128], BF, es)
            mO = self.sb("c_mO", [128, 128], BF, es)
            P.ts(P.dve, mU[:, :], self.k_U[:, :], -8.0, None, ALU.mult)
            P.memset(P.dve, mO[:, :], -8.0)
            qT = self.sb("c_q", [128, S], BF, es)
            kT = self.sb("c_k", [128, S], BF, es)
            Vb = self.sb("c_v", [128, NT, 2, 64], BF, es)
            Zr = Ring(self.psum_banks(es, 6))
            Ob = Ring([Tn(P, f"c_o{i}", [128, 512], F32, "psum", es) for i in range(1)])
            Er = Ring([self.sb(f"c_e{i}", [128, 512], F32, es) for i in range(4)])
            Sr = Ring([self.sb(f"c_s{i}", [128, 512], BF, es) for i in range(6)])
            Ar = Ring([self.sb(f"c_a{i}", [128, 512], BF, es) for i in range(6)])
            ssum = [self.sb(f"c_ss{h}", [128, 512], F32, es) for h in range(2)]
            ssbf = [Ring([self.sb(f"c_sb{h}_{i}", [128, 512], BF, es) for i in range(2)]) for h in range(2)]
            ostg = Ring([self.sb(f"c_og{i}", [128, 512], BF, es) for i in range(2)])
            for hp in range(3):
                P.dma(P.sp, qT[:, :], (QT, QT.ap[30 + hp]))
                P.dma(P.sp, kT[:, :], (QT, QT.ap[33 + hp]))
                self.load_v_tm(Vb, VT, VO_C + hp * 128, 128, 2, 64)
                for tt in range(S // 512):
                    O = Ob.next()
                    sgs = list(range(4 * tt + 3, -1, -1))
                    n = len(sgs)
                    st = {}

                    def s0(i):
                        sg = sgs[i]
                        zs, es_ = [], []
                        for h in range(2):
                            hs = slice(64 * h, 64 * h + 64)
                            z = Zr.next()
                            P.mm(z[:, :], kT[hs, sg * 128:(sg + 1) * 128], qT[hs, tt * 512:(tt + 1) * 512], start=True,
                                 stop=False, sig=True)
                            zs.append(z)
                        for h in range(2):
                            e = Er.next()
                            P.activation(e[:, :], zs[h][:, :], AF.Exp, scale=0.125)
                            es_.append(e)
                        if sg >= 4 * tt:
                            for h in range(2):
                                P.tt(P.dve, es_[h][:, :], es_[h][:, :], cm.v(cm.h[:, 4 * tt - sg + 3, :]), ALU.mult)
                        for h in range(2):
                            sp_ = Sr.next()
                            P.activation(sp_[:, :], es_[h][:, :], AF.Ln, bias=self.k_one[:, 0:1])
                            st[(i, h)] = [zs[h], sp_, None]

                    def s1(i):
                        sg = sgs[i]
                        for h in range(2):
                            z, sp_, _ = st[(i, h)]
                            P.mm(z[:, :], mU[:, :], sp_[:, :], start=False, stop=(i == 0), sig=(i == 0))
                            if i > 0:
                                P.mm(z[:, :], mO[:, :], st[("sb", h)][:, :], start=False, stop=True, sig=True)
                        for h in range(2):
                            sp_ = st[(i, h)][1]
                            if i == 0:
                                P.copy(P.pool, ssum[h][:, :], sp_[:, :])
                            else:
                                P.tt(P.pool, ssum[h][:, :], ssum[h][:, :], sp_[:, :], ALU.add)
                        if i + 1 < n:
                            for h in range(2):
                                sb_ = ssbf[h].next()
                                P.copy(P.dve, sb_[:, :], ssum[h][:, :])
                                st[("sb", h)] = sb_
                        for h in range(2):
                            a_ = Ar.next()
                            P.activation(a_[:, :], st[(i, h)][0][:, :], AF.Exp, scale=0.125)
                            st[(i, h)][2] = a_
                        if sg >= 4 * tt:
                            for h in range(2):
                                a_ = st[(i, h)][2]
                                P.tt(P.dve, a_[:, :], a_[:, :], cm.v(cm.h[:, 4 * tt - sg + 3, :]), ALU.mult)

                    def s2(i):
                        sg = sgs[i]
                        for h in range(2):
                            a_ = st[(i, h)][2]
                            P.mm(O[64 * h:64 * h + 64, :], Vb.v(Vb.h[:, sg, h, :]), a_[:, :], start=(i == 0), stop=(i == n - 1))

                    stages = [s0, s1, s2]
                    for it in range(n + 2):
                        for k, fn in enumerate(stages):
                            i = it - k
                            if 0 <= i < n:
                                fn(i)
                    og = ostg.next()
                    P.activation(og[:, :], O[:, :], AF.Identity)
                    P.dma(self.qst, (OTC, OTC.ap[hp, :, tt * 512:(tt + 1) * 512]), og[:, :])
            P.barrier()

    def mixer_b(self, l):
        import os
        skip = os.environ.get('K_B_SKIP', '')
        P, S, NT = self.P, self.S, self.NT
        QT, VT, OTOK = self.QT[l], self.VT[l], self.OTOK[l]
        ncmp = S // 16 - 1
        GC = 0.7978845608028654
        with contextlib.ExitStack() as es:
            kslc = self.sb("b_kslc", [128, S], BF, es)
            kwin = self.sb("b_kwin", [128, S], BF, es)
            P.dma(P.sp, kslc[:, :], (QT, QT.ap[28]))
            P.dma(P.sp, kwin[:, :], (QT, QT.ap[29]))
            Vs = self.sb("b_vs", [128, NT, 2, 65], BF, es)
            Vw = self.sb("b_vw", [128, NT, 2, 65], BF, es)
            for (vb, c0) in ((Vs, VO_SLC), (Vw, VO_WIN)):
                P.memset(P.pool, vb.v(vb.h[:, :, :, 64:65]), 1.0)
                self.load_v_tm(vb, VT, c0, 128, 2, 65)
            kcT = self.sb("b_kcT", [128, 256], BF, es)
            vca = [self.sb(f"b_vca{g}", [128, 2, 129], BF, es) for g in range(2)]
            wm = self.sb("b_wm", [128, 8, 512], BF, es)
            P.dma(P.sp, wm[:, :, :], (self.C["c_wmask"], self.C["c_wmask"].ap.rearrange("t p n -> p t n")))
            sm = self.sb("b_sm", [128, 4, 512], BF, es)
            P.dma(P.sp, sm[:, :, :], (self.C["c_smask"], self.C["c_smask"].ap.rearrange("t p n -> p t n")))
            P.ts(P.dve, wm[:, :, :], wm[:, :, :], NEG / 2, None, ALU.is_gt)
            P.ts(P.dve, sm[:, :, :], sm[:, :, :], NEG / 2, None, ALU.is_gt)
            esel = self.sb("b_esel", [128, NT, 128], BF, es)
            for hh in range(2):
                P.dma(P.sp, esel.v(esel.h[64 * hh:64 * hh + 64, :, :]), (self.C["c_esel"], self.C["c_esel"].ap), part=(hh > 0))
            vnf = self.sb("b_vnf", [128, NT, 64], F32, es)
            add = self.sb("b_add", [128, NT, 64], F32, es)
            P.dma(P.sp, vnf[:, :, :], (self.C["c_vnf"], self.C["c_vnf"].ap.rearrange("(s p) j -> p s j", p=128)))
            P.dma(P.sp, add[:, :, :], (self.C["c_add"], self.C["c_add"].ap.rearrange("(s p) j -> p s j", p=128)))
            Sr = Ring(self.psum_banks(es, 4))
            Ob = [Tn(P, f"b_o{i}", [128, 512], F32, "psum", es) for i in range(3)]
            Obr = Ring(Ob)
            psT = Tn(P, "b_psT", [128, 1024], BF, "psum", es)
            X = {"S": Sr, "P": Ring([self.sb(f"b_p{i}", [128, 512], BF, es) for i in range(6)])}

            with contextlib.ExitStack() as es2:
                for which, (qi, pe_n, w1_n, w2_n) in enumerate(((26, "cmp_pe_k", "cmp_w1_k", "cmp_w2_k"),
                                                              (27, "cmp_pe_v", "cmp_w1_v", "cmp_w2_v"))):
                    if 'pro' in skip:
                        continue
                    src = self.sb(f"b_cin{which}", [128, S], BF, es2)
                    P.dma(P.sp, src[:, :], (QT, QT.ap[qi]))
                    w1T = self.sb(f"b_w1T{which}", [128, 32, 128], BF, es2)
                    peT = self.sb(f"b_peT{which}", [128, 32], F32, es2)
                    w2 = self.sb(f"b_w2{which}", [128, 128], BF, es2)
                    w1src = self.W[w1_n].ap[l].rearrange("(p d) h -> d p h", d=64)
                    pesrc = self.W[pe_n].ap[l].rearrange("p d -> d p")
                    for hh in range(2):
                        P.dma(P.pool, w1T.v(w1T.h[64 * hh:64 * hh + 64, :, :]), (self.x, w1src), part=(hh > 0))
                        P.dma(P.sp, peT.v(peT.h[64 * hh:64 * hh + 64, :]), (self.x, pesrc), part=(hh > 0),
                              allow_slow_non_contiguous=True)
                        P.dma(P.pool, w2.v(w2.h[:, 64 * hh:64 * hh + 64]), (self.x, self.W[w2_n].ap[l]), part=(hh > 0))
                    kpe = self.sb(f"b_kpe{which}", [128, 32, 256], BF, es2)
                    P.memset(P.pool, kpe[:, :, :], 0.0)
                    win_ap = bass.AP(tensor=src.h, offset=0, ap=[[S, 128], [1, 32], [16, ncmp]])
                    P.tt(P.dve, kpe.v(kpe.h[:, :, 0:ncmp]), src.v(win_ap),
                         peT.v(peT.h[:, :].unsqueeze(2).broadcast_to([128, 32, ncmp])), ALU.add)
                    xs = self.sb(f"b_xs{which}", [128, 256], F32, es2)
                    x2 = self.sb(f"b_x2{which}", [128, 256], F32, es2)
                    hg = self.sb(f"b_hg{which}", [128, 256], BF, es2)
                    for g in range(2):
                        hs = slice(64 * g, 64 * g + 64)
                        ph = Sr.next()
                        for p_ in range(32):
                            P.mm(ph[:, 0:256], w1T.v(w1T.h[hs, p_, :]), kpe.v(kpe.h[hs, p_, :]), start=(p_ == 0),
                                 stop=(p_ == 31))
                        P.activation(xs[:, :], ph[:, 0:256], AF.Identity)
                        P.tt(P.dve, x2[:, :], xs[:, :], xs[:, :], ALU.mult)
                        P.ts(P.dve, x2[:, :], x2[:, :], 0.044715, 1.0, ALU.mult, ALU.add)
                        P.tt(P.dve, x2[:, :], x2[:, :], xs[:, :], ALU.mult)
                        P.activation(x2[:, :], x2[:, :], AF.Sigmoid, scale=2.0 * GC)
                        P.tt(P.dve, hg[:, :], xs[:, :], x2[:, :], ALU.mult)
                        if which == 0:
                            pk = Sr.next()
                            P.mm(pk[:, 0:256], w2[:, :], hg[:, :], start=True, stop=True)
                            P.copy(P.act, kcT[hs, :], pk[hs, 0:256])
                        else:
                            for ct in range(2):
                                pv = Sr.next()
                                P.mm(pv[:, 0:64], hg[:, ct * 128:(ct + 1) * 128], w2[:, 0:64], start=True, stop=True)
                                P.copy(P.act, vca[g].v(vca[g].h[:, ct, 0:64]), pv[:, 0:64])
                for g in range(2):
                    P.memset(P.pool, vca[g].v(vca[g].h[:, :, 64:65]), 1.0)
                    P.dma(P.sp, vca[g].v(vca[g].h[:, :, 65:129]),
                          (self.C["c_wsel"], self.C["c_wsel"].ap.rearrange("(ct p) j -> p ct j", p=128)))
                P.barrier()

            qu = [Ring([self.sb(f"b_qu{r}_{i}", [128, 512], BF, es) for i in range(2)]) for r in range(4)]
            qr = [Ring([self.sb(f"b_qr{r}_{i}", [128, 512], BF, es) for i in range(2)]) for r in range(4)]
            glr = Ring([self.sb(f"b_gl{i}", [128, 4, 24], BF, es) for i in range(2)])
            gsr = Ring([self.sb(f"b_gs{i}", [128, 4, 24], F32, es) for i in range(2)])
            cbr = Ring([self.sb(f"b_cb{i}", [128, 2, 512], BF, es) for i in range(2)])
            sacc = [self.sb(f"b_sacc{g}", [128, 4, 64], F32, es) for g in range(2)]
            oB = self.sb("b_oB", [128, 4, 512], F32, es)
            ob16 = Ring([self.sb(f"b_ob16_{i}", [128, 4, 512], BF, es) for i in range(2)])
            BT = self.sb("b_BT", [128, 512], BF, es)
            rlr = Ring([self.sb(f"b_rl{i}", [128, 4, 1], F32, es) for i in range(6)])
            tmpr = Ring([self.sb(f"b_tmp{i}", [128, 4, 64], F32, es) for i in range(4)])
            sc = self.sb("b_sc", [128, 64], F32, es)
            wk = self.sb("b_wk", [128, 64], F32, es)
            m8 = self.sb("b_m8", [128, 8], F32, es)
            m8b = self.sb("b_m8b", [128, 8], F32, es)
            btr = Ring([self.sb(f"b_bt{i}", [128, 128], BF, es) for i in range(4)])
            for tt in range(S // 512):
                if 'main' in skip:
                    continue
                tsl = slice(tt * 512, (tt + 1) * 512)
                qut = []
                qrt = []
                for r in range(4):
                    a = qu[r].next()
                    P.dma(P.sp, a[:, :], (QT, QT.ap[18 + r, :, tsl]))
                    qut.append(a)
                    b = qr[r].next()
                    P.dma(P.sp, b[:, :], (QT, QT.ap[22 + r, :, tsl]))
                    qrt.append(b)
                gl = glr.next()
                P.dma(P.sp, gl[:, :, :], (VT, VT.ap[tsl, VO_GATE:VO_GATE + 24].rearrange("(i p) c -> p i c", p=128)))
                gs = gsr.next()
                P.activation(gs[:, :, :], gl[:, :, :], AF.Sigmoid)
                cb = cbr.next()
                P.dma(P.sp, cb[:, :, :], (self.C["c_cmpb"], self.C["c_cmpb"].ap[:, tsl].rearrange("(ct p) n -> p ct n", p=128)))
                P.ts(P.dve, cb[:, :, :], cb[:, :, :], NEG / 2, None, ALU.is_gt)

                def epilogue(O3, nsub, sub0, h, br, accumulate):
                    r_ = rlr.next()
                    rv = r_.v(r_.h[:, 0:nsub, :])
                    P.ts(P.dve, rv, O3[2], 1e-30, None, ALU.max)
                    P.recip(rv, rv)
                    rg = rlr.next()
                    rgv = rg.v(rg.h[:, 0:nsub, :])
                    P.tt(P.dve, rgv, rv, gs.v(gs.h[:, sub0:sub0 + nsub, 3 * h + br:3 * h + br + 1]), ALU.mult)
                    dst = oB.v(oB.h[:, sub0:sub0 + nsub, 64 * h:64 * h + 64])
                    bc = rg.v(rg.h[:, 0:nsub, :].broadcast_to([128, nsub, 64]))
                    if not accumulate:
                        P.tt(P.dve, dst, O3[0], bc, ALU.mult)
                    else:
                        t_ = tmpr.next()
                        tv = t_.v(t_.h[:, 0:nsub, :])
                        P.tt(P.dve, tv, O3[0], bc, ALU.mult)
                        P.tt(P.dve, dst, dst, tv, ALU.add)
                    return r_

                for h in range(8):
                    if 'cmp' in skip:
                        continue
                    g, r = h // 4, h % 4
                    hs = slice(64 * g, 64 * g + 64)
                    pairs = [{"kT": kcT[hs, ct * 128:(ct + 1) * 128], "q": qut[r][hs, :],
                              "bias": [], "mask": cb.v(cb.h[:, ct, :]),
                              "v": vca[g].v(vca[g].h[:, ct, :])} for ct in range(2)]
                    obs = [Obr.next(), Obr.next()]
                    outs = [obs[i // 2][:, (i % 2) * 129:(i % 2) * 129 + 129] for i in range(4)]
                    self.attn(pairs, None, 129, X, nsub=4, outs=outs, starts=(0, 2))
                    for bk in range(2):
                        O = obs[bk]
                        O3 = O.h[:, 0:258].rearrange("p (i c) -> p i c", c=129)
                        views = (O.v(O3[:, :, 0:64]), O.v(O3[:, :, 65:129]), O.v(O3[:, :, 64:65]))
                        r_ = epilogue(views, 2, 2 * bk, h, 0, False)
                        bc = r_.v(r_.h[:, 0:2, :].broadcast_to([128, 2, 64]))
                        sd = sacc[g].v(sacc[g].h[:, 2 * bk:2 * bk + 2, :])
                        if r == 0:
                            P.tt(P.dve, sd, views[1], bc, ALU.mult)
                        else:
                            t_ = tmpr.next()
                            tv = t_.v(t_.h[:, 0:2, :])
                            P.tt(P.dve, tv, views[1], bc, ALU.mult)
                            P.tt(P.pool, sd, sd, tv, ALU.add)
                if 'topk' not in skip:
                    for i in range(4):
                        tix = 4 * tt + i
                        bt = btr.next()
                        for g in range(2):
                            P.tt(P.dve, sc[:, :], sacc[g].v(sacc[g].h[:, i, :]), vnf.v(vnf.h[:, tix, :]), ALU.mult)
                            P.tt(P.dve, sc[:, :], sc[:, :], add.v(add.h[:, tix, :]), ALU.add)
                            P.op(P.dve, lambda: self.nc.vector.max(out=m8.h[:, :], in_=sc.h[:, :]), [sc[:, :]], [m8[:, :]])
                            P.op(P.dve, lambda: self.nc.vector.match_replace(out=wk.h[:, :], in_to_replace=m8.h[:, :],
                                                                              in_values=sc.h[:, :], imm_value=-1e30),
                                 [sc[:, :], m8[:, :]], [wk[:, :]])
                            P.op(P.dve, lambda: self.nc.vector.max(out=m8b.h[:, :], in_=wk.h[:, :]), [wk[:, :]], [m8b[:, :]])
                            P.ts(P.dve, bt[:, 64 * g:64 * g + 64], sc[:, :], m8b[:, 7:8], NEG, ALU.is_lt, ALU.mult)
                        P.transpose(psT[:, i * 128:(i + 1) * 128], bt[:, :], self.k_ident[:, :], sig=True)
                    P.copy(P.act, BT[:, :], psT[:, 0:512])
                for r in range(4):
                    for br in (1, 2):
                        if ('sel' in skip and br == 1) or ('win' in skip and br == 2):
                            continue
                        lanes = []
                        for g in range(2):
                            hs = slice(64 * g, 64 * g + 64)
                            pairs = []
                            if br == 1:
                                for sg in range(0, 4 * tt + 4):
                                    bias = [(esel.v(esel.h[hs, sg, :]), BT[hs, :])]
                                    mk = sm.v(sm.h[:, 4 * tt - sg + 3, :]) if sg >= 4 * tt else None
                                    pairs.append({"kT": kslc[hs, sg * 128:(sg + 1) * 128], "q": qrt[r][hs, :], "bias": bias,
                                                  "mask": mk, "v": Vs.v(Vs.h[:, sg, g, :])})
                            else:
                                for sg in range(max(0, 4 * tt - 4), 4 * tt + 4):
                                    pairs.append({"kT": kwin[hs, sg * 128:(sg + 1) * 128], "q": qrt[r][hs, :], "bias": [],
                                                  "mask": wm.v(wm.h[:, 4 * tt - sg + 3, :]), "v": Vw.v(Vw.h[:, sg, g, :])})
                            lanes.append((pairs, Obr.next()))
                        self.attn_lanes(lanes, 65, X)
                        for g in range(2):
                            O = lanes[g][1]
                            O3 = O.h[:, 0:260].rearrange("p (i c) -> p i c", c=65)
                            views = (O.v(O3[:, :, 0:64]), None, O.v(O3[:, :, 64:65]))
                            epilogue(views, 4, 0, 4 * g + r, br, True)
                o16 = ob16.next()
                P.activation(o16[:, :, :], oB[:, :, :], AF.Identity)
                P.dma(self.qst, (OTOK, OTOK.ap[tsl, 384:896].rearrange("(i p) c -> p i c", p=128)), o16[:, :, :])
            P.barrier()

def _in_map(consts, x_b, pos_b, weights, norm_final):
    m = {"x": np.ascontiguousarray(x_b), "pos": np.ascontiguousarray(pos_b).reshape(1, -1).astype(np.int32)}
    for n in WNAMES:
        m[n] = weights[n]
    m["norm_final"] = norm_final
    m.update(consts)
    return m


def kernel(x, positions, norm_ffn1, ffn1_w1, ffn1_w3, ffn1_w2, norm_mix, w_in,
           cmp_pe_k, cmp_w1_k, cmp_w2_k, cmp_pe_v, cmp_w1_v, cmp_w2_v,
           w_gate, w_up, w_out, norm_ffn2, ffn2_w1, ffn2_w3, ffn2_w2, norm_final):
    loc = locals()
    x = np.asarray(x)
    B, S, _ = x.shape
    L = int(np.asarray(norm_ffn1).shape[0])
    weights = {n: np.ascontiguousarray(np.asarray(loc[n], dtype=np.float32)) for n in WNAMES}
    bld = Builder(S, L)
    nc = bld.build()
    nf = np.ascontiguousarray(np.asarray(norm_final, dtype=np.float32))
    pos = np.asarray(positions)
    in_maps = [_in_map(bld.consts, x[b], pos[b], weights, nf) for b in range(B)]
    res = run_bass_kernel_spmd(nc, in_maps, core_ids=list(range(B)))
    return np.stack([np.asarray(res.results[b]["out"]) for b in range(B)], axis=0).astype(np.float32)
```
